# Optimizing a Trainium2 kernel written in Bass

```python
import math
import jax, jax.numpy as jnp
from jax import lax
import numpy as np


D_MODEL = 1024
BATCH = 4
SEQ = 8192
DEPTH = 4

CHUNK = 64
Q_BLOCK = 128
EPS = 1e-6

D_MIX = D_MODEL
CONV_W = D_MIX // 4
CONV_K = 3
DIFF_HEADS = 4
DIFF_DH = D_MIX // 16
DIFF_DV = 2 * DIFF_DH
DIFF_W = DIFF_HEADS * DIFF_DV
GLA_HEADS = 4
GLA_DK = D_MIX // 16
GLA_DV = D_MIX // 16
GLA_W = GLA_HEADS * GLA_DV
GLA_RANK = 16
GLA_TAU = 16.0
D_FF = 4 * D_MODEL

SIZES = (CONV_W, CONV_W, CONV_W,
         DIFF_HEADS * 2 * DIFF_DH, DIFF_HEADS * 2 * DIFF_DH, DIFF_W,
         GLA_HEADS * GLA_DK, GLA_HEADS * GLA_DK, GLA_W, GLA_W, GLA_RANK)
D_IN = sum(SIZES)
SPLITS = tuple(int(s) for s in np.cumsum(SIZES)[:-1])

kernel_name = 'hybrid_conv_diffattn_gla_block'


def rms_norm(x, g):
    xf = x.astype(jnp.float32)
    y = xf * lax.rsqrt(jnp.mean(jnp.square(xf), axis=-1, keepdims=True) + EPS)
    return (y * g.astype(jnp.float32)).astype(x.dtype)


def short_conv(u, b_gate, c_gate, w):
    z = c_gate * u
    y = lax.conv_general_dilated(z, w.astype(z.dtype)[:, None, :], window_strides=(1,),
                                 padding=[(CONV_K - 1, 0)],
                                 dimension_numbers=('NWC', 'WIO', 'NWC'),
                                 feature_group_count=CONV_W)
    return b_gate * y


def diff_attention(q, k, v, q_g, k_g, lam, sub_g, lam_init):
    bsz, seq = q.shape[0], q.shape[1]
    q = rms_norm(q, q_g).astype(jnp.float32) * (DIFF_DH ** -0.5)
    k = rms_norm(k, k_g).astype(jnp.float32)
    v = v.astype(jnp.float32)
    nb = seq // Q_BLOCK
    qb = jnp.moveaxis(q.reshape(bsz, nb, Q_BLOCK, DIFF_HEADS, 2, DIFF_DH), 1, 0)
    pos = jnp.arange(seq)
    key_chunk = pos // CHUNK
    slopes = jnp.asarray([2.0 ** (-8.0 * (h + 1) / DIFF_HEADS) for h in range(DIFF_HEADS)],
                         dtype=jnp.float32)

    def block(args):
        q_blk, i = args
        tq = i * Q_BLOCK + jnp.arange(Q_BLOCK)
        dist = jnp.abs(tq[:, None] - pos[None, :]).astype(jnp.float32)
        visible = key_chunk[None, :] <= (tq // CHUNK)[:, None]
        bias = jnp.where(visible[None], -slopes[:, None, None] * dist[None], -jnp.inf)
        s = jnp.einsum('bqhcd,bkhcd->bchqk', q_blk, k) + bias
        p = jax.nn.softmax(s, axis=-1)
        a = p[:, 0] - lam * p[:, 1]
        return jnp.einsum('bhqk,bkhe->bqhe', a, v)

    o = lax.map(block, (qb, jnp.arange(nb)))
    o = jnp.moveaxis(o, 0, 1).reshape(bsz, seq, DIFF_HEADS, DIFF_DV)
    return rms_norm(o, sub_g) * (1.0 - lam_init)


def gla(q, k, v, g_out, a_lr, a_w, a_b, norm_g):
    bsz, seq = q.shape[0], q.shape[1]
    nc = seq // CHUNK
    shp_k = (bsz, nc, CHUNK, GLA_HEADS, GLA_DK)
    log_a = jax.nn.log_sigmoid((a_lr @ a_w + a_b).astype(jnp.float32)) / GLA_TAU
    b = jnp.cumsum(log_a.reshape(shp_k), axis=2)
    b_last = b[:, :, -1]
    qf = q.astype(jnp.float32).reshape(shp_k) * (GLA_DK ** -0.5)
    kf = k.astype(jnp.float32).reshape(shp_k)
    vf = v.astype(jnp.float32).reshape(bsz, nc, CHUNK, GLA_HEADS, GLA_DV)
    q_in = qf * jnp.exp(b)
    k_in = kf * jnp.exp(-b)
    k_end = kf * jnp.exp(b_last[:, :, None] - b)
    causal = jnp.tril(jnp.ones((CHUNK, CHUNK), dtype=bool))
    att = jnp.where(causal, jnp.einsum('bcthd,bcshd->bchts', q_in, k_in), 0.0)
    o_intra = jnp.einsum('bchts,bcshe->bcthe', att, vf)
    kv = jnp.einsum('bcshd,bcshe->bchde', k_end, vf)

    def step(state, inp):
        dec, kv_c = inp
        return dec[..., None] * state + kv_c, state

    s0 = jnp.zeros((bsz, GLA_HEADS, GLA_DK, GLA_DV), jnp.float32)
    _, s_prev = lax.scan(step, s0, (jnp.moveaxis(jnp.exp(b_last), 1, 0), jnp.moveaxis(kv, 1, 0)))
    o_inter = jnp.einsum('bcthd,bchde->bcthe', q_in, jnp.moveaxis(s_prev, 0, 1))
    o = (o_intra + o_inter).reshape(bsz, seq, GLA_HEADS, GLA_DV)
    o = rms_norm(o, norm_g).reshape(bsz, seq, GLA_W)
    return o * jax.nn.silu(g_out.astype(jnp.float32))


def setup_inputs(seed: int = 0) -> dict:
    key = jax.random.key(seed)
    ks = jax.random.split(key, 15)

    def nrm(k, shape, scale):
        return jax.random.normal(k, shape, jnp.float32) * scale

    return {
        'x': nrm(ks[0], (BATCH, SEQ, D_MODEL), 1.0),
        'ln1_g': 1.0 + nrm(ks[1], (DEPTH, D_MODEL), 0.02),
        'w_in': nrm(ks[2], (DEPTH, D_MODEL, D_IN), D_MODEL ** -0.5),
        'conv_w': nrm(ks[3], (DEPTH, CONV_K, CONV_W), CONV_K ** -0.5),
        'q_norm_g': 1.0 + nrm(ks[4], (DEPTH, DIFF_DH), 0.02),
        'k_norm_g': 1.0 + nrm(ks[5], (DEPTH, DIFF_DH), 0.02),
        'diff_lambda': nrm(ks[6], (DEPTH, 4, DIFF_DH), 0.1),
        'diff_subln_g': 1.0 + nrm(ks[7], (DEPTH, DIFF_DV), 0.02),
        'gla_alpha_w': nrm(ks[8], (DEPTH, GLA_RANK, GLA_HEADS * GLA_DK), GLA_RANK ** -0.5),
        'gla_alpha_b': nrm(ks[9], (DEPTH, GLA_HEADS * GLA_DK), 0.01),
        'gla_norm_g': 1.0 + nrm(ks[10], (DEPTH, GLA_DV), 0.02),
        'w_out': nrm(ks[11], (DEPTH, D_MIX, D_MODEL), D_MIX ** -0.5),
        'ln2_g': 1.0 + nrm(ks[12], (DEPTH, D_MODEL), 0.02),
        'w_mlp1': nrm(ks[13], (DEPTH, D_MODEL, D_FF), D_MODEL ** -0.5),
        'w_mlp2': nrm(ks[14], (DEPTH, D_FF, D_MODEL), D_FF ** -0.5),
    }


def reference(x, ln1_g, w_in, conv_w, q_norm_g, k_norm_g, diff_lambda, diff_subln_g,
              gla_alpha_w, gla_alpha_b, gla_norm_g, w_out, ln2_g, w_mlp1, w_mlp2):
    bsz, seq = x.shape[0], x.shape[1]
    for l in range(DEPTH):
        h = rms_norm(x, ln1_g[l])
        z = h @ w_in[l]
        (u, c_b, c_c, d_q, d_k, d_v, g_q, g_k, g_v, g_g, g_a) = jnp.split(z, SPLITS, axis=-1)
        y_conv = short_conv(u, c_b, c_c, conv_w[l])
        lam_init = 0.8 - 0.6 * math.exp(-0.3 * l)
        lp = diff_lambda[l].astype(jnp.float32)
        lam = jnp.exp(jnp.sum(lp[0] * lp[1])) - jnp.exp(jnp.sum(lp[2] * lp[3])) + lam_init
        y_diff = diff_attention(d_q.reshape(bsz, seq, DIFF_HEADS, 2, DIFF_DH),
                                d_k.reshape(bsz, seq, DIFF_HEADS, 2, DIFF_DH),
                                d_v.reshape(bsz, seq, DIFF_HEADS, DIFF_DV),
                                q_norm_g[l], k_norm_g[l], lam, diff_subln_g[l], lam_init)
        y_gla = gla(g_q, g_k, g_v, g_g, g_a, gla_alpha_w[l], gla_alpha_b[l], gla_norm_g[l])
        y = jnp.concatenate([y_conv.astype(x.dtype),
                             y_diff.reshape(bsz, seq, DIFF_W).astype(x.dtype),
                             y_gla.astype(x.dtype)], axis=-1)
        x = x + y @ w_out[l]
        h2 = rms_norm(x, ln2_g[l])
        x = x + jnp.square(jax.nn.relu(h2 @ w_mlp1[l])) @ w_mlp2[l]
    return x
```

```python
import math
from contextlib import ExitStack
import numpy as np
import concourse.bass as bass
import concourse.mybir as mybir
from concourse.bass_utils import run_bass_kernel_spmd

F32 = mybir.dt.float32
BF16 = mybir.dt.bfloat16
I32 = mybir.dt.int32
ALU = mybir.AluOpType
AF = mybir.ActivationFunctionType
AX = mybir.AxisListType

ENGS = ("pe", "act", "dve", "pool", "sp")
SEM_EPOCH = 24000
DMA_RING = 8

D = 1024
DIN = 3344
DFF = 4096
EPS = 1e-6
NEG = -30000.0


class Buf:
    __slots__ = ("name", "w", "r")

    def __init__(self, name=""):
        self.name = name
        self.w = None
        self.r = []


class Sched:
    def __init__(self):
        self.ops = []
        self.sameeng_sync = {"act": True, "dve": True, "pool": True, "pe": False, "sp": False}
        self.marks = []
        self.fence_deps = None
        self.fence_id = 0
        self.fence_passed = {}
        self.last_ops = {e: [] for e in ENGS}

    def mark(self, name):
        self.marks.append((name, len(self.ops)))

    def fence(self):
        deps = set()
        for e in ENGS:
            deps.update(self.last_ops[e][-(DMA_RING + 1):])
        self.fence_deps = deps
        self.fence_id += 1

    def add(self, eng, fn, reads=(), writes=(), dma=False, ring="d"):
        idx = len(self.ops)
        deps = set()
        if self.fence_deps is not None and self.fence_passed.get(eng) != self.fence_id:
            deps.update(self.fence_deps)
            self.fence_passed[eng] = self.fence_id
        self.last_ops[eng].append(idx)
        if len(self.last_ops[eng]) > 4 * DMA_RING:
            del self.last_ops[eng][:-2 * DMA_RING]
        for b in reads:
            if b.w is not None:
                deps.add(b.w)
        for b in writes:
            if b.w is not None:
                deps.add(b.w)
            for r in b.r:
                deps.add(r)
        for b in reads:
            b.r.append(idx)
        for b in writes:
            b.w = idx
            b.r = []
        deps.discard(idx)
        self.ops.append([eng, fn, deps, dma, ring])
        return idx

    def pe(self, fn, reads=(), writes=()):
        return self.add("pe", fn, reads, writes)

    def act(self, fn, reads=(), writes=()):
        return self.add("act", fn, reads, writes)

    def dve(self, fn, reads=(), writes=()):
        return self.add("dve", fn, reads, writes)

    def pool(self, fn, reads=(), writes=()):
        return self.add("pool", fn, reads, writes)

    def dma(self, fn, reads=(), writes=(), q="sp", ring="d"):
        return self.add(q, fn, reads, writes, dma=True, ring=ring)

    def emit(self, nc, final_wait_ops=()):
        ops = self.ops
        n = len(ops)
        signal = [False] * n
        for i, (eng, fn, deps, dma, _rg) in enumerate(ops):
            if dma:
                signal[i] = True
            for d in deps:
                if ops[d][3]:
                    continue
                if ops[d][0] != eng or self.sameeng_sync.get(eng, False):
                    signal[d] = True
        for i in final_wait_ops:
            signal[i] = True
        sem_of = [None] * n
        sem_specs = []
        cur = {e: None for e in ENGS}
        dma_ring = {}
        dma_rr = {}
        dma_prev = [None] * n
        ring_last = {}
        for i, (eng, fn, deps, dma, rg) in enumerate(ops):
            if not signal[i]:
                continue
            if dma:
                rk = (eng, rg)
                ring = dma_ring.setdefault(rk, [])
                nslots = DMA_RING if rg == "d" else 1
                slot = dma_rr.get(rk, 0) % nslots
                dma_rr[rk] = dma_rr.get(rk, 0) + 1
                if len(ring) <= slot:
                    sem_specs.append(f"{rg}_{eng}_{slot}_{len(sem_specs)}")
                    ring.append([len(sem_specs) - 1, 0])
                if ring[slot][1] + 16 > SEM_EPOCH:
                    sem_specs.append(f"{rg}_{eng}_{slot}_{len(sem_specs)}")
                    ring[slot] = [len(sem_specs) - 1, 0]
                ring[slot][1] += 16
                sem_of[i] = (ring[slot][0], ring[slot][1])
                key = (eng, rg, slot)
                dma_prev[i] = ring_last.get(key)
                ring_last[key] = i
            else:
                c = cur[eng]
                if c is None or c[1] + 1 > SEM_EPOCH:
                    sem_specs.append(f"s_{eng}_{len(sem_specs)}")
                    c = [len(sem_specs) - 1, 0]
                    cur[eng] = c
                c[1] += 1
                sem_of[i] = (c[0], c[1])
        self.n_sems = len(sem_specs)
        per_eng = {e: [] for e in ENGS}
        for i, op in enumerate(ops):
            per_eng[op[0]].append(i)
        self.stats = {e: len(per_eng[e]) for e in ENGS}
        self.stats["signals"] = sum(signal)

        with ExitStack() as st:
            sems = [st.enter_context(nc.semaphore(nm)) for nm in sem_specs]
            block = st.enter_context(nc.Block())

            def make(engname):
                idxs = per_eng[engname]

                def body(e):
                    seen = {}
                    nwait = 0
                    for i in idxs:
                        eng, fn, deps, dma, _rg = ops[i]
                        waits = {}
                        dl = list(deps)
                        if dma and dma_prev[i] is not None:
                            dl.append(dma_prev[i])
                        for d in dl:
                            if (not ops[d][3]) and ops[d][0] == eng and not self.sameeng_sync.get(eng, False):
                                continue
                            s, v = sem_of[d]
                            if seen.get(s, 0) >= v:
                                continue
                            if waits.get(s, 0) < v:
                                waits[s] = v
                        for s, v in waits.items():
                            e.wait_ge(sems[s], v)
                            seen[s] = v
                            nwait += 1
                        ins = fn(e)
                        if signal[i]:
                            s, v = sem_of[i]
                            ins.then_inc(sems[s], 16 if dma else 1)
                    if engname == "sp":
                        for i in final_wait_ops:
                            s, v = sem_of[i]
                            e.wait_ge(sems[s], v)
                    self.stats["waits_" + engname] = nwait
                return body

            block.tensor(make("pe"))
            block.scalar(make("act"))
            block.vector(make("dve"))
            block.gpsimd(make("pool"))
            block.sync(make("sp"))


class Tl:
    def __init__(self, t, nb=1, name=""):
        self.t = t
        self.b = [Buf(f"{name}{i}") for i in range(nb)]

    @property
    def B(self):
        return self.b[0]


def build(S=8192, DEPTH=4, stop_after=None, debug=False, trunc=None, pair=False, ncores=4):
    NT = S // 512
    NB = S // 128
    nc = bass.Bass("TRN2", target_bir_lowering=False)
    if pair:
        NCV, NH, NP = 1, 2, 1
        C_U, C_CB, C_CC, C_Q, C_K, C_V, C_GQ, C_GK, C_GG, C_GA = 0, 128, 256, 384, 640, 896, 1152, 1280, 1536, 1664
        DINL = 1680
        SP = S // 2
    else:
        NCV, NH, NP = 2, 4, 2
        C_U, C_CB, C_CC, C_Q, C_K, C_V, C_GQ, C_GK, C_GG, C_GA = 0, 256, 512, 768, 1280, 1792, 2304, 2560, 3072, 3328
        DINL = DIN
        SP = S
    GW = NP * 128
    VW = NH * 128
    NYC = NCV + NH + NP
    BPB = 512 // GW

    def din(name, shape):
        return nc.dram_tensor(name, shape, F32, kind="ExternalInput").ap()

    x_in = din("x", [S, D])
    if pair:
        xh_in = din("x_half", [SP, D])
        slopes_in = din("slopes", [1, NH])
        rmask_in = din("rmask", [1, 2])
    ln1_g = din("ln1_g", [DEPTH, D])
    w_in = din("w_in", [DEPTH, D, DINL])
    conv_w = din("conv_w", [DEPTH, 3, NCV * 128])
    q_norm_g = din("q_norm_g", [DEPTH, 64])
    k_norm_g = din("k_norm_g", [DEPTH, 64])
    diff_lambda = din("diff_lambda", [DEPTH, 4, 64])
    diff_subln_g = din("diff_subln_g", [DEPTH, 128])
    gla_alpha_w = din("gla_alpha_w", [DEPTH, 16, GW])
    gla_alpha_b = din("gla_alpha_b", [DEPTH, GW])
    gla_norm_g = din("gla_norm_g", [DEPTH, 64])
    w_out = din("w_out", [DEPTH, D, D])
    ln2_g = din("ln2_g", [DEPTH, D])
    w_mlp1 = din("w_mlp1", [DEPTH, D, DFF])
    w_mlp2 = din("w_mlp2", [DEPTH, DFF, D])
    out_d = nc.dram_tensor("out", [SP, D], F32, kind="ExternalOutput").ap()

    xs_d = nc.dram_tensor("xs_scr", [S, D], F32, kind="Internal").ap()
    sk = "ExternalOutput" if debug else "Internal"
    qT_d = nc.dram_tensor("qT_scr", [NH, 128, S], BF16, kind=sk).ap()
    kT_d = nc.dram_tensor("kT_scr", [NH, 128, S], BF16, kind=sk).ap()
    v_d = nc.dram_tensor("v_scr", [S, VW], BF16, kind=sk).ap()
    yT_d = nc.dram_tensor("yT_scr", [NYC, 128, S], BF16, kind=sk).ap()
    if pair:
        xm_d = nc.dram_tensor("xm_scr", [SP, D], F32, kind="Internal").ap()
        yTall_d = nc.dram_tensor("yTall_scr", [2 * NYC, 128, S], BF16, kind="Internal").ap()
        Bxm, ByTall = Buf("xm"), Buf("yTall")
        groups = [[2 * i, 2 * i + 1] for i in range(ncores // 2)]
    Bxs, BqT, BkT, Bv, ByT, Bout = Buf("xs"), Buf("qT"), Buf("kT"), Buf("v"), Buf("yT"), Buf("out")

    S_ = Sched()
    out_ops = []

    with ExitStack() as st:
        def sb(name, shape, dt, nb=1):
            return Tl(st.enter_context(nc.sbuf_tensor(name, shape, dt)), nb, name)

        PSf = st.enter_context(nc.psum_tensor("psf", [128, 8 * 512], F32))
        bankB = [Buf(f"bank{i}") for i in range(8)]

        def bank(i):
            return PSf[:, i * 512:(i + 1) * 512]

        PSb = bank(7).bitcast(BF16)

        ident = sb("ident", [128, 128], BF16)
        ones_bf = sb("ones_bf", [128, 128], BF16)
        blk64 = sb("blk64", [128, 128], BF16)
        o128 = sb("o128", [128, 128], BF16)
        triu = sb("triu", [128, 128], F32)
        strl = sb("strl", [128, 128], F32)
        triu_b = sb("triu_b", [128, 128], BF16)
        strl_b = sb("strl_b", [128, 128], BF16)
        iot = sb("iot", [128, 128], I32)
        dkq = sb("dkq", [128, 128], F32)
        tdiag = sb("tdiag", [128, 4, 128], BF16)
        kcol = sb("kcol", [128, 1], F32)
        kcoli = sb("kcoli", [128, 1], I32)
        NDEL = 4 * (NT - 1) + 4
        kbias = sb("kbias", [128, 4, NDEL], F32)
        kdel = sb("kdel", [128, NDEL], F32)
        slopec = sb("slopec", [128, 4], F32)
        slopes = [2.0 ** (-8.0 * (h + 1) / 4) for h in range(4)]
        if pair:
            rmask = sb("rmask_sb", [128, 2], F32)
            S_.dma(lambda e: e.dma_start(out=slopec.t[:, 0:NH], in_=slopes_in[0:1, :].partition_broadcast(128)), writes=[slopec.B])
            S_.dma(lambda e: e.dma_start(out=rmask.t[:, :], in_=rmask_in[0:1, :].partition_broadcast(128)), writes=[rmask.B])
        else:
            for h in range(4):
                S_.pool(lambda e, h=h: e.memset(slopec.t[:, h:h + 1], slopes[h]), writes=[slopec.B])

        hmask = sb("hmask", [128, 2], F32)
        cmask = sb("cmask", [128, 2], F32)
        S_.pool(lambda e: e.memset(hmask.t[:], 0.0), writes=[hmask.B])
        S_.pool(lambda e: e.memset(hmask.t[0:64, 0:1], 0.125), writes=[hmask.B])
        S_.pool(lambda e: e.memset(hmask.t[64:128, 1:2], 0.125), writes=[hmask.B])
        S_.pool(lambda e: e.memset(cmask.t[:], 0.0), writes=[cmask.B])
        S_.pool(lambda e: e.memset(cmask.t[0:64, 0:1], 1.0), writes=[cmask.B])
        S_.pool(lambda e: e.memset(cmask.t[64:128, 1:2], 1.0), writes=[cmask.B])
        epsc = sb("epsc", [128, 1], F32)
        onec = sb("onec", [128, 1], F32)
        S_.pool(lambda e: e.memset(epsc.t[:], EPS), writes=[epsc.B])
        S_.pool(lambda e: e.memset(onec.t[:], 1.0), writes=[onec.B])
        S_.pool(lambda e: e.memset(ident.t[:], 0.0), writes=[ident.B])
        S_.pool(lambda e: e.affine_select(out=ident.t[:], in_=ident.t[:], pattern=[[-1, 128]],
                                          compare_op=ALU.not_equal, fill=1.0, base=0, channel_multiplier=1),
                reads=[ident.B], writes=[ident.B])
        S_.pool(lambda e: e.memset(ones_bf.t[:], 1.0), writes=[ones_bf.B])
        S_.pool(lambda e: e.memset(o128.t[:], 1.0 / 128), writes=[o128.B])
        S_.pool(lambda e: e.memset(blk64.t[:], 1.0 / 64), writes=[blk64.B])
        S_.pool(lambda e: e.memset(blk64.t[0:64, 64:128], 0.0), writes=[blk64.B])
        S_.pool(lambda e: e.memset(blk64.t[64:128, 0:64], 0.0), writes=[blk64.B])
        S_.pool(lambda e: e.memset(triu.t[:], 1.0), writes=[triu.B])
        S_.pool(lambda e: e.affine_select(out=triu.t[:], in_=triu.t[:], pattern=[[1, 128]],
                                          compare_op=ALU.is_ge, fill=0.0, base=0, channel_multiplier=-1),
                reads=[triu.B], writes=[triu.B])
        S_.pool(lambda e: e.memset(triu.t[0:64, 64:128], 0.0), writes=[triu.B])
        S_.pool(lambda e: e.memset(strl.t[:], 1.0), writes=[strl.B])
        S_.pool(lambda e: e.affine_select(out=strl.t[:], in_=strl.t[:], pattern=[[-1, 128]],
                                          compare_op=ALU.is_gt, fill=0.0, base=0, channel_multiplier=1),
                reads=[strl.B], writes=[strl.B])
        S_.pool(lambda e: e.memset(strl.t[64:128, 0:64], 0.0), writes=[strl.B])
        S_.dve(lambda e: e.tensor_copy(out=triu_b.t[:], in_=triu.t[:]), reads=[triu.B], writes=[triu_b.B])
        S_.dve(lambda e: e.tensor_copy(out=strl_b.t[:], in_=strl.t[:]), reads=[strl.B], writes=[strl_b.B])
        S_.pool(lambda e: e.iota(iot.t[:], pattern=[[-1, 128]], base=0, channel_multiplier=1), writes=[iot.B])
        S_.dve(lambda e: e.tensor_copy(out=dkq.t[:], in_=iot.t[:]), reads=[iot.B], writes=[dkq.B])
        S_.dve(lambda e: e.tensor_scalar_max(out=dkq.t[:], in0=dkq.t[:], scalar1=0.0), reads=[dkq.B], writes=[dkq.B])
        S_.dve(lambda e: e.tensor_scalar_mul(out=dkq.t[:], in0=dkq.t[:], scalar1=-2.0), reads=[dkq.B], writes=[dkq.B])
        for h in range(NH):
            def f(e, h=h):
                return e.tensor_scalar_mul(out=tdiag.t[:, h, :], in0=dkq.t[:], scalar1=slopec.t[:, h:h + 1])
            S_.dve(f, reads=[dkq.B, slopec.B], writes=[tdiag.B])
        S_.dve(lambda e: e.memset(tdiag.t[64:128, :, 0:64], NEG), reads=[tdiag.B], writes=[tdiag.B])
        S_.pool(lambda e: e.iota(kcoli.t[:], pattern=[[0, 1]], base=0, channel_multiplier=1), writes=[kcoli.B])
        S_.dve(lambda e: e.tensor_copy(out=kcol.t[:], in_=kcoli.t[:]), reads=[kcoli.B], writes=[kcol.B])
        for di in range(NDEL):
            delta = di - 4 * (NT - 1)

            def f(e, di=di, delta=delta):
                return e.tensor_scalar_add(out=kdel.t[:, di:di + 1], in0=kcol.t[:], scalar1=float(128 * delta - 256))
            S_.dve(f, reads=[kcol.B], writes=[kdel.B])
        for h in range(NH):
            S_.dve(lambda e, h=h: e.tensor_scalar_mul(out=kbias.t[:, h, :], in0=kdel.t[:, :], scalar1=slopec.t[:, h:h + 1]),
                   reads=[kdel.B, slopec.B], writes=[kbias.B])

        gb1 = sb("gb", [128, D], F32)
        gb2 = gb1
        cw = sb("cw", [128, 2, 3], F32)
        qg = sb("qg", [128, 1], F32)
        kg = sb("kg", [128, 1], F32)
        lamt = sb("lamt", [128, 4, 64], F32)
        lamw = sb("lamw", [128, 2, 64], F32)
        lams = sb("lams", [128, 2], F32)
        nlam = sb("nlam", [128, 1], F32)
        subg = sb("subg", [128, 1], F32)
        aw = sb("aw", [17, GW], F32)
        aw_hi = sb("aw_hi", [17, GW], BF16)
        aw_lo = sb("aw_lo", [17, GW], BF16)
        gng = sb("gng", [128, 1], F32)

        def load_params(l):
            S_.mark(f"params{l}")
            S_.dma(lambda e: e.dma_start(out=gb1.t[:], in_=ln1_g[l:l + 1, :].partition_broadcast(128)), writes=[gb1.B])
            for cc_ in range(NCV):
                for k_ in range(3):
                    S_.dma(lambda e, cc_=cc_, k_=k_: e.dma_start(
                        out=cw.t[:, cc_, k_:k_ + 1],
                        in_=conv_w[l, k_, cc_ * 128:(cc_ + 1) * 128].rearrange("(p o) -> p o", o=1)), writes=[cw.B])
            for hh in range(2):
                S_.dma(lambda e, hh=hh: e.dma_start(out=qg.t[hh * 64:(hh + 1) * 64, :],
                                                    in_=q_norm_g[l].rearrange("(p o) -> p o", o=1)), writes=[qg.B])
                S_.dma(lambda e, hh=hh: e.dma_start(out=kg.t[hh * 64:(hh + 1) * 64, :],
                                                    in_=k_norm_g[l].rearrange("(p o) -> p o", o=1)), writes=[kg.B])
                S_.dma(lambda e, hh=hh: e.dma_start(out=gng.t[hh * 64:(hh + 1) * 64, :],
                                                    in_=gla_norm_g[l].rearrange("(p o) -> p o", o=1)), writes=[gng.B])
            S_.dma(lambda e: e.dma_start(out=lamt.t[:].rearrange("p a b -> p (a b)"),
                                         in_=diff_lambda[l:l + 1].rearrange("o a b -> o (a b)").partition_broadcast(128)),
                   writes=[lamt.B])
            S_.dma(lambda e: e.dma_start(out=subg.t[:], in_=diff_subln_g[l].rearrange("(p o) -> p o", o=1)),
                   writes=[subg.B])
            S_.dma(lambda e: e.dma_start(out=aw.t[0:16, :], in_=gla_alpha_w[l]), writes=[aw.B])
            S_.dma(lambda e: e.dma_start(out=aw.t[16:17, :], in_=gla_alpha_b[l:l + 1, :]), writes=[aw.B])
            S_.dve(lambda e: e.tensor_copy(out=aw_hi.t[:], in_=aw.t[:]), reads=[aw.B], writes=[aw_hi.B])
            S_.dve(lambda e: e.tensor_tensor(out=aw_lo.t[:], in0=aw.t[:], in1=aw_hi.t[:], op=ALU.subtract),
                   reads=[aw.B, aw_hi.B], writes=[aw_lo.B])
            lam_init = 0.8 - 0.6 * math.exp(-0.3 * l)
            S_.dve(lambda e: e.tensor_scalar_mul(out=qg.t[:], in0=qg.t[:], scalar1=0.125), reads=[qg.B], writes=[qg.B])
            S_.dve(lambda e: e.tensor_tensor(out=lamw.t[:, 0, :], in0=lamt.t[:, 0, :], in1=lamt.t[:, 1, :], op=ALU.mult),
                   reads=[lamt.B], writes=[lamw.B])
            S_.dve(lambda e: e.tensor_tensor(out=lamw.t[:, 1, :], in0=lamt.t[:, 2, :], in1=lamt.t[:, 3, :], op=ALU.mult),
                   reads=[lamt.B], writes=[lamw.B])
            S_.dve(lambda e: e.tensor_reduce(out=lams.t[:], in_=lamw.t[:], axis=AX.X, op=ALU.add),
                   reads=[lamw.B], writes=[lams.B])
            S_.act(lambda e: e.activation(out=lams.t[:], in_=lams.t[:], func=AF.Exp), reads=[lams.B], writes=[lams.B])
            S_.dve(lambda e: e.tensor_tensor(out=nlam.t[:], in0=lams.t[:, 1:2], in1=lams.t[:, 0:1], op=ALU.subtract),
                   reads=[lams.B], writes=[nlam.B])
            S_.dve(lambda e: e.tensor_scalar_add(out=nlam.t[:], in0=nlam.t[:], scalar1=-lam_init),
                   reads=[nlam.B], writes=[nlam.B])
            S_.dve(lambda e: e.tensor_scalar_mul(out=subg.t[:], in0=subg.t[:], scalar1=1.0 - lam_init),
                   reads=[subg.B], writes=[subg.B])

        AWORDS = 48700
        ARENA = st.enter_context(nc.sbuf_tensor("arena", [128, AWORDS], F32))
        WW = (8 * D + 8 * DFF + 32 * D) // 2
        WREG = ARENA[:, 0:WW].bitcast(BF16)
        BW = Buf("wreg")
        BWo, BW1lo, BW1hi, BW2 = Buf("wout"), Buf("w1lo"), Buf("w1hi"), Buf("w2")
        win_v = WREG[:, 0:8 * DINL].rearrange("p (c n) -> p c n", c=8)
        wout_v = WREG[:, 0:8 * D].rearrange("p (c n) -> p c n", c=8)
        w1_v = WREG[:, 8 * D:8 * D + 8 * DFF].rearrange("p (c n) -> p c n", c=8)
        w2_v = WREG[:, 8 * D + 8 * DFF:].rearrange("p (c n) -> p c n", c=32)

        def phase1(l, x_src, Bx_src):
            S_.fence()
            for c in range(8):
                S_.dma(lambda e, c=c: e.dma_start(out=win_v[:, c, :], in_=w_in[l, c * 128:(c + 1) * 128, :]),
                       writes=[BW], q="pool")
            a = ARENA
            o = 8 * DINL // 2

            def carve(n, dt=F32, shape=None):
                nonlocal o
                v = a[:, o:o + n]
                o += n
                return v

            xt_v = [carve(4096).rearrange("p (j n) -> p j n", j=4) for _ in range(1)]
            Bxt = [Buf("xt0")]
            hb_raw = carve(2048)
            hb_v = hb_raw.bitcast(BF16).rearrange("p (j n) -> p j n", j=4)
            Bhb = Buf("hb")
            hT_raw = carve(2048)
            hT_v = hT_raw.bitcast(BF16).rearrange("p (c n) -> p c n", c=8)
            BhT = Buf("hT")
            junk_v = carve(512).bitcast(BF16)
            Bjunk = Buf("junk")
            ss_v = carve(4)
            rs_v = carve(4)
            Bss, Brs = Buf("ss"), Buf("rs")
            u_v = carve(512)
            Bu = Buf("u")
            zc_v = [carve(514) for _ in range(2)]
            Bzc = [Buf("zc0"), Buf("zc1")]
            acc_v = carve(512)
            Bacc = Buf("acc")
            yc_v = carve(256).bitcast(BF16)
            Byc = Buf("yc")
            sq_v = carve(256).bitcast(BF16)
            Bsq = Buf("sq")
            sq2_v = [carve(256).bitcast(BF16) for _ in range(2)]
            Bsq2 = [Buf("sq2a"), Buf("sq2b")]
            rstd_v = carve(512)
            Brstd = Buf("rstd")
            qk_v = carve(256).bitcast(BF16)
            Bqk = Buf("qk")
            vt_v = carve(256).bitcast(BF16)[:, 0:VW]
            Bvt = Buf("vt")
            aT_v = carve(512)
            BaT = Buf("aT")
            aTh_v = carve(256).bitcast(BF16)
            aTl_v = carve(256).bitcast(BF16)
            BaTh, BaTl = Buf("aTh"), Buf("aTl")
            lah_v = carve(2 * GW).bitcast(BF16).rearrange("p (j n) -> p j n", j=4)
            lal_v = carve(2 * GW).bitcast(BF16).rearrange("p (j n) -> p j n", j=4)
            Blah, Blal = Buf("lah"), Buf("lal")
            la_v = carve(4 * GW).rearrange("p (j n) -> p j n", j=4)
            Bla = Buf("la")
            ebT_v = [carve(512) for _ in range(2)]
            enbT_v = [carve(512) for _ in range(2)]
            BebT = [Buf("ebT0"), Buf("ebT1")]
            BenbT = [Buf("enbT0"), Buf("enbT1")]
            erev_v = carve(4 * GW).rearrange("p (j n) -> p j n", j=4)
            Berev = Buf("erev")
            qinm_v = carve(512 * NP).bitcast(BF16).rearrange("p (c h n) -> p c h n", c=NP, h=2)
            kin_v = carve(256 * NP).bitcast(BF16).rearrange("p (c n) -> p c n", c=NP)
            Bqin, Bkin = [Buf("qin0"), Buf("qin1")], [Buf("kin0"), Buf("kin1")]
            erevm_v = [carve(4 * GW).rearrange("p (j n) -> p j n", j=4) for _ in range(2)]
            Berevm = [Buf("erevm0"), Buf("erevm1")]
            kendm_v = carve(4 * GW).bitcast(BF16).rearrange("p (j c n) -> p j c n", j=4, c=2)
            vtok_v = carve(2 * GW).bitcast(BF16).rearrange("p (j n) -> p j n", j=4)
            vpad_v = carve(512 * NP).bitcast(BF16).rearrange("p (j a b n) -> p j a b n", j=4, a=NP, b=2)
            Bkend, Bvtok = [Buf(f"kend{j}") for j in range(4)], [Buf(f"vtok{j}") for j in range(4)]
            Bvpad = [Buf(f"vpad{j}") for j in range(4)]
            am_v = carve(128 * NP).bitcast(BF16).rearrange("p (h n) -> p h n", h=2 * NP)
            Bam = [Buf("am0"), Buf("am1")]
            Sf_v = carve(128 * NP).rearrange("p (c n) -> p c n", c=NP)
            BSf = Buf("Sf")
            NSL = 4
            Sbf_v = carve(64 * NP * NSL).bitcast(BF16).rearrange("p (s c n) -> p s c n", s=NSL, c=NP)
            BSbf = [Buf(f"Sbf{i}") for i in range(NSL)]
            sg_v = carve(512)
            Bsg = Buf("sg")
            t1_v = carve(512)
            Bt1 = Buf("t1")
            yg_v = carve(256).bitcast(BF16)
            Byg = Buf("yg")
            assert o <= AWORDS, o

            S_.dve(lambda e: e.memset(aT_v[:, :], 1.0), writes=[BaT])
            S_.dve(lambda e: e.memset(aTh_v[:, :], 1.0), writes=[BaTh])
            S_.dve(lambda e: e.memset(aTl_v[:, :], 0.0), writes=[BaTl])
            S_.dve(lambda e: e.memset(Sf_v[:, :, :], 0.0), writes=[BSf])
            S_.pool(lambda e: e.memset(vpad_v[:, :, :, :, :], 0.0), writes=Bvpad)
            for i in range(2):
                S_.dve(lambda e, i=i: e.memset(zc_v[i][:, 0:2], 0.0), writes=[Bzc[i]])

            rr = [0]

            def nb():
                k = rr[0] % 4
                rr[0] += 1
                return k

            cp = [0]

            def copy_rr(out, in_, reads, writes):
                cp[0] += 1
                if cp[0] % 2 == 0:
                    S_.act(lambda e: e.copy(out=out, in_=in_), reads=reads, writes=writes)
                else:
                    S_.dve(lambda e: e.tensor_copy(out=out, in_=in_), reads=reads, writes=writes)

            def fm_group(col0, ncols, bk, t):
                for c in range(8):
                    S_.pe(lambda e, c=c: e.matmul(bank(bk)[0:ncols, :], lhsT=win_v[:, c, col0:col0 + ncols],
                                                  rhs=hT_v[:, c, :], start=(c == 0), stop=(c == 7)),
                          reads=[BW, BhT], writes=[bankB[bk]])

            chunk_ctr = [0]

            def sec_norm(tn):
                tok0 = tn * 512
                xt = xt_v[0]
                bxt = Bxt[0]
                S_.dma(lambda e, xt=xt, tok0=tok0: e.dma_start(
                    out=xt, in_=x_src[tok0:tok0 + 512, :].rearrange("(j p) n -> p j n", p=128)),
                    reads=[Bx_src], writes=[bxt])
                for j in range(4):
                    S_.act(lambda e, j=j, xt=xt: e.activation(out=junk_v[:, :], in_=xt[:, j, :], func=AF.Square,
                                                              accum_out=ss_v[:, j:j + 1]),
                           reads=[bxt], writes=[Bjunk, Bss])
                S_.act(lambda e: e.activation(out=rs_v[:, :], in_=ss_v[:, :], func=AF.Ln, bias=epsc.t[:, 0:1], scale=1.0 / D),
                       reads=[Bss, epsc.B], writes=[Brs])
                S_.act(lambda e: e.activation(out=rs_v[:, :], in_=rs_v[:, :], func=AF.Exp, scale=-0.5),
                       reads=[Brs], writes=[Brs])
                for j in range(4):
                    S_.dve(lambda e, j=j, xt=xt: e.scalar_tensor_tensor(out=hb_v[:, j, :], in0=xt[:, j, :],
                                                                        scalar=rs_v[:, j:j + 1], in1=gb1.t[:, :],
                                                                        op0=ALU.mult, op1=ALU.mult),
                           reads=[bxt, Brs, gb1.B], writes=[Bhb])


            sec_norm(0)
            for t in range(NT):
                S_.mark(f"p1_tile{t}")
                tok0 = t * 512
                S_.mark(f"p1_t{t}_transposes")
                for c2 in range(4):
                    for cc_ in range(2):
                        c = c2 * 2 + cc_
                        for j in range(4):
                            S_.pe(lambda e, c=c, j=j, cc_=cc_: e.transpose(
                                out=PSb[:, cc_ * 512 + j * 128: cc_ * 512 + (j + 1) * 128],
                                in_=hb_v[:, j, c * 128:(c + 1) * 128], identity=ident.t[:, :]),
                                reads=[Bhb, ident.B], writes=[bankB[7]])
                    copy_rr(hT_v[:, c2 * 2:c2 * 2 + 2, :], PSb[:, :].rearrange("p (c n) -> p c n", c=2), [bankB[7]], [BhT])

                def sec_conv():
                    S_.mark(f"p1_t{t}_conv")
                    for cc in range(NCV):
                        bu = nb()
                        fm_group(C_U + cc * 128, 128, bu, t)
                        S_.act(lambda e, bu=bu: e.copy(out=u_v[:, :], in_=bank(bu)), reads=[bankB[bu]], writes=[Bu])
                        bc = nb()
                        fm_group(C_CC + cc * 128, 128, bc, t)
                        S_.dve(lambda e, bc=bc, cc=cc: e.tensor_tensor(out=zc_v[cc][:, 2:514], in0=bank(bc), in1=u_v[:, :],
                                                                       op=ALU.mult),
                               reads=[bankB[bc], Bu], writes=[Bzc[cc]])
                        S_.dve(lambda e, cc=cc: e.tensor_scalar_mul(out=acc_v[:, :], in0=zc_v[cc][:, 2:514],
                                                                    scalar1=cw.t[:, cc, 2:3]),
                               reads=[Bzc[cc], cw.B], writes=[Bacc])
                        S_.dve(lambda e, cc=cc: e.scalar_tensor_tensor(out=acc_v[:, :], in0=zc_v[cc][:, 1:513],
                                                                       scalar=cw.t[:, cc, 1:2], in1=acc_v[:, :],
                                                                       op0=ALU.mult, op1=ALU.add),
                               reads=[Bzc[cc], cw.B, Bacc], writes=[Bacc])
                        S_.dve(lambda e, cc=cc: e.scalar_tensor_tensor(out=acc_v[:, :], in0=zc_v[cc][:, 0:512],
                                                                       scalar=cw.t[:, cc, 0:1], in1=acc_v[:, :],
                                                                       op0=ALU.mult, op1=ALU.add),
                               reads=[Bzc[cc], cw.B, Bacc], writes=[Bacc])
                        bb = nb()
                        fm_group(C_CB + cc * 128, 128, bb, t)
                        S_.dve(lambda e, bb=bb: e.tensor_tensor(out=yc_v[:, :], in0=bank(bb), in1=acc_v[:, :], op=ALU.mult),
                               reads=[bankB[bb], Bacc], writes=[Byc])
                        S_.dma(lambda e, cc=cc, tok0=tok0: e.dma_start(out=yT_d[cc, :, tok0:tok0 + 512], in_=yc_v[:, :]),
                               reads=[Byc], writes=[ByT])
                        S_.dve(lambda e, cc=cc: e.tensor_copy(out=zc_v[cc][:, 0:2], in_=zc_v[cc][:, 512:514]),
                               reads=[Bzc[cc]], writes=[Bzc[cc]])


                def sec_qk():
                    S_.mark(f"p1_t{t}_qk")
                    groups_ = [(which, h) for which in range(2) for h in range(NH)]
                    pend = None
                    for gi, g in enumerate(groups_ + [None]):
                        if g is not None:
                            which, h = g
                            col0 = (C_Q if which == 0 else C_K) + h * 128
                            bz = nb()
                            fm_group(col0, 128, bz, t)
                            sqi = gi % 2
                            S_.act(lambda e, bz=bz, sqi=sqi: e.activation(out=sq2_v[sqi][:, :], in_=bank(bz), func=AF.Square),
                                   reads=[bankB[bz]], writes=[Bsq2[sqi]])
                        if pend is not None:
                            which_p, h_p, bz_p, sqi_p = pend
                            bm = nb()
                            S_.pe(lambda e, bm=bm, sqi_p=sqi_p: e.matmul(bank(bm), lhsT=blk64.t[:, :], rhs=sq2_v[sqi_p][:, :],
                                                                         start=True, stop=True),
                                  reads=[Bsq2[sqi_p], blk64.B], writes=[bankB[bm]])
                            S_.act(lambda e, bm=bm: e.activation(out=rstd_v[:, :], in_=bank(bm), func=AF.Ln, bias=epsc.t[:, 0:1]),
                                   reads=[bankB[bm], epsc.B], writes=[Brstd])
                            S_.act(lambda e: e.activation(out=rstd_v[:, :], in_=rstd_v[:, :], func=AF.Exp, scale=-0.5),
                                   reads=[Brstd], writes=[Brstd])
                            gcol = qg if which_p == 0 else kg
                            S_.dve(lambda e, bz_p=bz_p, gcol=gcol: e.scalar_tensor_tensor(
                                out=qk_v[:, :], in0=bank(bz_p), scalar=gcol.t[:, 0:1], in1=rstd_v[:, :],
                                op0=ALU.mult, op1=ALU.mult),
                                reads=[bankB[bz_p], gcol.B, Brstd], writes=[Bqk])
                            dst = qT_d if which_p == 0 else kT_d
                            Bdst = BqT if which_p == 0 else BkT
                            S_.dma(lambda e, dst=dst, h_p=h_p, tok0=tok0: e.dma_start(out=dst[h_p, :, tok0:tok0 + 512], in_=qk_v[:, :]),
                                   reads=[Bqk], writes=[Bdst])
                        pend = (g[0], g[1], bz, gi % 2) if g is not None else None

                def sec_v():
                    S_.mark(f"p1_t{t}_v")
                    for j in range(4):
                        bv = nb()
                        for c in range(8):
                            S_.pe(lambda e, c=c, j=j, bv=bv: e.matmul(bank(bv)[:, 0:VW], lhsT=hT_v[:, c, j * 128:(j + 1) * 128],
                                                                      rhs=win_v[:, c, C_V:C_V + VW], start=(c == 0), stop=(c == 7)),
                                  reads=[BW, BhT], writes=[bankB[bv]])
                        copy_rr(vt_v[:, :], bank(bv)[:, 0:VW], [bankB[bv]], [Bvt])
                        S_.dma(lambda e, j=j, tok0=tok0: e.dma_start(out=v_d[tok0 + j * 128: tok0 + (j + 1) * 128, :], in_=vt_v[:, :]),
                               reads=[Bvt], writes=[Bv])


                def sec_gla():
                    S_.mark(f"p1_t{t}_gla")
                    ba = nb()
                    fm_group(C_GA, 16, ba, t)
                    S_.act(lambda e, ba=ba: e.copy(out=aT_v[0:16, :], in_=bank(ba)[0:16, :]), reads=[bankB[ba]], writes=[BaT])
                    S_.dve(lambda e: e.tensor_copy(out=aTh_v[0:16, :], in_=aT_v[0:16, :]), reads=[BaT], writes=[BaTh])
                    S_.dve(lambda e: e.tensor_tensor(out=aTl_v[0:16, :], in0=aT_v[0:16, :], in1=aTh_v[0:16, :], op=ALU.subtract),
                           reads=[BaT, BaTh], writes=[BaTl])
                    for j in range(4):
                        bk = 4 + j // BPB
                        passes = [(aTh_v, BaTh, aw_hi), (aTl_v, BaTl, aw_hi), (aTh_v, BaTh, aw_lo)]
                        for pi, (av, aB, wv) in enumerate(passes):
                            S_.pe(lambda e, j=j, bk=bk, av=av, wv=wv, pi=pi: e.matmul(
                                bank(bk)[:, (j % BPB) * GW:(j % BPB + 1) * GW],
                                lhsT=av[0:17, j * 128:(j + 1) * 128], rhs=wv.t[0:17, :],
                                start=(pi == 0), stop=(pi == 2)),
                                reads=[aB, wv.B], writes=[bankB[bk]])
                    for half in range(4 // BPB):
                        S_.act(lambda e, half=half: e.activation(out=la_v[:, BPB * half:BPB * half + BPB, :].rearrange("p j n -> p (j n)"),
                                                                 in_=bank(4 + half), func=AF.Exp, scale=-1.0),
                               reads=[bankB[4 + half]], writes=[Bla])
                    S_.act(lambda e: e.activation(out=la_v[:, :, :].rearrange("p j n -> p (j n)"),
                                                  in_=la_v[:, :, :].rearrange("p j n -> p (j n)"), func=AF.Ln, bias=onec.t[:, 0:1]),
                           reads=[Bla, onec.B], writes=[Bla])
                    S_.dve(lambda e: e.tensor_copy(out=lah_v[:, :, :], in_=la_v[:, :, :]), reads=[Bla], writes=[Blah])
                    S_.dve(lambda e: e.tensor_tensor(out=lal_v[:, :, :], in0=la_v[:, :, :], in1=lah_v[:, :, :], op=ALU.subtract),
                           reads=[Bla, Blah], writes=[Blal])

                def sec_gla_cumsum():
                    S_.mark(f"p1_t{t}_gla_cumsum")
                    for p in range(NP):
                        bk = 4 + p
                        for j in range(4):
                            for pi, (lv, lB) in enumerate([(lah_v, Blah), (lal_v, Blal)]):
                                S_.pe(lambda e, p=p, j=j, bk=bk, lv=lv, pi=pi: e.matmul(
                                    bank(bk)[:, j * 128:(j + 1) * 128],
                                    lhsT=lv[:, j, p * 128:(p + 1) * 128], rhs=triu_b.t[:, :],
                                    start=(pi == 0), stop=(pi == 1)),
                                    reads=[lB, triu_b.B], writes=[bankB[bk]])
                        S_.act(lambda e, p=p, bk=bk: e.activation(out=ebT_v[p][:, :], in_=bank(bk), func=AF.Exp, scale=-1.0 / 16),
                               reads=[bankB[bk]], writes=[BebT[p]])
                        S_.act(lambda e, p=p, bk=bk: e.activation(out=enbT_v[p][:, :], in_=bank(bk), func=AF.Exp, scale=1.0 / 16),
                               reads=[bankB[bk]], writes=[BenbT[p]])
                    for j in range(4):
                        bk = 4 + j // BPB
                        for pi, (lv, lB) in enumerate([(lah_v, Blah), (lal_v, Blal)]):
                            S_.pe(lambda e, j=j, bk=bk, lv=lv, pi=pi: e.matmul(
                                bank(bk)[:, (j % BPB) * GW:(j % BPB + 1) * GW],
                                lhsT=strl_b.t[:, :], rhs=lv[:, j, :], start=(pi == 0), stop=(pi == 1)),
                                reads=[lB, strl_b.B], writes=[bankB[bk]])
                    for half in range(4 // BPB):
                        S_.act(lambda e, half=half: e.activation(out=erev_v[:, BPB * half:BPB * half + BPB, :].rearrange("p j n -> p (j n)"),
                                                                 in_=bank(4 + half), func=AF.Exp, scale=-1.0 / 16),
                               reads=[bankB[4 + half]], writes=[Berev])

                def sec_gla_qin():
                    S_.mark(f"p1_t{t}_gla_qin")
                    for p in range(NP):
                        bq = nb()
                        fm_group(C_GQ + p * 128, 128, bq, t)
                        for hl in range(2):
                            S_.dve(lambda e, p=p, hl=hl, bq=bq: e.scalar_tensor_tensor(
                                out=qinm_v[:, p, hl, :], in0=bank(bq), scalar=hmask.t[:, hl:hl + 1],
                                in1=ebT_v[p][:, :], op0=ALU.mult, op1=ALU.mult),
                                reads=[bankB[bq], BebT[p], hmask.B], writes=[Bqin[p]])
                        bkk = nb()
                        fm_group(C_GK + p * 128, 128, bkk, t)
                        S_.dve(lambda e, p=p, bkk=bkk: e.tensor_tensor(out=kin_v[:, p, :], in0=bank(bkk), in1=enbT_v[p][:, :],
                                                                       op=ALU.mult),
                               reads=[bankB[bkk], BenbT[p]], writes=[Bkin[p]])
                    for j in range(4):
                        bv = nb()
                        for c in range(8):
                            S_.pe(lambda e, c=c, j=j, bv=bv: e.matmul(bank(bv)[:, 0:2 * GW], lhsT=hT_v[:, c, j * 128:(j + 1) * 128],
                                                                      rhs=win_v[:, c, C_GK:C_GK + 2 * GW], start=(c == 0), stop=(c == 7)),
                                  reads=[BW, BhT], writes=[bankB[bv]])
                        for cc in range(2):
                            S_.dve(lambda e, j=j, bv=bv, cc=cc: e.scalar_tensor_tensor(
                                out=kendm_v[:, j, cc, :], in0=bank(bv)[:, 0:GW], scalar=cmask.t[:, cc:cc + 1],
                                in1=erev_v[:, j, :], op0=ALU.mult, op1=ALU.mult),
                                reads=[bankB[bv], Berev, cmask.B], writes=[Bkend[j]])
                        S_.dve(lambda e, j=j, bv=bv: e.tensor_copy(out=vtok_v[:, j, :], in_=bank(bv)[:, GW:2 * GW]),
                               reads=[bankB[bv]], writes=[Bvtok[j]])
                        for hl in range(2):
                            S_.pool(lambda e, j=j, hl=hl: e.tensor_copy(
                                out=vpad_v[:, j, :, hl, hl * 64:(hl + 1) * 64],
                                in_=vtok_v[:, j, :].rearrange("p (a b e) -> p a b e", a=NP, b=2)[:, :, hl, :]),
                                reads=[Bvtok[j]], writes=[Bvpad[j]])

                def sec_gla_blocks():
                    S_.mark(f"p1_t{t}_gla_blocks")
                    for j in range(4):
                        bas = [nb() for _ in range(NP)]
                        for h in range(2 * NP):
                            p, hl = h // 2, h % 2
                            S_.pe(lambda e, p=p, hl=hl, j=j, bas=bas: e.matmul(
                                bank(bas[p])[:, hl * 128:(hl + 1) * 128],
                                lhsT=kin_v[:, p, j * 128:(j + 1) * 128],
                                rhs=qinm_v[:, p, hl, j * 128:(j + 1) * 128], start=True, stop=True),
                                reads=[Bkin[p], Bqin[p]], writes=[bankB[bas[p]]])
                        for p in range(NP):
                            S_.dve(lambda e, p=p, bas=bas: e.tensor_tensor(
                                out=am_v[:, 2 * p:2 * p + 2, :], in0=bank(bas[p])[:, 0:256].rearrange("p (h n) -> p h n", h=2),
                                in1=triu.t[:, :].unsqueeze(1).broadcast_to([128, 2, 128]), op=ALU.mult),
                                reads=[bankB[bas[p]], triu.B], writes=[Bam[p]])
                        for p in range(NP):
                            ob = bank(4 + p)[:, j * 128:(j + 1) * 128]
                            for hl in range(2):
                                S_.pe(lambda e, p=p, hl=hl, j=j, ob=ob: e.matmul(
                                    ob, lhsT=vpad_v[:, j, p, hl, :], rhs=am_v[:, 2 * p + hl, :], start=(hl == 0), stop=False),
                                    reads=[Bvpad[j], Bam[p]], writes=[bankB[4 + p]])
                        for cc in range(2):
                            ch = chunk_ctr[0]
                            chunk_ctr[0] += 1
                            sl = ch % NSL
                            S_.pool(lambda e, sl=sl: e.tensor_copy(out=Sbf_v[:, sl, :, :], in_=Sf_v[:, :, :]),
                                    reads=[BSf], writes=[BSbf[sl]])
                            for p in range(NP):
                                obc = bank(4 + p)[:, j * 128 + cc * 64: j * 128 + (cc + 1) * 64]
                                for hl in range(2):
                                    S_.pe(lambda e, p=p, hl=hl, j=j, cc=cc, sl=sl, obc=obc: e.matmul(
                                        obc, lhsT=Sbf_v[:, sl, p, :],
                                        rhs=qinm_v[:, p, hl, j * 128 + cc * 64: j * 128 + (cc + 1) * 64],
                                        start=False, stop=(cc == 1 and hl == 1)),
                                        reads=[BSbf[sl], Bqin[p]], writes=[bankB[4 + p]])
                            for p in range(NP):
                                S_.pe(lambda e, p=p, j=j, cc=cc: e.matmul(
                                    bank(6)[:, p * 128:(p + 1) * 128],
                                    lhsT=kendm_v[:, j, cc, p * 128:(p + 1) * 128],
                                    rhs=vtok_v[:, j, p * 128:(p + 1) * 128], start=True, stop=True),
                                    reads=[Bkend[j], Bvtok[j]], writes=[bankB[6]])
                            col = j * 128 + cc * 64 + 63
                            for p in range(NP):
                                for hl in range(2):
                                    S_.dve(lambda e, p=p, hl=hl, col=col: e.scalar_tensor_tensor(
                                        out=Sf_v[hl * 64:(hl + 1) * 64, p, hl * 64:(hl + 1) * 64],
                                        in0=Sf_v[hl * 64:(hl + 1) * 64, p, hl * 64:(hl + 1) * 64],
                                        scalar=ebT_v[p][hl * 64:(hl + 1) * 64, col:col + 1],
                                        in1=bank(6)[hl * 64:(hl + 1) * 64, p * 128 + hl * 64: p * 128 + (hl + 1) * 64],
                                        op0=ALU.mult, op1=ALU.add),
                                        reads=[BSf, BebT[p], bankB[6]], writes=[BSf])

                def sec_gla_out():
                    S_.mark(f"p1_t{t}_gla_out")
                    for p in range(NP):
                        bo = 4 + p
                        S_.act(lambda e, bo=bo: e.activation(out=sq_v[:, :], in_=bank(bo), func=AF.Square),
                               reads=[bankB[bo]], writes=[Bsq])
                        bm = nb()
                        S_.pe(lambda e, bm=bm: e.matmul(bank(bm), lhsT=blk64.t[:, :], rhs=sq_v[:, :], start=True, stop=True),
                              reads=[Bsq, blk64.B], writes=[bankB[bm]])
                        S_.act(lambda e, bm=bm: e.activation(out=rstd_v[:, :], in_=bank(bm), func=AF.Ln, bias=epsc.t[:, 0:1]),
                               reads=[bankB[bm], epsc.B], writes=[Brstd])
                        S_.act(lambda e: e.activation(out=rstd_v[:, :], in_=rstd_v[:, :], func=AF.Exp, scale=-0.5),
                               reads=[Brstd], writes=[Brstd])
                        S_.dve(lambda e, bo=bo: e.scalar_tensor_tensor(out=t1_v[:, :], in0=bank(bo), scalar=gng.t[:, 0:1],
                                                                       in1=rstd_v[:, :], op0=ALU.mult, op1=ALU.mult),
                               reads=[bankB[bo], gng.B, Brstd], writes=[Bt1])
                        bg = nb()
                        fm_group(C_GG + p * 128, 128, bg, t)
                        S_.act(lambda e, bg=bg: e.activation(out=sg_v[:, :], in_=bank(bg), func=AF.Silu),
                               reads=[bankB[bg]], writes=[Bsg])
                        S_.dve(lambda e: e.tensor_tensor(out=yg_v[:, :], in0=t1_v[:, :], in1=sg_v[:, :], op=ALU.mult),
                               reads=[Bt1, Bsg], writes=[Byg])
                        S_.dma(lambda e, p=p, tok0=tok0: e.dma_start(out=yT_d[NCV + NH + p, :, tok0:tok0 + 512], in_=yg_v[:, :]),
                               reads=[Byg], writes=[ByT])

                sec_gla()
                sec_conv()
                sec_gla_cumsum()
                sec_qk()
                if t + 1 < NT:
                    sec_norm(t + 1)
                sec_gla_qin()
                sec_v()
                sec_gla_blocks()
                sec_gla_out()


        def phase2(l):
            S_.mark(f"p2_{l}")
            S_.fence()
            a = ARENA
            o = 0

            def carve(n):
                nonlocal o
                v = a[:, o:o + n]
                o += n
                return v
            KT_v = carve(S // 2).bitcast(BF16)
            QT_v = carve(S).bitcast(BF16).rearrange("p (c n) -> p c n", c=2)
            V_v = carve(S // 2).bitcast(BF16).rearrange("p (b e) -> p b e", e=128)
            BKT, BQT, BV = Buf("KT"), Buf("QT"), Buf("V")
            prefetch_w = (o <= 16384) and not pair
            if prefetch_w:
                o = WW
                for c in range(6, 8):
                    S_.dma(lambda e, c=c: e.dma_start(out=w1_v[:, c, :], in_=w_mlp1[l, c * 128:(c + 1) * 128, :]),
                           writes=[BW1hi], q="pool")
                for c in range(32):
                    S_.dma(lambda e, c=c: e.dma_start(out=w2_v[:, c, :], in_=w_mlp2[l, c * 128:(c + 1) * 128, :]),
                           writes=[BW2], q="pool")
            PT_v = [carve(512).bitcast(BF16).rearrange("p (c n) -> p c n", c=2) for _ in range(2)]
            BPT = [Buf("PT0"), Buf("PT1")]
            r_v = carve(1024).rearrange("p (c n) -> p c n", c=2)
            Br = Buf("r")
            Oc_v = [carve(1024).rearrange("p (c n) -> p c n", c=2) for _ in range(2)]
            BOc = [Buf("Oc0"), Buf("Oc1")]
            sums_v = [carve(1024).rearrange("p (c n) -> p c n", c=2) for _ in range(2)]
            Bsums = [Buf("sums0"), Buf("sums1")]
            o_v = [carve(512) for _ in range(2)]
            Bo = [Buf("o0"), Buf("o1")]
            tile_ctr = [0]
            cur_it = [0]
            sq_v = carve(256).bitcast(BF16)
            Bsq = Buf("sq2")
            rstd_v = carve(512)
            Brstd = Buf("rstd2")
            y_v = carve(256).bitcast(BF16)
            By = Buf("y2")
            assert o <= AWORDS, o
            S_.dve(lambda e: e.memset(QT_v[64:128, 0, :], 0.0), writes=[BQT])
            S_.dve(lambda e: e.memset(QT_v[0:64, 1, :], 0.0), writes=[BQT])
            for h in range(NH):
                S_.dma(lambda e, h=h: e.dma_start(out=KT_v[:, :], in_=kT_d[h, :, :]), reads=[BkT], writes=[BKT])
                for comp in range(2):
                    S_.dma(lambda e, h=h, comp=comp: e.dma_start(out=QT_v[comp * 64:(comp + 1) * 64, comp, :],
                                                                 in_=qT_d[h, comp * 64:(comp + 1) * 64, :]),
                           reads=[BqT], writes=[BQT])
                S_.dma(lambda e, h=h: e.dma_start(out=V_v[:, :, :],
                                                  in_=v_d[:, h * 128:(h + 1) * 128].rearrange("(b p) e -> p b e", p=128)),
                       reads=[Bv], writes=[BV])
                THR = 64.0
                its = []
                for qt in range(NT):
                    q0 = qt * 512
                    kbs = []
                    for kb in range(4 * qt + 4):
                        dmin = q0 - (kb * 128 + 127)
                        if (not pair) and kb < 4 * qt and slopes[h] * dmin > THR:
                            continue
                        kbs.append(kb)
                    for n_, kb in enumerate(kbs):
                        its.append(dict(qt=qt, q0=q0, kb=kb, first=(n_ == 0), last=(n_ == len(kbs) - 1), idx=len(its)))

                def emit_qk(itx, h=h):
                    kb, qt, q0 = itx["kb"], itx["qt"], itx["q0"]
                    j = kb - 4 * qt
                    c0 = 0 if j < 0 else j * 128
                    sbk = (itx["idx"] % 2) * 2
                    diag = j >= 0
                    for comp in range(2):
                        S_.pe(lambda e, comp=comp: e.matmul(
                            bank(sbk + comp)[:, c0:512], lhsT=KT_v[:, kb * 128:(kb + 1) * 128],
                            rhs=QT_v[:, comp, q0 + c0:q0 + 512], start=True, stop=(not diag)),
                            reads=[BKT, BQT], writes=[bankB[sbk + comp]])
                        if diag:
                            S_.pe(lambda e, comp=comp: e.matmul(
                                bank(sbk + comp)[:, c0:c0 + 128], lhsT=ident.t[:, :], rhs=tdiag.t[:, h, :],
                                start=False, stop=True),
                                reads=[ident.B, tdiag.B], writes=[bankB[sbk + comp]])

                def emit_exp_pv(itx, h=h):
                    kb, qt, q0 = itx["kb"], itx["qt"], itx["q0"]
                    j = kb - 4 * qt
                    c0 = 0 if j < 0 else j * 128
                    sbk = (itx["idx"] % 2) * 2
                    pt = itx["idx"] % 2
                    di = (kb - 4 * qt) + 4 * (NT - 1)
                    first, last = itx["first"], itx["last"]
                    S_.act(lambda e: e.activation(
                        out=PT_v[pt][:, :, c0:512],
                        in_=PSf[:, sbk * 512:(sbk + 2) * 512].rearrange("p (c n) -> p c n", c=2)[:, :, c0:512],
                        func=AF.Exp, bias=kbias.t[:, h, di:di + 1], scale=1.0),
                        reads=[bankB[sbk], bankB[sbk + 1], kbias.B], writes=[BPT[pt]])
                    for comp in range(2):
                        S_.pe(lambda e, comp=comp: e.matmul(
                            bank(4 + comp)[:, c0:512], lhsT=V_v[:, kb, :], rhs=PT_v[pt][:, comp, c0:512],
                            start=first, stop=last),
                            reads=[BV, BPT[pt]], writes=[bankB[4 + comp]])
                        S_.pe(lambda e, comp=comp: e.matmul(
                            bank(6 + comp)[:, c0:512], lhsT=ones_bf.t[:, :], rhs=PT_v[pt][:, comp, c0:512],
                            start=first, stop=last),
                            reads=[ones_bf.B, BPT[pt]], writes=[bankB[6 + comp]])

                def stage_a(itx, h=h):
                    ab = itx["tile"] % 2
                    S_.dve(lambda e: e.tensor_copy(out=Oc_v[ab][:, :, :],
                                                   in_=PSf[:, 4 * 512:6 * 512].rearrange("p (c n) -> p c n", c=2)),
                           reads=[bankB[4], bankB[5]], writes=[BOc[ab]])
                    S_.dve(lambda e: e.tensor_copy(out=sums_v[ab][:, :, :],
                                                   in_=PSf[:, 6 * 512:8 * 512].rearrange("p (c n) -> p c n", c=2)),
                           reads=[bankB[6], bankB[7]], writes=[Bsums[ab]])

                def stage_b(itx, h=h):
                    ab = itx["tile"] % 2
                    S_.dve(lambda e: e.reciprocal(out=r_v[:, :, :], in_=sums_v[ab][:, :, :]), reads=[Bsums[ab]], writes=[Br])
                    S_.dve(lambda e: e.tensor_tensor(out=Oc_v[ab][:, :, :], in0=Oc_v[ab][:, :, :], in1=r_v[:, :, :], op=ALU.mult),
                           reads=[BOc[ab], Br], writes=[BOc[ab]])
                    S_.dve(lambda e: e.scalar_tensor_tensor(out=o_v[ab][:, :], in0=Oc_v[ab][:, 1, :], scalar=nlam.t[:, 0:1],
                                                            in1=Oc_v[ab][:, 0, :], op0=ALU.mult, op1=ALU.add),
                           reads=[BOc[ab], nlam.B], writes=[Bo[ab]])
                    S_.dve(lambda e: e.tensor_tensor(out=sq_v[:, :], in0=o_v[ab][:, :], in1=o_v[ab][:, :], op=ALU.mult),
                           reads=[Bo[ab]], writes=[Bsq])

                def stage_c(itx, h=h):
                    ab = itx["tile"] % 2
                    q0 = itx["q0"]
                    sbk = (cur_it[0] % 2) * 2
                    S_.pe(lambda e: e.matmul(bank(sbk), lhsT=o128.t[:, :], rhs=sq_v[:, :], start=True, stop=True),
                          reads=[o128.B, Bsq], writes=[bankB[sbk]])
                    S_.act(lambda e: e.activation(out=rstd_v[:, :], in_=bank(sbk), func=AF.Ln, bias=epsc.t[:, 0:1]),
                           reads=[bankB[sbk], epsc.B], writes=[Brstd])
                    S_.act(lambda e: e.activation(out=rstd_v[:, :], in_=rstd_v[:, :], func=AF.Exp, scale=-0.5),
                           reads=[Brstd], writes=[Brstd])
                    S_.dve(lambda e: e.scalar_tensor_tensor(out=y_v[:, :], in0=o_v[ab][:, :], scalar=subg.t[:, 0:1], in1=rstd_v[:, :],
                                                            op0=ALU.mult, op1=ALU.mult),
                           reads=[Bo[ab], subg.B, Brstd], writes=[By])
                    S_.dma(lambda e: e.dma_start(out=yT_d[NCV + h, :, q0:q0 + 512], in_=y_v[:, :]),
                           reads=[By], writes=[ByT])

                for itx in its:
                    if itx["first"]:
                        tile_ctr[0] += 1
                    itx["tile"] = tile_ctr[0]
                pending = []

                def tick():
                    for p_ in pending:
                        p_[0] -= 1
                    while pending and pending[0][0] <= 0:
                        _, fn_, it_ = pending.pop(0)
                        fn_(it_)

                emit_qk(its[0])
                for i_, itx in enumerate(its):
                    if i_ + 1 < len(its):
                        emit_qk(its[i_ + 1])
                    emit_exp_pv(itx)
                    cur_it[0] = itx["idx"]
                    tick()
                    if itx["last"]:
                        stage_a(itx)
                        pending.append([1, stage_b, itx])
                        pending.append([5, stage_c, itx])
                while pending:
                    _, fn_, it_ = pending.pop(0)
                    fn_(it_)

        def phase3(l, x_src, Bx_src, x_dst, Bx_dst, is_out):
            TT = 256
            NJ = TT // 128
            S_.mark(f"p3_{l}")
            S_.fence()
            S_.dma(lambda e: e.dma_start(out=gb2.t[:], in_=ln2_g[l:l + 1, :].partition_broadcast(128)), writes=[gb2.B])
            pre = (S // 2 + S + S // 2 <= 16384) and not pair and stop_after is None
            for c in range(8):
                S_.dma(lambda e, c=c: e.dma_start(out=wout_v[:, c, :], in_=w_out[l, c * 128:(c + 1) * 128, :]),
                       writes=[BWo], q="pool")
            for c in range(8):
                if pre and c >= 6:
                    continue
                S_.dma(lambda e, c=c: e.dma_start(out=w1_v[:, c, :], in_=w_mlp1[l, c * 128:(c + 1) * 128, :]),
                       writes=[BW1lo if c < 6 else BW1hi], q="pool")
            if not pre:
                for c in range(32):
                    S_.dma(lambda e, c=c: e.dma_start(out=w2_v[:, c, :], in_=w_mlp2[l, c * 128:(c + 1) * 128, :]),
                           writes=[BW2], q="pool")
            a = ARENA
            o = WW

            def carve(n):
                nonlocal o
                v = a[:, o:o + n]
                o += n
                return v
            yT_v = [carve(8 * TT // 2).bitcast(BF16).rearrange("p (c n) -> p c n", c=8)] * 2
            ByT_s = [Buf("yTs0")] * 2
            if pair:
                yTb_v = carve(8 * TT // 2).bitcast(BF16).rearrange("p (c n) -> p c n", c=8)
                ByTb = Buf("yTb")
            x_v = [carve(NJ * D).rearrange("p (j n) -> p j n", j=NJ) for _ in range(2)]
            Bx_s = [Buf("xs0"), Buf("xs1")]
            hb_v = carve(NJ * D // 2).bitcast(BF16).rearrange("p (j n) -> p j n", j=NJ)
            Bhb = Buf("hb3")
            hT_v = carve(8 * TT // 2).bitcast(BF16).rearrange("p (c n) -> p c n", c=8)
            BhT = Buf("hT3")
            aT_v = carve(32 * TT // 2).bitcast(BF16).rearrange("p (f n) -> p f n", f=32)
            BaT = [Buf(f"aT{f}") for f in range(32)]
            ss_v = carve(NJ)
            rs_v = carve(NJ)
            Bss, Brs = Buf("ss3"), Buf("rs3")
            relu_v = [carve(TT) for _ in range(2)]
            Brelu = [Buf("relu0"), Buf("relu1")]
            assert o <= AWORDS, o
            rr = [0]

            def nb():
                k = rr[0] % 7
                rr[0] += 1
                return k
            cp = [0]
            NTT = SP // TT

            def st_load(t):
                tok0 = t * TT
                s = t % 2
                if pair:
                    S_.dma(lambda e: e.dma_start(out=yT_v[s][:, :, :],
                                                 in_=yTall_d[:, :, tok0:tok0 + TT].rearrange("c p n -> p c n")),
                           reads=[ByTall], writes=[ByT_s[s]])
                    S_.dma(lambda e: e.dma_start(out=yTb_v[:, :, :],
                                                 in_=yTall_d[:, :, SP + tok0:SP + tok0 + TT].rearrange("c p n -> p c n")),
                           reads=[ByTall], writes=[ByTb])
                    S_.dve(lambda e: e.tensor_scalar_mul(out=yT_v[s][:, :, :], in0=yT_v[s][:, :, :], scalar1=rmask.t[:, 0:1]),
                           reads=[ByT_s[s], rmask.B], writes=[ByT_s[s]])
                    S_.dve(lambda e: e.scalar_tensor_tensor(out=yT_v[s][:, :, :], in0=yTb_v[:, :, :], scalar=rmask.t[:, 1:2],
                                                            in1=yT_v[s][:, :, :], op0=ALU.mult, op1=ALU.add),
                           reads=[ByT_s[s], ByTb, rmask.B], writes=[ByT_s[s]])
                else:
                    S_.dma(lambda e: e.dma_start(out=yT_v[s][:, :, :],
                                                 in_=yT_d[:, :, tok0:tok0 + TT].rearrange("c p n -> p c n")),
                           reads=[ByT], writes=[ByT_s[s]])
                S_.dma(lambda e: e.dma_start(out=x_v[s][:, :, :],
                                             in_=x_src[tok0:tok0 + TT, :].rearrange("(j p) n -> p j n", p=128)),
                       reads=[Bx_src], writes=[Bx_s[s]])

            def st_outproj(t):
                s = t % 2
                for j in range(NJ):
                    for n in range(2):
                        bk = nb()
                        for c in range(8):
                            S_.pe(lambda e, c=c, j=j, n=n, bk=bk: e.matmul(
                                bank(bk), lhsT=yT_v[s][:, c, j * 128:(j + 1) * 128], rhs=wout_v[:, c, n * 512:(n + 1) * 512],
                                start=(c == 0), stop=(c == 7)), reads=[ByT_s[s], BWo], writes=[bankB[bk]])
                        S_.dve(lambda e, j=j, n=n, bk=bk: e.tensor_tensor(
                            out=x_v[s][:, j, n * 512:(n + 1) * 512], in0=bank(bk), in1=x_v[s][:, j, n * 512:(n + 1) * 512],
                            op=ALU.add), reads=[bankB[bk], Bx_s[s]], writes=[Bx_s[s]])

            def st_norm(t):
                s = t % 2
                for j in range(NJ):
                    S_.act(lambda e, j=j: e.activation(out=hb_v[:, j, :], in_=x_v[s][:, j, :], func=AF.Square,
                                                       accum_out=ss_v[:, j:j + 1]),
                           reads=[Bx_s[s]], writes=[Bhb, Bss])
                S_.act(lambda e: e.activation(out=rs_v[:, :], in_=ss_v[:, :], func=AF.Ln, bias=epsc.t[:, 0:1], scale=1.0 / D),
                       reads=[Bss, epsc.B], writes=[Brs])
                S_.act(lambda e: e.activation(out=rs_v[:, :], in_=rs_v[:, :], func=AF.Exp, scale=-0.5),
                       reads=[Brs], writes=[Brs])
                for j in range(NJ):
                    S_.dve(lambda e, j=j: e.scalar_tensor_tensor(out=hb_v[:, j, :], in0=x_v[s][:, j, :],
                                                                 scalar=rs_v[:, j:j + 1], in1=gb2.t[:, :],
                                                                 op0=ALU.mult, op1=ALU.mult),
                           reads=[Bx_s[s], Brs, gb2.B], writes=[Bhb])

            def st_transpose(t):
                for c4 in range(2):
                    for cc_ in range(4):
                        c = c4 * 4 + cc_
                        for j in range(NJ):
                            S_.pe(lambda e, c=c, j=j, cc_=cc_: e.transpose(
                                out=PSb[:, cc_ * TT + j * 128: cc_ * TT + (j + 1) * 128],
                                in_=hb_v[:, j, c * 128:(c + 1) * 128], identity=ident.t[:, :]),
                                reads=[Bhb, ident.B], writes=[bankB[7]])
                    cp[0] += 1
                    if cp[0] % 2 == 0:
                        S_.act(lambda e, c4=c4: e.copy(out=hT_v[:, c4 * 4:c4 * 4 + 4, :],
                                                       in_=PSb[:, :].rearrange("p (c n) -> p c n", c=4)),
                               reads=[bankB[7]], writes=[BhT])
                    else:
                        S_.dve(lambda e, c4=c4: e.tensor_copy(out=hT_v[:, c4 * 4:c4 * 4 + 4, :],
                                                              in_=PSb[:, :].rearrange("p (c n) -> p c n", c=4)),
                               reads=[bankB[7]], writes=[BhT])

            def st_hidden(t):
                for f in range(32):
                    bk = nb()
                    for c in range(8):
                        S_.pe(lambda e, c=c, f=f, bk=bk: e.matmul(
                            bank(bk)[:, 0:TT], lhsT=w1_v[:, c, f * 128:(f + 1) * 128], rhs=hT_v[:, c, :],
                            start=(c == 0), stop=(c == 7)), reads=[BW1lo, BW1hi, BhT], writes=[bankB[bk]])
                    rt = f % 2
                    S_.act(lambda e, rt=rt, bk=bk: e.activation(out=relu_v[rt][:, :], in_=bank(bk)[:, 0:TT], func=AF.Relu),
                           reads=[bankB[bk]], writes=[Brelu[rt]])
                    if f % 2 == 0:
                        S_.dve(lambda e, f=f, rt=rt: e.tensor_tensor(out=aT_v[:, f, :], in0=relu_v[rt][:, :], in1=relu_v[rt][:, :],
                                                                     op=ALU.mult), reads=[Brelu[rt]], writes=[BaT[f]])
                    else:
                        S_.pool(lambda e, f=f, rt=rt: e.tensor_tensor(out=aT_v[:, f, :], in0=relu_v[rt][:, :], in1=relu_v[rt][:, :],
                                                                      op=ALU.mult), reads=[Brelu[rt]], writes=[BaT[f]])

            def st_down(t, j):
                s = t % 2
                for n in range(2):
                    bk = nb()
                    for f in range(32):
                        S_.pe(lambda e, f=f, n=n, bk=bk: e.matmul(
                            bank(bk), lhsT=aT_v[:, f, j * 128:(j + 1) * 128], rhs=w2_v[:, f, n * 512:(n + 1) * 512],
                            start=(f == 0), stop=(f == 31)), reads=[BaT[f], BW2], writes=[bankB[bk]])
                    S_.dve(lambda e, n=n, bk=bk: e.tensor_tensor(
                        out=x_v[s][:, j, n * 512:(n + 1) * 512], in0=bank(bk), in1=x_v[s][:, j, n * 512:(n + 1) * 512],
                        op=ALU.add), reads=[bankB[bk], Bx_s[s]], writes=[Bx_s[s]])

            def st_store(t):
                tok0 = t * TT
                s = t % 2
                op = S_.dma(lambda e: e.dma_start(
                    out=x_dst[tok0:tok0 + TT, :].rearrange("(j p) n -> p j n", p=128), in_=x_v[s][:, :, :]),
                    reads=[Bx_s[s]], writes=[Bx_dst])
                if is_out:
                    out_ops.append(op)

            st_load(0)
            st_outproj(0)
            st_norm(0)
            st_transpose(0)
            for t in range(NTT):
                if t + 1 < NTT:
                    st_load(t + 1)
                st_hidden(t)
                if t + 1 < NTT:
                    st_outproj(t + 1)
                    st_norm(t + 1)
                st_down(t, 0)
                if t + 1 < NTT:
                    st_transpose(t + 1)
                for j in range(1, NJ):
                    st_down(t, j)
                st_store(t)

        for l in range(DEPTH):
            load_params(l)
            x_src, Bsrc = (x_in, Buf("xin")) if l == 0 else (xs_d, Bxs)
            last = (l == DEPTH - 1)
            phase1(l, x_src, Bsrc)
            if stop_after == "p1":
                break
            phase2(l)
            if stop_after == "p2":
                break
            if pair:
                S_.dma(lambda e: e.collective_compute(
                    "AllGather", ALU.bypass, replica_groups=groups,
                    ins=[yT_d.rearrange("c p n -> (c p) n")], outs=[yTall_d.rearrange("c p n -> (c p) n")]),
                    reads=[ByT], writes=[ByTall], q="pool", ring="cc")
                x3_src, B3src = (xh_in, Buf("xhin")) if l == 0 else (xm_d, Bxm)
                x_dst, Bdst = (out_d, Bout) if last else (xm_d, Bxm)
                phase3(l, x3_src, B3src, x_dst, Bdst, last)
                if not last:
                    S_.dma(lambda e: e.collective_compute(
                        "AllGather", ALU.bypass, replica_groups=groups, ins=[xm_d[:, :]], outs=[xs_d[:, :]]),
                        reads=[Bxm], writes=[Bxs], q="pool", ring="cc")
            else:
                x_dst, Bdst = (out_d, Bout) if last else (xs_d, Bxs)
                phase3(l, x_src, Bsrc, x_dst, Bdst, last)

        if trunc is not None:
            S_.ops = S_.ops[:trunc]
            out_ops = []
            for en in ENGS:
                idxs = [i for i, o_ in enumerate(S_.ops) if o_[0] == en]
                if idxs:
                    out_ops.append(idxs[-1])
        if not out_ops:
            out_ops.append(len(S_.ops) - 1)
        S_.emit(nc, final_wait_ops=out_ops)
    return nc, S_


def _pair_layout(r, x_b, p):
    S = x_b.shape[0]
    SP = S // 2
    cols = np.concatenate([
        np.arange(0 + r * 128, 0 + (r + 1) * 128),
        np.arange(256 + r * 128, 256 + (r + 1) * 128),
        np.arange(512 + r * 128, 512 + (r + 1) * 128),
        np.arange(768 + 2 * r * 128, 768 + (2 * r + 2) * 128),
        np.arange(1280 + 2 * r * 128, 1280 + (2 * r + 2) * 128),
        np.arange(1792 + 2 * r * 128, 1792 + (2 * r + 2) * 128),
        np.arange(2304 + r * 128, 2304 + (r + 1) * 128),
        np.arange(2560 + r * 128, 2560 + (r + 1) * 128),
        np.arange(2816 + r * 128, 2816 + (r + 1) * 128),
        np.arange(3072 + r * 128, 3072 + (r + 1) * 128),
        np.arange(3328, 3344),
    ])
    rows = np.concatenate([
        np.concatenate([np.arange(i * 128, (i + 1) * 128),
                        np.arange(256 + 2 * i * 128, 256 + (2 * i + 2) * 128),
                        np.arange(768 + i * 128, 768 + (i + 1) * 128)]) for i in range(2)])
    slopes = np.array([[2.0 ** (-8.0 * (2 * r + hh + 1) / 4) for hh in range(2)]], dtype=np.float32)
    m = dict(p)
    m["x"] = np.ascontiguousarray(x_b)
    m["x_half"] = np.ascontiguousarray(x_b[r * SP:(r + 1) * SP])
    m["w_in"] = np.ascontiguousarray(p["w_in"][:, :, cols])
    m["conv_w"] = np.ascontiguousarray(p["conv_w"][:, :, r * 128:(r + 1) * 128])
    m["gla_alpha_w"] = np.ascontiguousarray(p["gla_alpha_w"][:, :, r * 128:(r + 1) * 128])
    m["gla_alpha_b"] = np.ascontiguousarray(p["gla_alpha_b"][:, r * 128:(r + 1) * 128])
    m["w_out"] = np.ascontiguousarray(p["w_out"][:, rows, :])
    m["slopes"] = slopes
    m["rmask"] = np.array([[1.0 - r, float(r)]], dtype=np.float32)
    return m


def kernel(x, ln1_g, w_in, conv_w, q_norm_g, k_norm_g, diff_lambda, diff_subln_g,
           gla_alpha_w, gla_alpha_b, gla_norm_g, w_out, ln2_g, w_mlp1, w_mlp2):
    x = np.asarray(x, dtype=np.float32)
    B, S, _ = x.shape
    depth = int(np.asarray(ln1_g).shape[0])
    nc, _ = build(S=S, DEPTH=depth)
    shared = dict(ln1_g=ln1_g, w_in=w_in, conv_w=conv_w, q_norm_g=q_norm_g, k_norm_g=k_norm_g,
                  diff_lambda=diff_lambda, diff_subln_g=diff_subln_g, gla_alpha_w=gla_alpha_w,
                  gla_alpha_b=gla_alpha_b, gla_norm_g=gla_norm_g, w_out=w_out, ln2_g=ln2_g,
                  w_mlp1=w_mlp1, w_mlp2=w_mlp2)
    shared = {k: np.ascontiguousarray(np.asarray(v, dtype=np.float32)) for k, v in shared.items()}
    in_maps = []
    for c in range(B):
        m = dict(shared)
        m["x"] = np.ascontiguousarray(x[c])
        in_maps.append(m)
    res = run_bass_kernel_spmd(nc, in_maps, core_ids=list(range(B)))
    return np.stack([np.asarray(r["out"], dtype=np.float32) for r in res.results], axis=0)
```

```python
import math
from contextlib import ExitStack
import numpy as np
import concourse.bass as bass
import concourse.mybir as mybir
from concourse.bass_utils import run_bass_kernel_spmd

F32 = mybir.dt.float32
BF16 = mybir.dt.bfloat16
I32 = mybir.dt.int32
ALU = mybir.AluOpType
AF = mybir.ActivationFunctionType
AX = mybir.AxisListType

ENGS = ("pe", "act", "dve", "pool", "sp")
SEM_EPOCH = 24000
DMA_RING = 8

D = 1024
DIN = 3344
DFF = 4096
EPS = 1e-6
NEG = -30000.0


class Buf:
    __slots__ = ("name", "w", "r")

    def __init__(self, name=""):
        self.name = name
        self.w = None
        self.r = []


class Sched:
    def __init__(self):
        self.ops = []
        self.sameeng_sync = {"act": True, "dve": True, "pool": True, "pe": False, "sp": False}
        self.marks = []
        self.fence_deps = None
        self.fence_id = 0
        self.fence_passed = {}
        self.last_ops = {e: [] for e in ENGS}

    def mark(self, name):
        self.marks.append((name, len(self.ops)))

    def fence(self):
        deps = set()
        for e in ENGS:
            deps.update(self.last_ops[e][-(DMA_RING + 1):])
        self.fence_deps = deps
        self.fence_id += 1

    def add(self, eng, fn, reads=(), writes=(), dma=False, ring="d"):
        idx = len(self.ops)
        deps = set()
        if self.fence_deps is not None and self.fence_passed.get(eng) != self.fence_id:
            deps.update(self.fence_deps)
            self.fence_passed[eng] = self.fence_id
        self.last_ops[eng].append(idx)
        if len(self.last_ops[eng]) > 4 * DMA_RING:
            del self.last_ops[eng][:-2 * DMA_RING]
        for b in reads:
            if b.w is not None:
                deps.add(b.w)
        for b in writes:
            if b.w is not None:
                deps.add(b.w)
            for r in b.r:
                deps.add(r)
        for b in reads:
            b.r.append(idx)
        for b in writes:
            b.w = idx
            b.r = []
        deps.discard(idx)
        self.ops.append([eng, fn, deps, dma, ring])
        return idx

    def pe(self, fn, reads=(), writes=()):
        return self.add("pe", fn, reads, writes)

    def act(self, fn, reads=(), writes=()):
        return self.add("act", fn, reads, writes)

    def dve(self, fn, reads=(), writes=()):
        return self.add("dve", fn, reads, writes)

    def pool(self, fn, reads=(), writes=()):
        return self.add("pool", fn, reads, writes)

    def dma(self, fn, reads=(), writes=(), q="sp", ring="d"):
        return self.add(q, fn, reads, writes, dma=True, ring=ring)

    def emit(self, nc, final_wait_ops=()):
        ops = self.ops
        n = len(ops)
        signal = [False] * n
        for i, (eng, fn, deps, dma, _rg) in enumerate(ops):
            if dma:
                signal[i] = True
            for d in deps:
                if ops[d][3]:
                    continue
                if ops[d][0] != eng or self.sameeng_sync.get(eng, False):
                    signal[d] = True
        for i in final_wait_ops:
            signal[i] = True
        sem_of = [None] * n
        sem_specs = []
        cur = {e: None for e in ENGS}
        dma_ring = {}
        dma_rr = {}
        dma_prev = [None] * n
        ring_last = {}
        for i, (eng, fn, deps, dma, rg) in enumerate(ops):
            if not signal[i]:
                continue
            if dma:
                rk = (eng, rg)
                ring = dma_ring.setdefault(rk, [])
                nslots = DMA_RING if rg == "d" else 1
                slot = dma_rr.get(rk, 0) % nslots
                dma_rr[rk] = dma_rr.get(rk, 0) + 1
                if len(ring) <= slot:
                    sem_specs.append(f"{rg}_{eng}_{slot}_{len(sem_specs)}")
                    ring.append([len(sem_specs) - 1, 0])
                if ring[slot][1] + 16 > SEM_EPOCH:
                    sem_specs.append(f"{rg}_{eng}_{slot}_{len(sem_specs)}")
                    ring[slot] = [len(sem_specs) - 1, 0]
                ring[slot][1] += 16
                sem_of[i] = (ring[slot][0], ring[slot][1])
                key = (eng, rg, slot)
                dma_prev[i] = ring_last.get(key)
                ring_last[key] = i
            else:
                c = cur[eng]
                if c is None or c[1] + 1 > SEM_EPOCH:
                    sem_specs.append(f"s_{eng}_{len(sem_specs)}")
                    c = [len(sem_specs) - 1, 0]
                    cur[eng] = c
                c[1] += 1
                sem_of[i] = (c[0], c[1])
        self.n_sems = len(sem_specs)
        per_eng = {e: [] for e in ENGS}
        for i, op in enumerate(ops):
            per_eng[op[0]].append(i)
        self.stats = {e: len(per_eng[e]) for e in ENGS}
        self.stats["signals"] = sum(signal)

        with ExitStack() as st:
            sems = [st.enter_context(nc.semaphore(nm)) for nm in sem_specs]
            block = st.enter_context(nc.Block())

            def make(engname):
                idxs = per_eng[engname]

                def body(e):
                    seen = {}
                    nwait = 0
                    for i in idxs:
                        eng, fn, deps, dma, _rg = ops[i]
                        waits = {}
                        dl = list(deps)
                        if dma and dma_prev[i] is not None:
                            dl.append(dma_prev[i])
                        for d in dl:
                            if (not ops[d][3]) and ops[d][0] == eng and not self.sameeng_sync.get(eng, False):
                                continue
                            s, v = sem_of[d]
                            if seen.get(s, 0) >= v:
                                continue
                            if waits.get(s, 0) < v:
                                waits[s] = v
                        for s, v in waits.items():
                            e.wait_ge(sems[s], v)
                            seen[s] = v
                            nwait += 1
                        ins = fn(e)
                        if signal[i]:
                            s, v = sem_of[i]
                            ins.then_inc(sems[s], 16 if dma else 1)
                    if engname == "sp":
                        for i in final_wait_ops:
                            s, v = sem_of[i]
                            e.wait_ge(sems[s], v)
                    self.stats["waits_" + engname] = nwait
                return body

            block.tensor(make("pe"))
            block.scalar(make("act"))
            block.vector(make("dve"))
            block.gpsimd(make("pool"))
            block.sync(make("sp"))


class Tl:
    def __init__(self, t, nb=1, name=""):
        self.t = t
        self.b = [Buf(f"{name}{i}") for i in range(nb)]

    @property
    def B(self):
        return self.b[0]


def build(S=8192, DEPTH=4, stop_after=None, debug=False, trunc=None, pair=False, ncores=4):
    NT = S // 512
    NB = S // 128
    nc = bass.Bass("TRN2", target_bir_lowering=False)
    if pair:
        NCV, NH, NP = 1, 2, 1
        C_U, C_CB, C_CC, C_Q, C_K, C_V, C_GQ, C_GK, C_GG, C_GA = 0, 128, 256, 384, 640, 896, 1152, 1280, 1536, 1664
        DINL = 1680
        SP = S // 2
    else:
        NCV, NH, NP = 2, 4, 2
        C_U, C_CB, C_CC, C_Q, C_K, C_V, C_GQ, C_GK, C_GG, C_GA = 0, 256, 512, 768, 1280, 1792, 2304, 2560, 3072, 3328
        DINL = DIN
        SP = S
    GW = NP * 128
    VW = NH * 128
    NYC = NCV + NH + NP
    BPB = 512 // GW

    def din(name, shape):
        return nc.dram_tensor(name, shape, F32, kind="ExternalInput").ap()

    x_in = din("x", [S, D])
    if pair:
        xh_in = din("x_half", [SP, D])
        slopes_in = din("slopes", [1, NH])
        rmask_in = din("rmask", [1, 2])
    ln1_g = din("ln1_g", [DEPTH, D])
    w_in = din("w_in", [DEPTH, D, DINL])
    conv_w = din("conv_w", [DEPTH, 3, NCV * 128])
    q_norm_g = din("q_norm_g", [DEPTH, 64])
    k_norm_g = din("k_norm_g", [DEPTH, 64])
    diff_lambda = din("diff_lambda", [DEPTH, 4, 64])
    diff_subln_g = din("diff_subln_g", [DEPTH, 128])
    gla_alpha_w = din("gla_alpha_w", [DEPTH, 16, GW])
    gla_alpha_b = din("gla_alpha_b", [DEPTH, GW])
    gla_norm_g = din("gla_norm_g", [DEPTH, 64])
    w_out = din("w_out", [DEPTH, D, D])
    ln2_g = din("ln2_g", [DEPTH, D])
    w_mlp1 = din("w_mlp1", [DEPTH, D, DFF])
    w_mlp2 = din("w_mlp2", [DEPTH, DFF, D])
    out_d = nc.dram_tensor("out", [SP, D], F32, kind="ExternalOutput").ap()

    xs_d = nc.dram_tensor("xs_scr", [S, D], F32, kind="Internal").ap()
    sk = "ExternalOutput" if debug else "Internal"
    qT_d = nc.dram_tensor("qT_scr", [NH, 128, S], BF16, kind=sk).ap()
    kT_d = nc.dram_tensor("kT_scr", [NH, 128, S], BF16, kind=sk).ap()
    v_d = nc.dram_tensor("v_scr", [S, VW], BF16, kind=sk).ap()
    yT_d = nc.dram_tensor("yT_scr", [NYC, 128, S], BF16, kind=sk).ap()
    if pair:
        xm_d = nc.dram_tensor("xm_scr", [SP, D], F32, kind="Internal").ap()
        yTall_d = nc.dram_tensor("yTall_scr", [2 * NYC, 128, S], BF16, kind="Internal").ap()
        Bxm, ByTall = Buf("xm"), Buf("yTall")
        groups = [[2 * i, 2 * i + 1] for i in range(ncores // 2)]
    Bxs, BqT, BkT, Bv, ByT, Bout = Buf("xs"), Buf("qT"), Buf("kT"), Buf("v"), Buf("yT"), Buf("out")

    S_ = Sched()
    out_ops = []

    with ExitStack() as st:
        def sb(name, shape, dt, nb=1):
            return Tl(st.enter_context(nc.sbuf_tensor(name, shape, dt)), nb, name)

        PSf = st.enter_context(nc.psum_tensor("psf", [128, 8 * 512], F32))
        bankB = [Buf(f"bank{i}") for i in range(8)]

        def bank(i):
            return PSf[:, i * 512:(i + 1) * 512]

        PSb = bank(7).bitcast(BF16)

        ident = sb("ident", [128, 128], BF16)
        ones_bf = sb("ones_bf", [128, 128], BF16)
        blk64 = sb("blk64", [128, 128], BF16)
        o128 = sb("o128", [128, 128], BF16)
        triu = sb("triu", [128, 128], F32)
        strl = sb("strl", [128, 128], F32)
        triu_b = sb("triu_b", [128, 128], BF16)
        strl_b = sb("strl_b", [128, 128], BF16)
        iot = sb("iot", [128, 128], I32)
        dkq = sb("dkq", [128, 128], F32)
        tdiag = sb("tdiag", [128, 4, 128], BF16)
        kcol = sb("kcol", [128, 1], F32)
        kcoli = sb("kcoli", [128, 1], I32)
        NDEL = 4 * (NT - 1) + 4
        kbias = sb("kbias", [128, 4, NDEL], F32)
        kdel = sb("kdel", [128, NDEL], F32)
        slopec = sb("slopec", [128, 4], F32)
        slopes = [2.0 ** (-8.0 * (h + 1) / 4) for h in range(4)]
        if pair:
            rmask = sb("rmask_sb", [128, 2], F32)
            S_.dma(lambda e: e.dma_start(out=slopec.t[:, 0:NH], in_=slopes_in[0:1, :].partition_broadcast(128)), writes=[slopec.B])
            S_.dma(lambda e: e.dma_start(out=rmask.t[:, :], in_=rmask_in[0:1, :].partition_broadcast(128)), writes=[rmask.B])
        else:
            for h in range(4):
                S_.pool(lambda e, h=h: e.memset(slopec.t[:, h:h + 1], slopes[h]), writes=[slopec.B])

        hmask = sb("hmask", [128, 2], F32)
        cmask = sb("cmask", [128, 2], F32)
        S_.pool(lambda e: e.memset(hmask.t[:], 0.0), writes=[hmask.B])
        S_.pool(lambda e: e.memset(hmask.t[0:64, 0:1], 0.125), writes=[hmask.B])
        S_.pool(lambda e: e.memset(hmask.t[64:128, 1:2], 0.125), writes=[hmask.B])
        S_.pool(lambda e: e.memset(cmask.t[:], 0.0), writes=[cmask.B])
        S_.pool(lambda e: e.memset(cmask.t[0:64, 0:1], 1.0), writes=[cmask.B])
        S_.pool(lambda e: e.memset(cmask.t[64:128, 1:2], 1.0), writes=[cmask.B])
        epsc = sb("epsc", [128, 1], F32)
        onec = sb("onec", [128, 1], F32)
        S_.pool(lambda e: e.memset(epsc.t[:], EPS), writes=[epsc.B])
        S_.pool(lambda e: e.memset(onec.t[:], 1.0), writes=[onec.B])
        S_.pool(lambda e: e.memset(ident.t[:], 0.0), writes=[ident.B])
        S_.pool(lambda e: e.affine_select(out=ident.t[:], in_=ident.t[:], pattern=[[-1, 128]],
                                          compare_op=ALU.not_equal, fill=1.0, base=0, channel_multiplier=1),
                reads=[ident.B], writes=[ident.B])
        S_.pool(lambda e: e.memset(ones_bf.t[:], 1.0), writes=[ones_bf.B])
        S_.pool(lambda e: e.memset(o128.t[:], 1.0 / 128), writes=[o128.B])
        S_.pool(lambda e: e.memset(blk64.t[:], 1.0 / 64), writes=[blk64.B])
        S_.pool(lambda e: e.memset(blk64.t[0:64, 64:128], 0.0), writes=[blk64.B])
        S_.pool(lambda e: e.memset(blk64.t[64:128, 0:64], 0.0), writes=[blk64.B])
        S_.pool(lambda e: e.memset(triu.t[:], 1.0), writes=[triu.B])
        S_.pool(lambda e: e.affine_select(out=triu.t[:], in_=triu.t[:], pattern=[[1, 128]],
                                          compare_op=ALU.is_ge, fill=0.0, base=0, channel_multiplier=-1),
                reads=[triu.B], writes=[triu.B])
        S_.pool(lambda e: e.memset(triu.t[0:64, 64:128], 0.0), writes=[triu.B])
        S_.pool(lambda e: e.memset(strl.t[:], 1.0), writes=[strl.B])
        S_.pool(lambda e: e.affine_select(out=strl.t[:], in_=strl.t[:], pattern=[[-1, 128]],
                                          compare_op=ALU.is_gt, fill=0.0, base=0, channel_multiplier=1),
                reads=[strl.B], writes=[strl.B])
        S_.pool(lambda e: e.memset(strl.t[64:128, 0:64], 0.0), writes=[strl.B])
        S_.dve(lambda e: e.tensor_copy(out=triu_b.t[:], in_=triu.t[:]), reads=[triu.B], writes=[triu_b.B])
        S_.dve(lambda e: e.tensor_copy(out=strl_b.t[:], in_=strl.t[:]), reads=[strl.B], writes=[strl_b.B])
        S_.pool(lambda e: e.iota(iot.t[:], pattern=[[-1, 128]], base=0, channel_multiplier=1), writes=[iot.B])
        S_.dve(lambda e: e.tensor_copy(out=dkq.t[:], in_=iot.t[:]), reads=[iot.B], writes=[dkq.B])
        S_.dve(lambda e: e.tensor_scalar_max(out=dkq.t[:], in0=dkq.t[:], scalar1=0.0), reads=[dkq.B], writes=[dkq.B])
        S_.dve(lambda e: e.tensor_scalar_mul(out=dkq.t[:], in0=dkq.t[:], scalar1=-2.0), reads=[dkq.B], writes=[dkq.B])
        for h in range(NH):
            def f(e, h=h):
                return e.tensor_scalar_mul(out=tdiag.t[:, h, :], in0=dkq.t[:], scalar1=slopec.t[:, h:h + 1])
            S_.dve(f, reads=[dkq.B, slopec.B], writes=[tdiag.B])
        S_.dve(lambda e: e.memset(tdiag.t[64:128, :, 0:64], NEG), reads=[tdiag.B], writes=[tdiag.B])
        S_.pool(lambda e: e.iota(kcoli.t[:], pattern=[[0, 1]], base=0, channel_multiplier=1), writes=[kcoli.B])
        S_.dve(lambda e: e.tensor_copy(out=kcol.t[:], in_=kcoli.t[:]), reads=[kcoli.B], writes=[kcol.B])
        for di in range(NDEL):
            delta = di - 4 * (NT - 1)

            def f(e, di=di, delta=delta):
                return e.tensor_scalar_add(out=kdel.t[:, di:di + 1], in0=kcol.t[:], scalar1=float(128 * delta - 256))
            S_.dve(f, reads=[kcol.B], writes=[kdel.B])
        for h in range(NH):
            S_.dve(lambda e, h=h: e.tensor_scalar_mul(out=kbias.t[:, h, :], in0=kdel.t[:, :], scalar1=slopec.t[:, h:h + 1]),
                   reads=[kdel.B, slopec.B], writes=[kbias.B])

        gb1 = sb("gb", [128, D], F32)
        gb2 = gb1
        cw = sb("cw", [128, 2, 3], F32)
        qg = sb("qg", [128, 1], F32)
        kg = sb("kg", [128, 1], F32)
        lamt = sb("lamt", [128, 4, 64], F32)
        lamw = sb("lamw", [128, 2, 64], F32)
        lams = sb("lams", [128, 2], F32)
        nlam = sb("nlam", [128, 1], F32)
        subg = sb("subg", [128, 1], F32)
        aw = sb("aw", [17, GW], F32)
        aw_hi = sb("aw_hi", [17, GW], BF16)
        aw_lo = sb("aw_lo", [17, GW], BF16)
        gng = sb("gng", [128, 1], F32)

        def load_params(l):
            S_.mark(f"params{l}")
            S_.dma(lambda e: e.dma_start(out=gb1.t[:], in_=ln1_g[l:l + 1, :].partition_broadcast(128)), writes=[gb1.B])
            for cc_ in range(NCV):
                for k_ in range(3):
                    S_.dma(lambda e, cc_=cc_, k_=k_: e.dma_start(
                        out=cw.t[:, cc_, k_:k_ + 1],
                        in_=conv_w[l, k_, cc_ * 128:(cc_ + 1) * 128].rearrange("(p o) -> p o", o=1)), writes=[cw.B])
            for hh in range(2):
                S_.dma(lambda e, hh=hh: e.dma_start(out=qg.t[hh * 64:(hh + 1) * 64, :],
                                                    in_=q_norm_g[l].rearrange("(p o) -> p o", o=1)), writes=[qg.B])
                S_.dma(lambda e, hh=hh: e.dma_start(out=kg.t[hh * 64:(hh + 1) * 64, :],
                                                    in_=k_norm_g[l].rearrange("(p o) -> p o", o=1)), writes=[kg.B])
                S_.dma(lambda e, hh=hh: e.dma_start(out=gng.t[hh * 64:(hh + 1) * 64, :],
                                                    in_=gla_norm_g[l].rearrange("(p o) -> p o", o=1)), writes=[gng.B])
            S_.dma(lambda e: e.dma_start(out=lamt.t[:].rearrange("p a b -> p (a b)"),
                                         in_=diff_lambda[l:l + 1].rearrange("o a b -> o (a b)").partition_broadcast(128)),
                   writes=[lamt.B])
            S_.dma(lambda e: e.dma_start(out=subg.t[:], in_=diff_subln_g[l].rearrange("(p o) -> p o", o=1)),
                   writes=[subg.B])
            S_.dma(lambda e: e.dma_start(out=aw.t[0:16, :], in_=gla_alpha_w[l]), writes=[aw.B])
            S_.dma(lambda e: e.dma_start(out=aw.t[16:17, :], in_=gla_alpha_b[l:l + 1, :]), writes=[aw.B])
            S_.dve(lambda e: e.tensor_copy(out=aw_hi.t[:], in_=aw.t[:]), reads=[aw.B], writes=[aw_hi.B])
            S_.dve(lambda e: e.tensor_tensor(out=aw_lo.t[:], in0=aw.t[:], in1=aw_hi.t[:], op=ALU.subtract),
                   reads=[aw.B, aw_hi.B], writes=[aw_lo.B])
            lam_init = 0.8 - 0.6 * math.exp(-0.3 * l)
            S_.dve(lambda e: e.tensor_scalar_mul(out=qg.t[:], in0=qg.t[:], scalar1=0.125), reads=[qg.B], writes=[qg.B])
            S_.dve(lambda e: e.tensor_tensor(out=lamw.t[:, 0, :], in0=lamt.t[:, 0, :], in1=lamt.t[:, 1, :], op=ALU.mult),
                   reads=[lamt.B], writes=[lamw.B])
            S_.dve(lambda e: e.tensor_tensor(out=lamw.t[:, 1, :], in0=lamt.t[:, 2, :], in1=lamt.t[:, 3, :], op=ALU.mult),
                   reads=[lamt.B], writes=[lamw.B])
            S_.dve(lambda e: e.tensor_reduce(out=lams.t[:], in_=lamw.t[:], axis=AX.X, op=ALU.add),
                   reads=[lamw.B], writes=[lams.B])
            S_.act(lambda e: e.activation(out=lams.t[:], in_=lams.t[:], func=AF.Exp), reads=[lams.B], writes=[lams.B])
            S_.dve(lambda e: e.tensor_tensor(out=nlam.t[:], in0=lams.t[:, 1:2], in1=lams.t[:, 0:1], op=ALU.subtract),
                   reads=[lams.B], writes=[nlam.B])
            S_.dve(lambda e: e.tensor_scalar_add(out=nlam.t[:], in0=nlam.t[:], scalar1=-lam_init),
                   reads=[nlam.B], writes=[nlam.B])
            S_.dve(lambda e: e.tensor_scalar_mul(out=subg.t[:], in0=subg.t[:], scalar1=1.0 - lam_init),
                   reads=[subg.B], writes=[subg.B])

        AWORDS = 48700
        ARENA = st.enter_context(nc.sbuf_tensor("arena", [128, AWORDS], F32))
        WW = (8 * D + 8 * DFF + 32 * D) // 2
        WREG = ARENA[:, 0:WW].bitcast(BF16)
        BW = Buf("wreg")
        BWo, BW1lo, BW1hi, BW2 = Buf("wout"), Buf("w1lo"), Buf("w1hi"), Buf("w2")
        win_v = WREG[:, 0:8 * DINL].rearrange("p (c n) -> p c n", c=8)
        wout_v = WREG[:, 0:8 * D].rearrange("p (c n) -> p c n", c=8)
        w1_v = WREG[:, 8 * D:8 * D + 8 * DFF].rearrange("p (c n) -> p c n", c=8)
        w2_v = WREG[:, 8 * D + 8 * DFF:].rearrange("p (c n) -> p c n", c=32)

        def phase1(l, x_src, Bx_src):
            S_.fence()
            for c in range(8):
                S_.dma(lambda e, c=c: e.dma_start(out=win_v[:, c, :], in_=w_in[l, c * 128:(c + 1) * 128, :]),
                       writes=[BW], q="pool")
            a = ARENA
            o = 8 * DINL // 2

            def carve(n, dt=F32, shape=None):
                nonlocal o
                v = a[:, o:o + n]
                o += n
                return v

            xt_v = [carve(4096).rearrange("p (j n) -> p j n", j=4) for _ in range(1)]
            Bxt = [Buf("xt0")]
            hb_raw = carve(2048)
            hb_v = hb_raw.bitcast(BF16).rearrange("p (j n) -> p j n", j=4)
            Bhb = Buf("hb")
            hT_raw = carve(2048)
            hT_v = hT_raw.bitcast(BF16).rearrange("p (c n) -> p c n", c=8)
            BhT = Buf("hT")
            junk_v = carve(512).bitcast(BF16)
            Bjunk = Buf("junk")
            ss_v = carve(4)
            rs_v = carve(4)
            Bss, Brs = Buf("ss"), Buf("rs")
            u_v = carve(512)
            Bu = Buf("u")
            bg_v = carve(512)
            Bbg = Buf("bg")
            zc_v = [carve(514) for _ in range(2)]
            Bzc = [Buf("zc0"), Buf("zc1")]
            acc_v = carve(512)
            Bacc = Buf("acc")
            yc_v = carve(256).bitcast(BF16)
            Byc = Buf("yc")
            sq_v = carve(256).bitcast(BF16)
            Bsq = Buf("sq")
            sq2_v = [carve(256).bitcast(BF16) for _ in range(2)]
            Bsq2 = [Buf("sq2a"), Buf("sq2b")]
            zq_v = [carve(512) for _ in range(2)]
            Bzq = [Buf("zqa"), Buf("zqb")]
            rstd_v = carve(512)
            Brstd = Buf("rstd")
            qk_v = carve(256).bitcast(BF16)
            Bqk = Buf("qk")
            vt_v = carve(256).bitcast(BF16)[:, 0:VW]
            Bvt = Buf("vt")
            aT_v = carve(512)
            BaT = Buf("aT")
            aTh_v = carve(256).bitcast(BF16)
            aTl_v = carve(256).bitcast(BF16)
            BaTh, BaTl = Buf("aTh"), Buf("aTl")
            lah_v = carve(2 * GW).bitcast(BF16).rearrange("p (j n) -> p j n", j=4)
            lal_v = carve(2 * GW).bitcast(BF16).rearrange("p (j n) -> p j n", j=4)
            Blah, Blal = Buf("lah"), Buf("lal")
            la_v = carve(4 * GW).rearrange("p (j n) -> p j n", j=4)
            Bla = Buf("la")
            ebT_v = [carve(512) for _ in range(2)]
            enbT_v = [carve(512) for _ in range(2)]
            BebT = [Buf("ebT0"), Buf("ebT1")]
            BenbT = [Buf("enbT0"), Buf("enbT1")]
            erev_v = carve(4 * GW).rearrange("p (j n) -> p j n", j=4)
            Berev = Buf("erev")
            qinm_v = carve(512 * NP).bitcast(BF16).rearrange("p (c h n) -> p c h n", c=NP, h=2)
            kin_v = carve(256 * NP).bitcast(BF16).rearrange("p (c n) -> p c n", c=NP)
            Bqin, Bkin = [Buf("qin0"), Buf("qin1")], [Buf("kin0"), Buf("kin1")]
            erevm_v = [carve(4 * GW).rearrange("p (j n) -> p j n", j=4) for _ in range(2)]
            Berevm = [Buf("erevm0"), Buf("erevm1")]
            kendm_v = carve(4 * GW).bitcast(BF16).rearrange("p (j c n) -> p j c n", j=4, c=2)
            vtok_v = carve(2 * GW).bitcast(BF16).rearrange("p (j n) -> p j n", j=4)
            vpad_v = carve(512 * NP).bitcast(BF16).rearrange("p (j a b n) -> p j a b n", j=4, a=NP, b=2)
            Bkend, Bvtok = [Buf(f"kend{j}") for j in range(4)], [Buf(f"vtok{j}") for j in range(4)]
            Bvpad = [Buf(f"vpad{j}") for j in range(4)]
            am_v = carve(128 * NP).bitcast(BF16).rearrange("p (h n) -> p h n", h=2 * NP)
            Bam = [Buf("am0"), Buf("am1")]
            Sf_v = carve(128 * NP).rearrange("p (c n) -> p c n", c=NP)
            BSf = Buf("Sf")
            NSL = 4
            Sbf_v = carve(64 * NP * NSL).bitcast(BF16).rearrange("p (s c n) -> p s c n", s=NSL, c=NP)
            BSbf = [Buf(f"Sbf{i}") for i in range(NSL)]
            sg_v = carve(512)
            Bsg = Buf("sg")
            t1_v = carve(512)
            Bt1 = Buf("t1")
            yg_v = carve(256).bitcast(BF16)
            Byg = Buf("yg")
            assert o <= AWORDS, o

            S_.dve(lambda e: e.memset(aT_v[:, :], 1.0), writes=[BaT])
            S_.dve(lambda e: e.memset(aTh_v[:, :], 1.0), writes=[BaTh])
            S_.dve(lambda e: e.memset(aTl_v[:, :], 0.0), writes=[BaTl])
            S_.dve(lambda e: e.memset(Sf_v[:, :, :], 0.0), writes=[BSf])
            S_.pool(lambda e: e.memset(vpad_v[:, :, :, :, :], 0.0), writes=Bvpad)
            for i in range(2):
                S_.dve(lambda e, i=i: e.memset(zc_v[i][:, 0:2], 0.0), writes=[Bzc[i]])

            rr = [0]

            def nb():
                k = rr[0] % 4
                rr[0] += 1
                return k

            cp = [0]

            def copy_rr(out, in_, reads, writes):
                cp[0] += 1
                if cp[0] % 2 == 0:
                    S_.act(lambda e: e.copy(out=out, in_=in_), reads=reads, writes=writes)
                else:
                    S_.dve(lambda e: e.tensor_copy(out=out, in_=in_), reads=reads, writes=writes)

            def fm_group(col0, ncols, bk, t):
                for c in range(8):
                    S_.pe(lambda e, c=c: e.matmul(bank(bk)[0:ncols, :], lhsT=win_v[:, c, col0:col0 + ncols],
                                                  rhs=hT_v[:, c, :], start=(c == 0), stop=(c == 7)),
                          reads=[BW, BhT], writes=[bankB[bk]])

            chunk_ctr = [0]

            def sec_norm(tn):
                tok0 = tn * 512
                xt = xt_v[0]
                bxt = Bxt[0]
                S_.dma(lambda e, xt=xt, tok0=tok0: e.dma_start(
                    out=xt, in_=x_src[tok0:tok0 + 512, :].rearrange("(j p) n -> p j n", p=128)),
                    reads=[Bx_src], writes=[bxt])
                for j in range(4):
                    S_.act(lambda e, j=j, xt=xt: e.activation(out=junk_v[:, :], in_=xt[:, j, :], func=AF.Square,
                                                              accum_out=ss_v[:, j:j + 1]),
                           reads=[bxt], writes=[Bjunk, Bss])
                S_.act(lambda e: e.activation(out=rs_v[:, :], in_=ss_v[:, :], func=AF.Ln, bias=epsc.t[:, 0:1], scale=1.0 / D),
                       reads=[Bss, epsc.B], writes=[Brs])
                S_.act(lambda e: e.activation(out=rs_v[:, :], in_=rs_v[:, :], func=AF.Exp, scale=-0.5),
                       reads=[Brs], writes=[Brs])
                for j in range(4):
                    S_.dve(lambda e, j=j, xt=xt: e.scalar_tensor_tensor(out=hb_v[:, j, :], in0=xt[:, j, :],
                                                                        scalar=rs_v[:, j:j + 1], in1=gb1.t[:, :],
                                                                        op0=ALU.mult, op1=ALU.mult),
                           reads=[bxt, Brs, gb1.B], writes=[Bhb])


            sec_norm(0)
            for t in range(NT):
                S_.mark(f"p1_tile{t}")
                tok0 = t * 512
                S_.mark(f"p1_t{t}_transposes")
                for c2 in range(4):
                    for cc_ in range(2):
                        c = c2 * 2 + cc_
                        for j in range(4):
                            S_.pe(lambda e, c=c, j=j, cc_=cc_: e.transpose(
                                out=PSb[:, cc_ * 512 + j * 128: cc_ * 512 + (j + 1) * 128],
                                in_=hb_v[:, j, c * 128:(c + 1) * 128], identity=ident.t[:, :]),
                                reads=[Bhb, ident.B], writes=[bankB[7]])
                    copy_rr(hT_v[:, c2 * 2:c2 * 2 + 2, :], PSb[:, :].rearrange("p (c n) -> p c n", c=2), [bankB[7]], [BhT])

                def sec_conv():
                    S_.mark(f"p1_t{t}_conv")
                    for cc in range(NCV):
                        bu = nb()
                        fm_group(C_U + cc * 128, 128, bu, t)
                        S_.act(lambda e, bu=bu: e.copy(out=u_v[:, :], in_=bank(bu)), reads=[bankB[bu]], writes=[Bu])
                        bc = nb()
                        fm_group(C_CC + cc * 128, 128, bc, t)
                        S_.dve(lambda e, bc=bc, cc=cc: e.tensor_tensor(out=zc_v[cc][:, 2:514], in0=bank(bc), in1=u_v[:, :],
                                                                       op=ALU.mult),
                               reads=[bankB[bc], Bu], writes=[Bzc[cc]])
                        S_.dve(lambda e, cc=cc: e.tensor_scalar_mul(out=acc_v[:, :], in0=zc_v[cc][:, 2:514],
                                                                    scalar1=cw.t[:, cc, 2:3]),
                               reads=[Bzc[cc], cw.B], writes=[Bacc])
                        S_.dve(lambda e, cc=cc: e.scalar_tensor_tensor(out=acc_v[:, :], in0=zc_v[cc][:, 1:513],
                                                                       scalar=cw.t[:, cc, 1:2], in1=acc_v[:, :],
                                                                       op0=ALU.mult, op1=ALU.add),
                               reads=[Bzc[cc], cw.B, Bacc], writes=[Bacc])
                        S_.dve(lambda e, cc=cc: e.scalar_tensor_tensor(out=acc_v[:, :], in0=zc_v[cc][:, 0:512],
                                                                       scalar=cw.t[:, cc, 0:1], in1=acc_v[:, :],
                                                                       op0=ALU.mult, op1=ALU.add),
                               reads=[Bzc[cc], cw.B, Bacc], writes=[Bacc])
                        bb = nb()
                        fm_group(C_CB + cc * 128, 128, bb, t)
                        S_.act(lambda e, bb=bb: e.copy(out=bg_v[:, :], in_=bank(bb)), reads=[bankB[bb]], writes=[Bbg])
                        S_.dve(lambda e: e.tensor_tensor(out=yc_v[:, :], in0=bg_v[:, :], in1=acc_v[:, :], op=ALU.mult),
                               reads=[Bbg, Bacc], writes=[Byc])
                        S_.dma(lambda e, cc=cc, tok0=tok0: e.dma_start(out=yT_d[cc, :, tok0:tok0 + 512], in_=yc_v[:, :]),
                               reads=[Byc], writes=[ByT])
                        S_.dve(lambda e, cc=cc: e.tensor_copy(out=zc_v[cc][:, 0:2], in_=zc_v[cc][:, 512:514]),
                               reads=[Bzc[cc]], writes=[Bzc[cc]])


                def sec_qk():
                    S_.mark(f"p1_t{t}_qk")
                    groups_ = [(which, h) for which in range(2) for h in range(NH)]
                    pend = None
                    for gi, g in enumerate(groups_ + [None]):
                        if g is not None:
                            which, h = g
                            col0 = (C_Q if which == 0 else C_K) + h * 128
                            bz = nb()
                            fm_group(col0, 128, bz, t)
                            sqi = gi % 2
                            S_.dve(lambda e, bz=bz, sqi=sqi: e.tensor_copy(out=zq_v[sqi][:, :], in_=bank(bz)),
                                   reads=[bankB[bz]], writes=[Bzq[sqi]])
                            S_.act(lambda e, sqi=sqi: e.activation(out=sq2_v[sqi][:, :], in_=zq_v[sqi][:, :], func=AF.Square),
                                   reads=[Bzq[sqi]], writes=[Bsq2[sqi]])
                        if pend is not None:
                            which_p, h_p, bz_p, sqi_p = pend
                            bm = nb()
                            S_.pe(lambda e, bm=bm, sqi_p=sqi_p: e.matmul(bank(bm), lhsT=blk64.t[:, :], rhs=sq2_v[sqi_p][:, :],
                                                                         start=True, stop=True),
                                  reads=[Bsq2[sqi_p], blk64.B], writes=[bankB[bm]])
                            S_.act(lambda e, bm=bm: e.activation(out=rstd_v[:, :], in_=bank(bm), func=AF.Ln, bias=epsc.t[:, 0:1]),
                                   reads=[bankB[bm], epsc.B], writes=[Brstd])
                            S_.act(lambda e: e.activation(out=rstd_v[:, :], in_=rstd_v[:, :], func=AF.Exp, scale=-0.5),
                                   reads=[Brstd], writes=[Brstd])
                            gcol = qg if which_p == 0 else kg
                            S_.dve(lambda e, sqi_p=sqi_p, gcol=gcol: e.scalar_tensor_tensor(
                                out=qk_v[:, :], in0=zq_v[sqi_p][:, :], scalar=gcol.t[:, 0:1], in1=rstd_v[:, :],
                                op0=ALU.mult, op1=ALU.mult),
                                reads=[Bzq[sqi_p], gcol.B, Brstd], writes=[Bqk])
                            dst = qT_d if which_p == 0 else kT_d
                            Bdst = BqT if which_p == 0 else BkT
                            S_.dma(lambda e, dst=dst, h_p=h_p, tok0=tok0: e.dma_start(out=dst[h_p, :, tok0:tok0 + 512], in_=qk_v[:, :]),
                                   reads=[Bqk], writes=[Bdst])
                        pend = (g[0], g[1], bz, gi % 2) if g is not None else None

                def sec_v():
                    S_.mark(f"p1_t{t}_v")
                    for j in range(4):
                        bv = nb()
                        for c in range(8):
                            S_.pe(lambda e, c=c, j=j, bv=bv: e.matmul(bank(bv)[:, 0:VW], lhsT=hT_v[:, c, j * 128:(j + 1) * 128],
                                                                      rhs=win_v[:, c, C_V:C_V + VW], start=(c == 0), stop=(c == 7)),
                                  reads=[BW, BhT], writes=[bankB[bv]])
                        copy_rr(vt_v[:, :], bank(bv)[:, 0:VW], [bankB[bv]], [Bvt])
                        S_.dma(lambda e, j=j, tok0=tok0: e.dma_start(out=v_d[tok0 + j * 128: tok0 + (j + 1) * 128, :], in_=vt_v[:, :]),
                               reads=[Bvt], writes=[Bv])


                def sec_gla():
                    S_.mark(f"p1_t{t}_gla")
                    ba = nb()
                    fm_group(C_GA, 16, ba, t)
                    S_.act(lambda e, ba=ba: e.copy(out=aT_v[0:16, :], in_=bank(ba)[0:16, :]), reads=[bankB[ba]], writes=[BaT])
                    S_.dve(lambda e: e.tensor_copy(out=aTh_v[0:16, :], in_=aT_v[0:16, :]), reads=[BaT], writes=[BaTh])
                    S_.dve(lambda e: e.tensor_tensor(out=aTl_v[0:16, :], in0=aT_v[0:16, :], in1=aTh_v[0:16, :], op=ALU.subtract),
                           reads=[BaT, BaTh], writes=[BaTl])
                    for j in range(4):
                        bk = 4 + j // BPB
                        passes = [(aTh_v, BaTh, aw_hi), (aTl_v, BaTl, aw_hi), (aTh_v, BaTh, aw_lo)]
                        for pi, (av, aB, wv) in enumerate(passes):
                            S_.pe(lambda e, j=j, bk=bk, av=av, wv=wv, pi=pi: e.matmul(
                                bank(bk)[:, (j % BPB) * GW:(j % BPB + 1) * GW],
                                lhsT=av[0:17, j * 128:(j + 1) * 128], rhs=wv.t[0:17, :],
                                start=(pi == 0), stop=(pi == 2)),
                                reads=[aB, wv.B], writes=[bankB[bk]])
                    for half in range(4 // BPB):
                        S_.act(lambda e, half=half: e.activation(out=la_v[:, BPB * half:BPB * half + BPB, :].rearrange("p j n -> p (j n)"),
                                                                 in_=bank(4 + half), func=AF.Exp, scale=-1.0),
                               reads=[bankB[4 + half]], writes=[Bla])
                    S_.act(lambda e: e.activation(out=la_v[:, :, :].rearrange("p j n -> p (j n)"),
                                                  in_=la_v[:, :, :].rearrange("p j n -> p (j n)"), func=AF.Ln, bias=onec.t[:, 0:1]),
                           reads=[Bla, onec.B], writes=[Bla])
                    S_.dve(lambda e: e.tensor_copy(out=lah_v[:, :, :], in_=la_v[:, :, :]), reads=[Bla], writes=[Blah])
                    S_.dve(lambda e: e.tensor_tensor(out=lal_v[:, :, :], in0=la_v[:, :, :], in1=lah_v[:, :, :], op=ALU.subtract),
                           reads=[Bla, Blah], writes=[Blal])

                def sec_gla_cumsum():
                    S_.mark(f"p1_t{t}_gla_cumsum")
                    for p in range(NP):
                        bk = 4 + p
                        for j in range(4):
                            for pi, (lv, lB) in enumerate([(lah_v, Blah), (lal_v, Blal)]):
                                S_.pe(lambda e, p=p, j=j, bk=bk, lv=lv, pi=pi: e.matmul(
                                    bank(bk)[:, j * 128:(j + 1) * 128],
                                    lhsT=lv[:, j, p * 128:(p + 1) * 128], rhs=triu_b.t[:, :],
                                    start=(pi == 0), stop=(pi == 1)),
                                    reads=[lB, triu_b.B], writes=[bankB[bk]])
                        S_.act(lambda e, p=p, bk=bk: e.activation(out=ebT_v[p][:, :], in_=bank(bk), func=AF.Exp, scale=-1.0 / 16),
                               reads=[bankB[bk]], writes=[BebT[p]])
                        S_.act(lambda e, p=p, bk=bk: e.activation(out=enbT_v[p][:, :], in_=bank(bk), func=AF.Exp, scale=1.0 / 16),
                               reads=[bankB[bk]], writes=[BenbT[p]])
                    for j in range(4):
                        bk = 4 + j // BPB
                        for pi, (lv, lB) in enumerate([(lah_v, Blah), (lal_v, Blal)]):
                            S_.pe(lambda e, j=j, bk=bk, lv=lv, pi=pi: e.matmul(
                                bank(bk)[:, (j % BPB) * GW:(j % BPB + 1) * GW],
                                lhsT=strl_b.t[:, :], rhs=lv[:, j, :], start=(pi == 0), stop=(pi == 1)),
                                reads=[lB, strl_b.B], writes=[bankB[bk]])
                    for half in range(4 // BPB):
                        S_.act(lambda e, half=half: e.activation(out=erev_v[:, BPB * half:BPB * half + BPB, :].rearrange("p j n -> p (j n)"),
                                                                 in_=bank(4 + half), func=AF.Exp, scale=-1.0 / 16),
                               reads=[bankB[4 + half]], writes=[Berev])

                def sec_gla_qin():
                    S_.mark(f"p1_t{t}_gla_qin")
                    for p in range(NP):
                        bq = nb()
                        fm_group(C_GQ + p * 128, 128, bq, t)
                        for hl in range(2):
                            S_.dve(lambda e, p=p, hl=hl, bq=bq: e.scalar_tensor_tensor(
                                out=qinm_v[:, p, hl, :], in0=bank(bq), scalar=hmask.t[:, hl:hl + 1],
                                in1=ebT_v[p][:, :], op0=ALU.mult, op1=ALU.mult),
                                reads=[bankB[bq], BebT[p], hmask.B], writes=[Bqin[p]])
                        bkk = nb()
                        fm_group(C_GK + p * 128, 128, bkk, t)
                        S_.dve(lambda e, p=p, bkk=bkk: e.tensor_tensor(out=kin_v[:, p, :], in0=bank(bkk), in1=enbT_v[p][:, :],
                                                                       op=ALU.mult),
                               reads=[bankB[bkk], BenbT[p]], writes=[Bkin[p]])
                    for j in range(4):
                        bv = nb()
                        for c in range(8):
                            S_.pe(lambda e, c=c, j=j, bv=bv: e.matmul(bank(bv)[:, 0:2 * GW], lhsT=hT_v[:, c, j * 128:(j + 1) * 128],
                                                                      rhs=win_v[:, c, C_GK:C_GK + 2 * GW], start=(c == 0), stop=(c == 7)),
                                  reads=[BW, BhT], writes=[bankB[bv]])
                        for cc in range(2):
                            S_.dve(lambda e, j=j, bv=bv, cc=cc: e.scalar_tensor_tensor(
                                out=kendm_v[:, j, cc, :], in0=bank(bv)[:, 0:GW], scalar=cmask.t[:, cc:cc + 1],
                                in1=erev_v[:, j, :], op0=ALU.mult, op1=ALU.mult),
                                reads=[bankB[bv], Berev, cmask.B], writes=[Bkend[j]])
                        S_.dve(lambda e, j=j, bv=bv: e.tensor_copy(out=vtok_v[:, j, :], in_=bank(bv)[:, GW:2 * GW]),
                               reads=[bankB[bv]], writes=[Bvtok[j]])
                        for hl in range(2):
                            S_.pool(lambda e, j=j, hl=hl: e.tensor_copy(
                                out=vpad_v[:, j, :, hl, hl * 64:(hl + 1) * 64],
                                in_=vtok_v[:, j, :].rearrange("p (a b e) -> p a b e", a=NP, b=2)[:, :, hl, :]),
                                reads=[Bvtok[j]], writes=[Bvpad[j]])

                def sec_gla_blocks():
                    S_.mark(f"p1_t{t}_gla_blocks")
                    for j in range(4):
                        bas = [nb() for _ in range(NP)]
                        for h in range(2 * NP):
                            p, hl = h // 2, h % 2
                            S_.pe(lambda e, p=p, hl=hl, j=j, bas=bas: e.matmul(
                                bank(bas[p])[:, hl * 128:(hl + 1) * 128],
                                lhsT=kin_v[:, p, j * 128:(j + 1) * 128],
                                rhs=qinm_v[:, p, hl, j * 128:(j + 1) * 128], start=True, stop=True),
                                reads=[Bkin[p], Bqin[p]], writes=[bankB[bas[p]]])
                        for p in range(NP):
                            S_.dve(lambda e, p=p, bas=bas: e.tensor_tensor(
                                out=am_v[:, 2 * p:2 * p + 2, :], in0=bank(bas[p])[:, 0:256].rearrange("p (h n) -> p h n", h=2),
                                in1=triu.t[:, :].unsqueeze(1).broadcast_to([128, 2, 128]), op=ALU.mult),
                                reads=[bankB[bas[p]], triu.B], writes=[Bam[p]])
                        for p in range(NP):
                            ob = bank(4 + p)[:, j * 128:(j + 1) * 128]
                            for hl in range(2):
                                S_.pe(lambda e, p=p, hl=hl, j=j, ob=ob: e.matmul(
                                    ob, lhsT=vpad_v[:, j, p, hl, :], rhs=am_v[:, 2 * p + hl, :], start=(hl == 0), stop=False),
                                    reads=[Bvpad[j], Bam[p]], writes=[bankB[4 + p]])
                        for cc in range(2):
                            ch = chunk_ctr[0]
                            chunk_ctr[0] += 1
                            sl = ch % NSL
                            S_.pool(lambda e, sl=sl: e.tensor_copy(out=Sbf_v[:, sl, :, :], in_=Sf_v[:, :, :]),
                                    reads=[BSf], writes=[BSbf[sl]])
                            for p in range(NP):
                                obc = bank(4 + p)[:, j * 128 + cc * 64: j * 128 + (cc + 1) * 64]
                                for hl in range(2):
                                    S_.pe(lambda e, p=p, hl=hl, j=j, cc=cc, sl=sl, obc=obc: e.matmul(
                                        obc, lhsT=Sbf_v[:, sl, p, :],
                                        rhs=qinm_v[:, p, hl, j * 128 + cc * 64: j * 128 + (cc + 1) * 64],
                                        start=False, stop=(cc == 1 and hl == 1)),
                                        reads=[BSbf[sl], Bqin[p]], writes=[bankB[4 + p]])
                            for p in range(NP):
                                S_.pe(lambda e, p=p, j=j, cc=cc: e.matmul(
                                    bank(6)[:, p * 128:(p + 1) * 128],
                                    lhsT=kendm_v[:, j, cc, p * 128:(p + 1) * 128],
                                    rhs=vtok_v[:, j, p * 128:(p + 1) * 128], start=True, stop=True),
                                    reads=[Bkend[j], Bvtok[j]], writes=[bankB[6]])
                            col = j * 128 + cc * 64 + 63
                            for p in range(NP):
                                for hl in range(2):
                                    S_.dve(lambda e, p=p, hl=hl, col=col: e.scalar_tensor_tensor(
                                        out=Sf_v[hl * 64:(hl + 1) * 64, p, hl * 64:(hl + 1) * 64],
                                        in0=Sf_v[hl * 64:(hl + 1) * 64, p, hl * 64:(hl + 1) * 64],
                                        scalar=ebT_v[p][hl * 64:(hl + 1) * 64, col:col + 1],
                                        in1=bank(6)[hl * 64:(hl + 1) * 64, p * 128 + hl * 64: p * 128 + (hl + 1) * 64],
                                        op0=ALU.mult, op1=ALU.add),
                                        reads=[BSf, BebT[p], bankB[6]], writes=[BSf])

                def sec_gla_out():
                    S_.mark(f"p1_t{t}_gla_out")
                    for p in range(NP):
                        bo = 4 + p
                        S_.act(lambda e, bo=bo: e.activation(out=sq_v[:, :], in_=bank(bo), func=AF.Square),
                               reads=[bankB[bo]], writes=[Bsq])
                        bm = nb()
                        S_.pe(lambda e, bm=bm: e.matmul(bank(bm), lhsT=blk64.t[:, :], rhs=sq_v[:, :], start=True, stop=True),
                              reads=[Bsq, blk64.B], writes=[bankB[bm]])
                        S_.act(lambda e, bm=bm: e.activation(out=rstd_v[:, :], in_=bank(bm), func=AF.Ln, bias=epsc.t[:, 0:1]),
                               reads=[bankB[bm], epsc.B], writes=[Brstd])
                        S_.act(lambda e: e.activation(out=rstd_v[:, :], in_=rstd_v[:, :], func=AF.Exp, scale=-0.5),
                               reads=[Brstd], writes=[Brstd])
                        S_.dve(lambda e, bo=bo: e.scalar_tensor_tensor(out=t1_v[:, :], in0=bank(bo), scalar=gng.t[:, 0:1],
                                                                       in1=rstd_v[:, :], op0=ALU.mult, op1=ALU.mult),
                               reads=[bankB[bo], gng.B, Brstd], writes=[Bt1])
                        bg = nb()
                        fm_group(C_GG + p * 128, 128, bg, t)
                        S_.act(lambda e, bg=bg: e.activation(out=sg_v[:, :], in_=bank(bg), func=AF.Silu),
                               reads=[bankB[bg]], writes=[Bsg])
                        S_.dve(lambda e: e.tensor_tensor(out=yg_v[:, :], in0=t1_v[:, :], in1=sg_v[:, :], op=ALU.mult),
                               reads=[Bt1, Bsg], writes=[Byg])
                        S_.dma(lambda e, p=p, tok0=tok0: e.dma_start(out=yT_d[NCV + NH + p, :, tok0:tok0 + 512], in_=yg_v[:, :]),
                               reads=[Byg], writes=[ByT])

                sec_gla()
                sec_conv()
                sec_gla_cumsum()
                sec_qk()
                if t + 1 < NT:
                    sec_norm(t + 1)
                sec_gla_qin()
                sec_v()
                sec_gla_blocks()
                sec_gla_out()


        def phase2(l):
            S_.mark(f"p2_{l}")
            S_.fence()
            a = ARENA
            o = 0

            def carve(n):
                nonlocal o
                v = a[:, o:o + n]
                o += n
                return v
            KT_v = carve(S // 2).bitcast(BF16)
            QT_v = carve(S).bitcast(BF16).rearrange("p (c n) -> p c n", c=2)
            V_v = carve(S // 2).bitcast(BF16).rearrange("p (b e) -> p b e", e=128)
            BKT, BQT, BV = Buf("KT"), Buf("QT"), Buf("V")
            prefetch_w = (o <= 16384) and not pair
            if prefetch_w:
                o = WW
                for c in range(6, 8):
                    S_.dma(lambda e, c=c: e.dma_start(out=w1_v[:, c, :], in_=w_mlp1[l, c * 128:(c + 1) * 128, :]),
                           writes=[BW1hi], q="pool")
                for c in range(32):
                    S_.dma(lambda e, c=c: e.dma_start(out=w2_v[:, c, :], in_=w_mlp2[l, c * 128:(c + 1) * 128, :]),
                           writes=[BW2], q="pool")
            PT_v = [carve(512).bitcast(BF16).rearrange("p (c n) -> p c n", c=2) for _ in range(2)]
            BPT = [Buf("PT0"), Buf("PT1")]
            r_v = carve(1024).rearrange("p (c n) -> p c n", c=2)
            Br = Buf("r")
            Oc_v = [carve(1024).rearrange("p (c n) -> p c n", c=2) for _ in range(2)]
            BOc = [Buf("Oc0"), Buf("Oc1")]
            sums_v = [carve(1024).rearrange("p (c n) -> p c n", c=2) for _ in range(2)]
            Bsums = [Buf("sums0"), Buf("sums1")]
            o_v = [carve(512) for _ in range(2)]
            Bo = [Buf("o0"), Buf("o1")]
            tile_ctr = [0]
            cur_it = [0]
            sq_v = carve(256).bitcast(BF16)
            Bsq = Buf("sq2")
            rstd_v = carve(512)
            Brstd = Buf("rstd2")
            y_v = carve(256).bitcast(BF16)
            By = Buf("y2")
            assert o <= AWORDS, o
            S_.dve(lambda e: e.memset(QT_v[64:128, 0, :], 0.0), writes=[BQT])
            S_.dve(lambda e: e.memset(QT_v[0:64, 1, :], 0.0), writes=[BQT])
            for h in range(NH):
                S_.dma(lambda e, h=h: e.dma_start(out=KT_v[:, :], in_=kT_d[h, :, :]), reads=[BkT], writes=[BKT])
                for comp in range(2):
                    S_.dma(lambda e, h=h, comp=comp: e.dma_start(out=QT_v[comp * 64:(comp + 1) * 64, comp, :],
                                                                 in_=qT_d[h, comp * 64:(comp + 1) * 64, :]),
                           reads=[BqT], writes=[BQT])
                S_.dma(lambda e, h=h: e.dma_start(out=V_v[:, :, :],
                                                  in_=v_d[:, h * 128:(h + 1) * 128].rearrange("(b p) e -> p b e", p=128)),
                       reads=[Bv], writes=[BV])
                THR = 64.0
                its = []
                for qt in range(NT):
                    q0 = qt * 512
                    kbs = []
                    for kb in range(4 * qt + 4):
                        dmin = q0 - (kb * 128 + 127)
                        if (not pair) and kb < 4 * qt and slopes[h] * dmin > THR:
                            continue
                        kbs.append(kb)
                    for n_, kb in enumerate(kbs):
                        its.append(dict(qt=qt, q0=q0, kb=kb, first=(n_ == 0), last=(n_ == len(kbs) - 1), idx=len(its)))

                def emit_qk(itx, h=h):
                    kb, qt, q0 = itx["kb"], itx["qt"], itx["q0"]
                    j = kb - 4 * qt
                    c0 = 0 if j < 0 else j * 128
                    sbk = (itx["idx"] % 2) * 2
                    diag = j >= 0
                    for comp in range(2):
                        S_.pe(lambda e, comp=comp: e.matmul(
                            bank(sbk + comp)[:, c0:512], lhsT=KT_v[:, kb * 128:(kb + 1) * 128],
                            rhs=QT_v[:, comp, q0 + c0:q0 + 512], start=True, stop=(not diag)),
                            reads=[BKT, BQT], writes=[bankB[sbk + comp]])
                        if diag:
                            S_.pe(lambda e, comp=comp: e.matmul(
                                bank(sbk + comp)[:, c0:c0 + 128], lhsT=ident.t[:, :], rhs=tdiag.t[:, h, :],
                                start=False, stop=True),
                                reads=[ident.B, tdiag.B], writes=[bankB[sbk + comp]])

                def emit_exp_pv(itx, h=h):
                    kb, qt, q0 = itx["kb"], itx["qt"], itx["q0"]
                    j = kb - 4 * qt
                    c0 = 0 if j < 0 else j * 128
                    sbk = (itx["idx"] % 2) * 2
                    pt = itx["idx"] % 2
                    di = (kb - 4 * qt) + 4 * (NT - 1)
                    first, last = itx["first"], itx["last"]
                    S_.act(lambda e: e.activation(
                        out=PT_v[pt][:, :, c0:512],
                        in_=PSf[:, sbk * 512:(sbk + 2) * 512].rearrange("p (c n) -> p c n", c=2)[:, :, c0:512],
                        func=AF.Exp, bias=kbias.t[:, h, di:di + 1], scale=1.0),
                        reads=[bankB[sbk], bankB[sbk + 1], kbias.B], writes=[BPT[pt]])
                    for comp in range(2):
                        S_.pe(lambda e, comp=comp: e.matmul(
                            bank(4 + comp)[:, c0:512], lhsT=V_v[:, kb, :], rhs=PT_v[pt][:, comp, c0:512],
                            start=first, stop=last),
                            reads=[BV, BPT[pt]], writes=[bankB[4 + comp]])
                        S_.pe(lambda e, comp=comp: e.matmul(
                            bank(6 + comp)[:, c0:512], lhsT=ones_bf.t[:, :], rhs=PT_v[pt][:, comp, c0:512],
                            start=first, stop=last),
                            reads=[ones_bf.B, BPT[pt]], writes=[bankB[6 + comp]])

                def stage_a(itx, h=h):
                    ab = itx["tile"] % 2
                    S_.dve(lambda e: e.tensor_copy(out=Oc_v[ab][:, :, :],
                                                   in_=PSf[:, 4 * 512:6 * 512].rearrange("p (c n) -> p c n", c=2)),
                           reads=[bankB[4], bankB[5]], writes=[BOc[ab]])
                    S_.dve(lambda e: e.tensor_copy(out=sums_v[ab][:, :, :],
                                                   in_=PSf[:, 6 * 512:8 * 512].rearrange("p (c n) -> p c n", c=2)),
                           reads=[bankB[6], bankB[7]], writes=[Bsums[ab]])

                def stage_b(itx, h=h):
                    ab = itx["tile"] % 2
                    S_.dve(lambda e: e.reciprocal(out=r_v[:, :, :], in_=sums_v[ab][:, :, :]), reads=[Bsums[ab]], writes=[Br])
                    S_.dve(lambda e: e.tensor_tensor(out=Oc_v[ab][:, :, :], in0=Oc_v[ab][:, :, :], in1=r_v[:, :, :], op=ALU.mult),
                           reads=[BOc[ab], Br], writes=[BOc[ab]])
                    S_.dve(lambda e: e.scalar_tensor_tensor(out=o_v[ab][:, :], in0=Oc_v[ab][:, 1, :], scalar=nlam.t[:, 0:1],
                                                            in1=Oc_v[ab][:, 0, :], op0=ALU.mult, op1=ALU.add),
                           reads=[BOc[ab], nlam.B], writes=[Bo[ab]])
                    S_.dve(lambda e: e.tensor_tensor(out=sq_v[:, :], in0=o_v[ab][:, :], in1=o_v[ab][:, :], op=ALU.mult),
                           reads=[Bo[ab]], writes=[Bsq])

                def stage_c(itx, h=h):
                    ab = itx["tile"] % 2
                    q0 = itx["q0"]
                    sbk = (cur_it[0] % 2) * 2
                    S_.pe(lambda e: e.matmul(bank(sbk), lhsT=o128.t[:, :], rhs=sq_v[:, :], start=True, stop=True),
                          reads=[o128.B, Bsq], writes=[bankB[sbk]])
                    S_.act(lambda e: e.activation(out=rstd_v[:, :], in_=bank(sbk), func=AF.Ln, bias=epsc.t[:, 0:1]),
                           reads=[bankB[sbk], epsc.B], writes=[Brstd])
                    S_.act(lambda e: e.activation(out=rstd_v[:, :], in_=rstd_v[:, :], func=AF.Exp, scale=-0.5),
                           reads=[Brstd], writes=[Brstd])
                    S_.dve(lambda e: e.scalar_tensor_tensor(out=y_v[:, :], in0=o_v[ab][:, :], scalar=subg.t[:, 0:1], in1=rstd_v[:, :],
                                                            op0=ALU.mult, op1=ALU.mult),
                           reads=[Bo[ab], subg.B, Brstd], writes=[By])
                    S_.dma(lambda e: e.dma_start(out=yT_d[NCV + h, :, q0:q0 + 512], in_=y_v[:, :]),
                           reads=[By], writes=[ByT])

                for itx in its:
                    if itx["first"]:
                        tile_ctr[0] += 1
                    itx["tile"] = tile_ctr[0]
                pending = []

                def tick():
                    for p_ in pending:
                        p_[0] -= 1
                    while pending and pending[0][0] <= 0:
                        _, fn_, it_ = pending.pop(0)
                        fn_(it_)

                emit_qk(its[0])
                for i_, itx in enumerate(its):
                    if i_ + 1 < len(its):
                        emit_qk(its[i_ + 1])
                    emit_exp_pv(itx)
                    cur_it[0] = itx["idx"]
                    tick()
                    if itx["last"]:
                        stage_a(itx)
                        pending.append([1, stage_b, itx])
                        pending.append([5, stage_c, itx])
                while pending:
                    _, fn_, it_ = pending.pop(0)
                    fn_(it_)

        def phase3(l, x_src, Bx_src, x_dst, Bx_dst, is_out):
            TT = 256
            NJ = TT // 128
            S_.mark(f"p3_{l}")
            S_.fence()
            S_.dma(lambda e: e.dma_start(out=gb2.t[:], in_=ln2_g[l:l + 1, :].partition_broadcast(128)), writes=[gb2.B])
            pre = (S // 2 + S + S // 2 <= 16384) and not pair and stop_after is None
            for c in range(8):
                S_.dma(lambda e, c=c: e.dma_start(out=wout_v[:, c, :], in_=w_out[l, c * 128:(c + 1) * 128, :]),
                       writes=[BWo], q="pool")
            for c in range(8):
                if pre and c >= 6:
                    continue
                S_.dma(lambda e, c=c: e.dma_start(out=w1_v[:, c, :], in_=w_mlp1[l, c * 128:(c + 1) * 128, :]),
                       writes=[BW1lo if c < 6 else BW1hi], q="pool")
            if not pre:
                for c in range(32):
                    S_.dma(lambda e, c=c: e.dma_start(out=w2_v[:, c, :], in_=w_mlp2[l, c * 128:(c + 1) * 128, :]),
                           writes=[BW2], q="pool")
            a = ARENA
            o = WW

            def carve(n):
                nonlocal o
                v = a[:, o:o + n]
                o += n
                return v
            yT_v = [carve(8 * TT // 2).bitcast(BF16).rearrange("p (c n) -> p c n", c=8)] * 2
            ByT_s = [Buf("yTs0")] * 2
            if pair:
                yTb_v = carve(8 * TT // 2).bitcast(BF16).rearrange("p (c n) -> p c n", c=8)
                ByTb = Buf("yTb")
            x_v = [carve(NJ * D).rearrange("p (j n) -> p j n", j=NJ) for _ in range(2)]
            Bx_s = [Buf("xs0"), Buf("xs1")]
            hb_v = carve(NJ * D // 2).bitcast(BF16).rearrange("p (j n) -> p j n", j=NJ)
            Bhb = Buf("hb3")
            hT_v = carve(8 * TT // 2).bitcast(BF16).rearrange("p (c n) -> p c n", c=8)
            BhT = Buf("hT3")
            aT_v = carve(32 * TT // 2).bitcast(BF16).rearrange("p (f n) -> p f n", f=32)
            BaT = [Buf(f"aT{f}") for f in range(32)]
            ss_v = carve(NJ)
            rs_v = carve(NJ)
            Bss, Brs = Buf("ss3"), Buf("rs3")
            relu_v = [carve(TT) for _ in range(2)]
            Brelu = [Buf("relu0"), Buf("relu1")]
            assert o <= AWORDS, o
            rr = [0]

            def nb():
                k = rr[0] % 7
                rr[0] += 1
                return k
            cp = [0]
            NTT = SP // TT

            def st_load(t):
                tok0 = t * TT
                s = t % 2
                if pair:
                    S_.dma(lambda e: e.dma_start(out=yT_v[s][:, :, :],
                                                 in_=yTall_d[:, :, tok0:tok0 + TT].rearrange("c p n -> p c n")),
                           reads=[ByTall], writes=[ByT_s[s]])
                    S_.dma(lambda e: e.dma_start(out=yTb_v[:, :, :],
                                                 in_=yTall_d[:, :, SP + tok0:SP + tok0 + TT].rearrange("c p n -> p c n")),
                           reads=[ByTall], writes=[ByTb])
                    S_.dve(lambda e: e.tensor_scalar_mul(out=yT_v[s][:, :, :], in0=yT_v[s][:, :, :], scalar1=rmask.t[:, 0:1]),
                           reads=[ByT_s[s], rmask.B], writes=[ByT_s[s]])
                    S_.dve(lambda e: e.scalar_tensor_tensor(out=yT_v[s][:, :, :], in0=yTb_v[:, :, :], scalar=rmask.t[:, 1:2],
                                                            in1=yT_v[s][:, :, :], op0=ALU.mult, op1=ALU.add),
                           reads=[ByT_s[s], ByTb, rmask.B], writes=[ByT_s[s]])
                else:
                    S_.dma(lambda e: e.dma_start(out=yT_v[s][:, :, :],
                                                 in_=yT_d[:, :, tok0:tok0 + TT].rearrange("c p n -> p c n")),
                           reads=[ByT], writes=[ByT_s[s]])
                S_.dma(lambda e: e.dma_start(out=x_v[s][:, :, :],
                                             in_=x_src[tok0:tok0 + TT, :].rearrange("(j p) n -> p j n", p=128)),
                       reads=[Bx_src], writes=[Bx_s[s]])

            def st_outproj(t):
                s = t % 2
                for j in range(NJ):
                    for n in range(2):
                        bk = nb()
                        for c in range(8):
                            S_.pe(lambda e, c=c, j=j, n=n, bk=bk: e.matmul(
                                bank(bk), lhsT=yT_v[s][:, c, j * 128:(j + 1) * 128], rhs=wout_v[:, c, n * 512:(n + 1) * 512],
                                start=(c == 0), stop=(c == 7)), reads=[ByT_s[s], BWo], writes=[bankB[bk]])
                        S_.dve(lambda e, j=j, n=n, bk=bk: e.tensor_tensor(
                            out=x_v[s][:, j, n * 512:(n + 1) * 512], in0=bank(bk), in1=x_v[s][:, j, n * 512:(n + 1) * 512],
                            op=ALU.add), reads=[bankB[bk], Bx_s[s]], writes=[Bx_s[s]])

            def st_norm(t):
                s = t % 2
                for j in range(NJ):
                    S_.act(lambda e, j=j: e.activation(out=hb_v[:, j, :], in_=x_v[s][:, j, :], func=AF.Square,
                                                       accum_out=ss_v[:, j:j + 1]),
                           reads=[Bx_s[s]], writes=[Bhb, Bss])
                S_.act(lambda e: e.activation(out=rs_v[:, :], in_=ss_v[:, :], func=AF.Ln, bias=epsc.t[:, 0:1], scale=1.0 / D),
                       reads=[Bss, epsc.B], writes=[Brs])
                S_.act(lambda e: e.activation(out=rs_v[:, :], in_=rs_v[:, :], func=AF.Exp, scale=-0.5),
                       reads=[Brs], writes=[Brs])
                for j in range(NJ):
                    S_.dve(lambda e, j=j: e.scalar_tensor_tensor(out=hb_v[:, j, :], in0=x_v[s][:, j, :],
                                                                 scalar=rs_v[:, j:j + 1], in1=gb2.t[:, :],
                                                                 op0=ALU.mult, op1=ALU.mult),
                           reads=[Bx_s[s], Brs, gb2.B], writes=[Bhb])

            def st_transpose(t):
                for c4 in range(2):
                    for cc_ in range(4):
                        c = c4 * 4 + cc_
                        for j in range(NJ):
                            S_.pe(lambda e, c=c, j=j, cc_=cc_: e.transpose(
                                out=PSb[:, cc_ * TT + j * 128: cc_ * TT + (j + 1) * 128],
                                in_=hb_v[:, j, c * 128:(c + 1) * 128], identity=ident.t[:, :]),
                                reads=[Bhb, ident.B], writes=[bankB[7]])
                    cp[0] += 1
                    if cp[0] % 2 == 0:
                        S_.act(lambda e, c4=c4: e.copy(out=hT_v[:, c4 * 4:c4 * 4 + 4, :],
                                                       in_=PSb[:, :].rearrange("p (c n) -> p c n", c=4)),
                               reads=[bankB[7]], writes=[BhT])
                    else:
                        S_.dve(lambda e, c4=c4: e.tensor_copy(out=hT_v[:, c4 * 4:c4 * 4 + 4, :],
                                                              in_=PSb[:, :].rearrange("p (c n) -> p c n", c=4)),
                               reads=[bankB[7]], writes=[BhT])

            def st_hidden(t):
                for f in range(32):
                    bk = nb()
                    for c in range(8):
                        S_.pe(lambda e, c=c, f=f, bk=bk: e.matmul(
                            bank(bk)[:, 0:TT], lhsT=w1_v[:, c, f * 128:(f + 1) * 128], rhs=hT_v[:, c, :],
                            start=(c == 0), stop=(c == 7)), reads=[BW1lo, BW1hi, BhT], writes=[bankB[bk]])
                    rt = f % 2
                    S_.act(lambda e, rt=rt, bk=bk: e.activation(out=relu_v[rt][:, :], in_=bank(bk)[:, 0:TT], func=AF.Relu),
                           reads=[bankB[bk]], writes=[Brelu[rt]])
                    if f % 2 == 0:
                        S_.dve(lambda e, f=f, rt=rt: e.tensor_tensor(out=aT_v[:, f, :], in0=relu_v[rt][:, :], in1=relu_v[rt][:, :],
                                                                     op=ALU.mult), reads=[Brelu[rt]], writes=[BaT[f]])
                    else:
                        S_.pool(lambda e, f=f, rt=rt: e.tensor_tensor(out=aT_v[:, f, :], in0=relu_v[rt][:, :], in1=relu_v[rt][:, :],
                                                                      op=ALU.mult), reads=[Brelu[rt]], writes=[BaT[f]])

            def st_down(t, j):
                s = t % 2
                for n in range(2):
                    bk = nb()
                    for f in range(32):
                        S_.pe(lambda e, f=f, n=n, bk=bk: e.matmul(
                            bank(bk), lhsT=aT_v[:, f, j * 128:(j + 1) * 128], rhs=w2_v[:, f, n * 512:(n + 1) * 512],
                            start=(f == 0), stop=(f == 31)), reads=[BaT[f], BW2], writes=[bankB[bk]])
                    S_.dve(lambda e, n=n, bk=bk: e.tensor_tensor(
                        out=x_v[s][:, j, n * 512:(n + 1) * 512], in0=bank(bk), in1=x_v[s][:, j, n * 512:(n + 1) * 512],
                        op=ALU.add), reads=[bankB[bk], Bx_s[s]], writes=[Bx_s[s]])

            def st_store(t):
                tok0 = t * TT
                s = t % 2
                op = S_.dma(lambda e: e.dma_start(
                    out=x_dst[tok0:tok0 + TT, :].rearrange("(j p) n -> p j n", p=128), in_=x_v[s][:, :, :]),
                    reads=[Bx_s[s]], writes=[Bx_dst])
                if is_out:
                    out_ops.append(op)

            st_load(0)
            st_outproj(0)
            st_norm(0)
            st_transpose(0)
            for t in range(NTT):
                if t + 1 < NTT:
                    st_load(t + 1)
                st_hidden(t)
                if t + 1 < NTT:
                    st_outproj(t + 1)
                    st_norm(t + 1)
                st_down(t, 0)
                if t + 1 < NTT:
                    st_transpose(t + 1)
                for j in range(1, NJ):
                    st_down(t, j)
                st_store(t)

        for l in range(DEPTH):
            load_params(l)
            x_src, Bsrc = (x_in, Buf("xin")) if l == 0 else (xs_d, Bxs)
            last = (l == DEPTH - 1)
            phase1(l, x_src, Bsrc)
            if stop_after == "p1":
                break
            phase2(l)
            if stop_after == "p2":
                break
            if pair:
                S_.dma(lambda e: e.collective_compute(
                    "AllGather", ALU.bypass, replica_groups=groups,
                    ins=[yT_d.rearrange("c p n -> (c p) n")], outs=[yTall_d.rearrange("c p n -> (c p) n")]),
                    reads=[ByT], writes=[ByTall], q="pool", ring="cc")
                x3_src, B3src = (xh_in, Buf("xhin")) if l == 0 else (xm_d, Bxm)
                x_dst, Bdst = (out_d, Bout) if last else (xm_d, Bxm)
                phase3(l, x3_src, B3src, x_dst, Bdst, last)
                if not last:
                    S_.dma(lambda e: e.collective_compute(
                        "AllGather", ALU.bypass, replica_groups=groups, ins=[xm_d[:, :]], outs=[xs_d[:, :]]),
                        reads=[Bxm], writes=[Bxs], q="pool", ring="cc")
            else:
                x_dst, Bdst = (out_d, Bout) if last else (xs_d, Bxs)
                phase3(l, x_src, Bsrc, x_dst, Bdst, last)

        if trunc is not None:
            S_.ops = S_.ops[:trunc]
            out_ops = []
            for en in ENGS:
                idxs = [i for i, o_ in enumerate(S_.ops) if o_[0] == en]
                if idxs:
                    out_ops.append(idxs[-1])
        if not out_ops:
            out_ops.append(len(S_.ops) - 1)
        S_.emit(nc, final_wait_ops=out_ops)
    return nc, S_


def _pair_layout(r, x_b, p):
    S = x_b.shape[0]
    SP = S // 2
    cols = np.concatenate([
        np.arange(0 + r * 128, 0 + (r + 1) * 128),
        np.arange(256 + r * 128, 256 + (r + 1) * 128),
        np.arange(512 + r * 128, 512 + (r + 1) * 128),
        np.arange(768 + 2 * r * 128, 768 + (2 * r + 2) * 128),
        np.arange(1280 + 2 * r * 128, 1280 + (2 * r + 2) * 128),
        np.arange(1792 + 2 * r * 128, 1792 + (2 * r + 2) * 128),
        np.arange(2304 + r * 128, 2304 + (r + 1) * 128),
        np.arange(2560 + r * 128, 2560 + (r + 1) * 128),
        np.arange(2816 + r * 128, 2816 + (r + 1) * 128),
        np.arange(3072 + r * 128, 3072 + (r + 1) * 128),
        np.arange(3328, 3344),
    ])
    rows = np.concatenate([
        np.concatenate([np.arange(i * 128, (i + 1) * 128),
                        np.arange(256 + 2 * i * 128, 256 + (2 * i + 2) * 128),
                        np.arange(768 + i * 128, 768 + (i + 1) * 128)]) for i in range(2)])
    slopes = np.array([[2.0 ** (-8.0 * (2 * r + hh + 1) / 4) for hh in range(2)]], dtype=np.float32)
    m = dict(p)
    m["x"] = np.ascontiguousarray(x_b)
    m["x_half"] = np.ascontiguousarray(x_b[r * SP:(r + 1) * SP])
    m["w_in"] = np.ascontiguousarray(p["w_in"][:, :, cols])
    m["conv_w"] = np.ascontiguousarray(p["conv_w"][:, :, r * 128:(r + 1) * 128])
    m["gla_alpha_w"] = np.ascontiguousarray(p["gla_alpha_w"][:, :, r * 128:(r + 1) * 128])
    m["gla_alpha_b"] = np.ascontiguousarray(p["gla_alpha_b"][:, r * 128:(r + 1) * 128])
    m["w_out"] = np.ascontiguousarray(p["w_out"][:, rows, :])
    m["slopes"] = slopes
    m["rmask"] = np.array([[1.0 - r, float(r)]], dtype=np.float32)
    return m


def kernel(x, ln1_g, w_in, conv_w, q_norm_g, k_norm_g, diff_lambda, diff_subln_g,
           gla_alpha_w, gla_alpha_b, gla_norm_g, w_out, ln2_g, w_mlp1, w_mlp2):
    x = np.asarray(x, dtype=np.float32)
    B, S, _ = x.shape
    depth = int(np.asarray(ln1_g).shape[0])
    nc, _ = build(S=S, DEPTH=depth)
    shared = dict(ln1_g=ln1_g, w_in=w_in, conv_w=conv_w, q_norm_g=q_norm_g, k_norm_g=k_norm_g,
                  diff_lambda=diff_lambda, diff_subln_g=diff_subln_g, gla_alpha_w=gla_alpha_w,
                  gla_alpha_b=gla_alpha_b, gla_norm_g=gla_norm_g, w_out=w_out, ln2_g=ln2_g,
                  w_mlp1=w_mlp1, w_mlp2=w_mlp2)
    shared = {k: np.ascontiguousarray(np.asarray(v, dtype=np.float32)) for k, v in shared.items()}
    in_maps = []
    for c in range(B):
        m = dict(shared)
        m["x"] = np.ascontiguousarray(x[c])
        in_maps.append(m)
    res = run_bass_kernel_spmd(nc, in_maps, core_ids=list(range(B)))
    return np.stack([np.asarray(r["out"], dtype=np.float32) for r in res.results], axis=0)
```

```python
import math
from contextlib import ExitStack
import numpy as np
import concourse.bass as bass
import concourse.mybir as mybir
from concourse.bass_utils import run_bass_kernel_spmd

F32 = mybir.dt.float32
BF16 = mybir.dt.bfloat16
I32 = mybir.dt.int32
ALU = mybir.AluOpType
AF = mybir.ActivationFunctionType
AX = mybir.AxisListType

ENGS = ("pe", "act", "dve", "pool", "sp")
SEM_EPOCH = 24000
DMA_RING = 8

D = 1024
DIN = 3344
DFF = 4096
EPS = 1e-6
NEG = -30000.0


class Buf:
    __slots__ = ("name", "w", "r")

    def __init__(self, name=""):
        self.name = name
        self.w = None
        self.r = []


class Sched:
    def __init__(self):
        self.ops = []
        self.sameeng_sync = {"act": True, "dve": True, "pool": True, "pe": False, "sp": False}
        self.marks = []
        self.fence_deps = None
        self.fence_id = 0
        self.fence_passed = {}
        self.last_ops = {e: [] for e in ENGS}

    def mark(self, name):
        self.marks.append((name, len(self.ops)))

    def fence(self):
        deps = set()
        for e in ENGS:
            deps.update(self.last_ops[e][-(DMA_RING + 1):])
        self.fence_deps = deps
        self.fence_id += 1

    def add(self, eng, fn, reads=(), writes=(), dma=False, ring="d"):
        idx = len(self.ops)
        deps = set()
        if self.fence_deps is not None and self.fence_passed.get(eng) != self.fence_id:
            deps.update(self.fence_deps)
            self.fence_passed[eng] = self.fence_id
        self.last_ops[eng].append(idx)
        if len(self.last_ops[eng]) > 4 * DMA_RING:
            del self.last_ops[eng][:-2 * DMA_RING]
        for b in reads:
            if b.w is not None:
                deps.add(b.w)
        for b in writes:
            if b.w is not None:
                deps.add(b.w)
            for r in b.r:
                deps.add(r)
        for b in reads:
            b.r.append(idx)
        for b in writes:
            b.w = idx
            b.r = []
        deps.discard(idx)
        self.ops.append([eng, fn, deps, dma, ring])
        return idx

    def pe(self, fn, reads=(), writes=()):
        return self.add("pe", fn, reads, writes)

    def act(self, fn, reads=(), writes=()):
        return self.add("act", fn, reads, writes)

    def dve(self, fn, reads=(), writes=()):
        return self.add("dve", fn, reads, writes)

    def pool(self, fn, reads=(), writes=()):
        return self.add("pool", fn, reads, writes)

    def dma(self, fn, reads=(), writes=(), q="sp", ring="d"):
        return self.add(q, fn, reads, writes, dma=True, ring=ring)

    def emit(self, nc, final_wait_ops=()):
        ops = self.ops
        n = len(ops)
        signal = [False] * n
        for i, (eng, fn, deps, dma, _rg) in enumerate(ops):
            if dma:
                signal[i] = True
            for d in deps:
                if ops[d][3]:
                    continue
                if ops[d][0] != eng or self.sameeng_sync.get(eng, False):
                    signal[d] = True
        for i in final_wait_ops:
            signal[i] = True
        sem_of = [None] * n
        sem_specs = []
        cur = {e: None for e in ENGS}
        dma_ring = {}
        dma_rr = {}
        dma_prev = [None] * n
        ring_last = {}
        for i, (eng, fn, deps, dma, rg) in enumerate(ops):
            if not signal[i]:
                continue
            if dma:
                rk = (eng, rg)
                ring = dma_ring.setdefault(rk, [])
                nslots = DMA_RING if rg == "d" else 1
                slot = dma_rr.get(rk, 0) % nslots
                dma_rr[rk] = dma_rr.get(rk, 0) + 1
                if len(ring) <= slot:
                    sem_specs.append(f"{rg}_{eng}_{slot}_{len(sem_specs)}")
                    ring.append([len(sem_specs) - 1, 0])
                if ring[slot][1] + 16 > SEM_EPOCH:
                    sem_specs.append(f"{rg}_{eng}_{slot}_{len(sem_specs)}")
                    ring[slot] = [len(sem_specs) - 1, 0]
                ring[slot][1] += 16
                sem_of[i] = (ring[slot][0], ring[slot][1])
                key = (eng, rg, slot)
                dma_prev[i] = ring_last.get(key)
                ring_last[key] = i
            else:
                c = cur[eng]
                if c is None or c[1] + 1 > SEM_EPOCH:
                    sem_specs.append(f"s_{eng}_{len(sem_specs)}")
                    c = [len(sem_specs) - 1, 0]
                    cur[eng] = c
                c[1] += 1
                sem_of[i] = (c[0], c[1])
        self.n_sems = len(sem_specs)
        per_eng = {e: [] for e in ENGS}
        for i, op in enumerate(ops):
            per_eng[op[0]].append(i)
        self.stats = {e: len(per_eng[e]) for e in ENGS}
        self.stats["signals"] = sum(signal)

        with ExitStack() as st:
            sems = [st.enter_context(nc.semaphore(nm)) for nm in sem_specs]
            block = st.enter_context(nc.Block())

            def make(engname):
                idxs = per_eng[engname]

                def body(e):
                    seen = {}
                    nwait = 0
                    for i in idxs:
                        eng, fn, deps, dma, _rg = ops[i]
                        waits = {}
                        dl = list(deps)
                        if dma and dma_prev[i] is not None:
                            dl.append(dma_prev[i])
                        for d in dl:
                            if (not ops[d][3]) and ops[d][0] == eng and not self.sameeng_sync.get(eng, False):
                                continue
                            s, v = sem_of[d]
                            if seen.get(s, 0) >= v:
                                continue
                            if waits.get(s, 0) < v:
                                waits[s] = v
                        for s, v in waits.items():
                            e.wait_ge(sems[s], v)
                            seen[s] = v
                            nwait += 1
                        ins = fn(e)
                        if signal[i]:
                            s, v = sem_of[i]
                            ins.then_inc(sems[s], 16 if dma else 1)
                    if engname == "sp":
                        for i in final_wait_ops:
                            s, v = sem_of[i]
                            e.wait_ge(sems[s], v)
                    self.stats["waits_" + engname] = nwait
                return body

            block.tensor(make("pe"))
            block.scalar(make("act"))
            block.vector(make("dve"))
            block.gpsimd(make("pool"))
            block.sync(make("sp"))


class Tl:
    def __init__(self, t, nb=1, name=""):
        self.t = t
        self.b = [Buf(f"{name}{i}") for i in range(nb)]

    @property
    def B(self):
        return self.b[0]


def build(S=8192, DEPTH=4, stop_after=None, debug=False, trunc=None, pair=False, ncores=4):
    NT = S // 512
    NB = S // 128
    nc = bass.Bass("TRN2", target_bir_lowering=False)
    if pair:
        NCV, NH, NP = 1, 2, 1
        C_U, C_CB, C_CC, C_Q, C_K, C_V, C_GQ, C_GK, C_GG, C_GA = 0, 128, 256, 384, 640, 896, 1152, 1280, 1536, 1664
        DINL = 1680
        SP = S // 2
    else:
        NCV, NH, NP = 2, 4, 2
        C_U, C_CB, C_CC, C_Q, C_K, C_V, C_GQ, C_GK, C_GG, C_GA = 0, 256, 512, 768, 1280, 1792, 2304, 2560, 3072, 3328
        DINL = DIN
        SP = S
    GW = NP * 128
    VW = NH * 128
    NYC = NCV + NH + NP
    BPB = 512 // GW

    def din(name, shape):
        return nc.dram_tensor(name, shape, F32, kind="ExternalInput").ap()

    x_in = din("x", [S, D])
    if pair:
        xh_in = din("x_half", [SP, D])
        slopes_in = din("slopes", [1, NH])
        rmask_in = din("rmask", [1, 2])
    ln1_g = din("ln1_g", [DEPTH, D])
    w_in = din("w_in", [DEPTH, D, DINL])
    conv_w = din("conv_w", [DEPTH, 3, NCV * 128])
    q_norm_g = din("q_norm_g", [DEPTH, 64])
    k_norm_g = din("k_norm_g", [DEPTH, 64])
    diff_lambda = din("diff_lambda", [DEPTH, 4, 64])
    diff_subln_g = din("diff_subln_g", [DEPTH, 128])
    gla_alpha_w = din("gla_alpha_w", [DEPTH, 16, GW])
    gla_alpha_b = din("gla_alpha_b", [DEPTH, GW])
    gla_norm_g = din("gla_norm_g", [DEPTH, 64])
    w_out = din("w_out", [DEPTH, D, D])
    ln2_g = din("ln2_g", [DEPTH, D])
    w_mlp1 = din("w_mlp1", [DEPTH, D, DFF])
    w_mlp2 = din("w_mlp2", [DEPTH, DFF, D])
    out_d = nc.dram_tensor("out", [SP, D], F32, kind="ExternalOutput").ap()

    xs_d = nc.dram_tensor("xs_scr", [S, D], F32, kind="Internal").ap()
    sk = "ExternalOutput" if debug else "Internal"
    qT_d = nc.dram_tensor("qT_scr", [NH, 128, S], BF16, kind=sk).ap()
    kT_d = nc.dram_tensor("kT_scr", [NH, 128, S], BF16, kind=sk).ap()
    v_d = nc.dram_tensor("v_scr", [S, VW], BF16, kind=sk).ap()
    yT_d = nc.dram_tensor("yT_scr", [NYC, 128, S], BF16, kind=sk).ap()
    if pair:
        xm_d = nc.dram_tensor("xm_scr", [SP, D], F32, kind="Internal").ap()
        yTall_d = nc.dram_tensor("yTall_scr", [2 * NYC, 128, S], BF16, kind="Internal").ap()
        Bxm, ByTall = Buf("xm"), Buf("yTall")
        groups = [[2 * i, 2 * i + 1] for i in range(ncores // 2)]
    Bxs, BqT, BkT, Bv, ByT, Bout = Buf("xs"), Buf("qT"), Buf("kT"), Buf("v"), Buf("yT"), Buf("out")

    S_ = Sched()
    out_ops = []

    with ExitStack() as st:
        def sb(name, shape, dt, nb=1):
            return Tl(st.enter_context(nc.sbuf_tensor(name, shape, dt)), nb, name)

        PSf = st.enter_context(nc.psum_tensor("psf", [128, 8 * 512], F32))
        bankB = [Buf(f"bank{i}") for i in range(8)]

        def bank(i):
            return PSf[:, i * 512:(i + 1) * 512]

        PSb = bank(7).bitcast(BF16)

        ident = sb("ident", [128, 128], BF16)
        ones_bf = sb("ones_bf", [128, 128], BF16)
        blk64 = sb("blk64", [128, 128], BF16)
        o128 = sb("o128", [128, 128], BF16)
        triu = sb("triu", [128, 128], F32)
        strl = sb("strl", [128, 128], F32)
        triu_b = sb("triu_b", [128, 128], BF16)
        strl_b = sb("strl_b", [128, 128], BF16)
        iot = sb("iot", [128, 128], I32)
        dkq = sb("dkq", [128, 128], F32)
        tdiag = sb("tdiag", [128, 4, 128], BF16)
        kcol = sb("kcol", [128, 1], F32)
        kcoli = sb("kcoli", [128, 1], I32)
        NDEL = 4 * (NT - 1) + 4
        kbias = sb("kbias", [128, 4, NDEL], F32)
        kdel = sb("kdel", [128, NDEL], F32)
        slopec = sb("slopec", [128, 4], F32)
        slopes = [2.0 ** (-8.0 * (h + 1) / 4) for h in range(4)]
        if pair:
            rmask = sb("rmask_sb", [128, 2], F32)
            S_.dma(lambda e: e.dma_start(out=slopec.t[:, 0:NH], in_=slopes_in[0:1, :].partition_broadcast(128)), writes=[slopec.B])
            S_.dma(lambda e: e.dma_start(out=rmask.t[:, :], in_=rmask_in[0:1, :].partition_broadcast(128)), writes=[rmask.B])
        else:
            for h in range(4):
                S_.pool(lambda e, h=h: e.memset(slopec.t[:, h:h + 1], slopes[h]), writes=[slopec.B])

        hmask = sb("hmask", [128, 2], F32)
        cmask = sb("cmask", [128, 2], F32)
        S_.pool(lambda e: e.memset(hmask.t[:], 0.0), writes=[hmask.B])
        S_.pool(lambda e: e.memset(hmask.t[0:64, 0:1], 0.125), writes=[hmask.B])
        S_.pool(lambda e: e.memset(hmask.t[64:128, 1:2], 0.125), writes=[hmask.B])
        S_.pool(lambda e: e.memset(cmask.t[:], 0.0), writes=[cmask.B])
        S_.pool(lambda e: e.memset(cmask.t[0:64, 0:1], 1.0), writes=[cmask.B])
        S_.pool(lambda e: e.memset(cmask.t[64:128, 1:2], 1.0), writes=[cmask.B])
        epsc = sb("epsc", [128, 1], F32)
        onec = sb("onec", [128, 1], F32)
        S_.pool(lambda e: e.memset(epsc.t[:], EPS), writes=[epsc.B])
        S_.pool(lambda e: e.memset(onec.t[:], 1.0), writes=[onec.B])
        S_.pool(lambda e: e.memset(ident.t[:], 0.0), writes=[ident.B])
        S_.pool(lambda e: e.affine_select(out=ident.t[:], in_=ident.t[:], pattern=[[-1, 128]],
                                          compare_op=ALU.not_equal, fill=1.0, base=0, channel_multiplier=1),
                reads=[ident.B], writes=[ident.B])
        S_.pool(lambda e: e.memset(ones_bf.t[:], 1.0), writes=[ones_bf.B])
        S_.pool(lambda e: e.memset(o128.t[:], 1.0 / 128), writes=[o128.B])
        S_.pool(lambda e: e.memset(blk64.t[:], 1.0 / 64), writes=[blk64.B])
        S_.pool(lambda e: e.memset(blk64.t[0:64, 64:128], 0.0), writes=[blk64.B])
        S_.pool(lambda e: e.memset(blk64.t[64:128, 0:64], 0.0), writes=[blk64.B])
        S_.pool(lambda e: e.memset(triu.t[:], 1.0), writes=[triu.B])
        S_.pool(lambda e: e.affine_select(out=triu.t[:], in_=triu.t[:], pattern=[[1, 128]],
                                          compare_op=ALU.is_ge, fill=0.0, base=0, channel_multiplier=-1),
                reads=[triu.B], writes=[triu.B])
        S_.pool(lambda e: e.memset(triu.t[0:64, 64:128], 0.0), writes=[triu.B])
        S_.pool(lambda e: e.memset(strl.t[:], 1.0), writes=[strl.B])
        S_.pool(lambda e: e.affine_select(out=strl.t[:], in_=strl.t[:], pattern=[[-1, 128]],
                                          compare_op=ALU.is_gt, fill=0.0, base=0, channel_multiplier=1),
                reads=[strl.B], writes=[strl.B])
        S_.pool(lambda e: e.memset(strl.t[64:128, 0:64], 0.0), writes=[strl.B])
        S_.dve(lambda e: e.tensor_copy(out=triu_b.t[:], in_=triu.t[:]), reads=[triu.B], writes=[triu_b.B])
        S_.dve(lambda e: e.tensor_copy(out=strl_b.t[:], in_=strl.t[:]), reads=[strl.B], writes=[strl_b.B])
        S_.pool(lambda e: e.iota(iot.t[:], pattern=[[-1, 128]], base=0, channel_multiplier=1), writes=[iot.B])
        S_.dve(lambda e: e.tensor_copy(out=dkq.t[:], in_=iot.t[:]), reads=[iot.B], writes=[dkq.B])
        S_.dve(lambda e: e.tensor_scalar_max(out=dkq.t[:], in0=dkq.t[:], scalar1=0.0), reads=[dkq.B], writes=[dkq.B])
        S_.dve(lambda e: e.tensor_scalar_mul(out=dkq.t[:], in0=dkq.t[:], scalar1=-2.0), reads=[dkq.B], writes=[dkq.B])
        for h in range(NH):
            def f(e, h=h):
                return e.tensor_scalar_mul(out=tdiag.t[:, h, :], in0=dkq.t[:], scalar1=slopec.t[:, h:h + 1])
            S_.dve(f, reads=[dkq.B, slopec.B], writes=[tdiag.B])
        S_.dve(lambda e: e.memset(tdiag.t[64:128, :, 0:64], NEG), reads=[tdiag.B], writes=[tdiag.B])
        S_.pool(lambda e: e.iota(kcoli.t[:], pattern=[[0, 1]], base=0, channel_multiplier=1), writes=[kcoli.B])
        S_.dve(lambda e: e.tensor_copy(out=kcol.t[:], in_=kcoli.t[:]), reads=[kcoli.B], writes=[kcol.B])
        for di in range(NDEL):
            delta = di - 4 * (NT - 1)

            def f(e, di=di, delta=delta):
                return e.tensor_scalar_add(out=kdel.t[:, di:di + 1], in0=kcol.t[:], scalar1=float(128 * delta - 256))
            S_.dve(f, reads=[kcol.B], writes=[kdel.B])
        for h in range(NH):
            S_.dve(lambda e, h=h: e.tensor_scalar_mul(out=kbias.t[:, h, :], in0=kdel.t[:, :], scalar1=slopec.t[:, h:h + 1]),
                   reads=[kdel.B, slopec.B], writes=[kbias.B])

        gb1 = sb("gb", [128, D], F32)
        gb2 = gb1
        cw = sb("cw", [128, 2, 3], F32)
        qg = sb("qg", [128, 1], F32)
        kg = sb("kg", [128, 1], F32)
        lamt = sb("lamt", [128, 4, 64], F32)
        lamw = sb("lamw", [128, 2, 64], F32)
        lams = sb("lams", [128, 2], F32)
        nlam = sb("nlam", [128, 1], F32)
        subg = sb("subg", [128, 1], F32)
        aw = sb("aw", [17, GW], F32)
        aw_hi = sb("aw_hi", [17, GW], BF16)
        aw_lo = sb("aw_lo", [17, GW], BF16)
        gng = sb("gng", [128, 1], F32)

        def load_params(l):
            S_.mark(f"params{l}")
            S_.dma(lambda e: e.dma_start(out=gb1.t[:], in_=ln1_g[l:l + 1, :].partition_broadcast(128)), writes=[gb1.B])
            for cc_ in range(NCV):
                for k_ in range(3):
                    S_.dma(lambda e, cc_=cc_, k_=k_: e.dma_start(
                        out=cw.t[:, cc_, k_:k_ + 1],
                        in_=conv_w[l, k_, cc_ * 128:(cc_ + 1) * 128].rearrange("(p o) -> p o", o=1)), writes=[cw.B])
            for hh in range(2):
                S_.dma(lambda e, hh=hh: e.dma_start(out=qg.t[hh * 64:(hh + 1) * 64, :],
                                                    in_=q_norm_g[l].rearrange("(p o) -> p o", o=1)), writes=[qg.B])
                S_.dma(lambda e, hh=hh: e.dma_start(out=kg.t[hh * 64:(hh + 1) * 64, :],
                                                    in_=k_norm_g[l].rearrange("(p o) -> p o", o=1)), writes=[kg.B])
                S_.dma(lambda e, hh=hh: e.dma_start(out=gng.t[hh * 64:(hh + 1) * 64, :],
                                                    in_=gla_norm_g[l].rearrange("(p o) -> p o", o=1)), writes=[gng.B])
            S_.dma(lambda e: e.dma_start(out=lamt.t[:].rearrange("p a b -> p (a b)"),
                                         in_=diff_lambda[l:l + 1].rearrange("o a b -> o (a b)").partition_broadcast(128)),
                   writes=[lamt.B])
            S_.dma(lambda e: e.dma_start(out=subg.t[:], in_=diff_subln_g[l].rearrange("(p o) -> p o", o=1)),
                   writes=[subg.B])
            S_.dma(lambda e: e.dma_start(out=aw.t[0:16, :], in_=gla_alpha_w[l]), writes=[aw.B])
            S_.dma(lambda e: e.dma_start(out=aw.t[16:17, :], in_=gla_alpha_b[l:l + 1, :]), writes=[aw.B])
            S_.dve(lambda e: e.tensor_copy(out=aw_hi.t[:], in_=aw.t[:]), reads=[aw.B], writes=[aw_hi.B])
            S_.dve(lambda e: e.tensor_tensor(out=aw_lo.t[:], in0=aw.t[:], in1=aw_hi.t[:], op=ALU.subtract),
                   reads=[aw.B, aw_hi.B], writes=[aw_lo.B])
            lam_init = 0.8 - 0.6 * math.exp(-0.3 * l)
            S_.dve(lambda e: e.tensor_scalar_mul(out=qg.t[:], in0=qg.t[:], scalar1=0.125), reads=[qg.B], writes=[qg.B])
            S_.dve(lambda e: e.tensor_tensor(out=lamw.t[:, 0, :], in0=lamt.t[:, 0, :], in1=lamt.t[:, 1, :], op=ALU.mult),
                   reads=[lamt.B], writes=[lamw.B])
            S_.dve(lambda e: e.tensor_tensor(out=lamw.t[:, 1, :], in0=lamt.t[:, 2, :], in1=lamt.t[:, 3, :], op=ALU.mult),
                   reads=[lamt.B], writes=[lamw.B])
            S_.dve(lambda e: e.tensor_reduce(out=lams.t[:], in_=lamw.t[:], axis=AX.X, op=ALU.add),
                   reads=[lamw.B], writes=[lams.B])
            S_.act(lambda e: e.activation(out=lams.t[:], in_=lams.t[:], func=AF.Exp), reads=[lams.B], writes=[lams.B])
            S_.dve(lambda e: e.tensor_tensor(out=nlam.t[:], in0=lams.t[:, 1:2], in1=lams.t[:, 0:1], op=ALU.subtract),
                   reads=[lams.B], writes=[nlam.B])
            S_.dve(lambda e: e.tensor_scalar_add(out=nlam.t[:], in0=nlam.t[:], scalar1=-lam_init),
                   reads=[nlam.B], writes=[nlam.B])
            S_.dve(lambda e: e.tensor_scalar_mul(out=subg.t[:], in0=subg.t[:], scalar1=1.0 - lam_init),
                   reads=[subg.B], writes=[subg.B])

        AWORDS = 48700
        ARENA = st.enter_context(nc.sbuf_tensor("arena", [128, AWORDS], F32))
        WW = (8 * D + 8 * DFF + 32 * D) // 2
        WREG = ARENA[:, 0:WW].bitcast(BF16)
        BW = Buf("wreg")
        BWo, BW1lo, BW1hi, BW2 = Buf("wout"), Buf("w1lo"), Buf("w1hi"), Buf("w2")
        win_v = WREG[:, 0:8 * DINL].rearrange("p (c n) -> p c n", c=8)
        wout_v = WREG[:, 0:8 * D].rearrange("p (c n) -> p c n", c=8)
        w1_v = WREG[:, 8 * D:8 * D + 8 * DFF].rearrange("p (c n) -> p c n", c=8)
        w2_v = WREG[:, 8 * D + 8 * DFF:].rearrange("p (c n) -> p c n", c=32)

        def phase1(l, x_src, Bx_src):
            S_.fence()
            for c in range(8):
                S_.dma(lambda e, c=c: e.dma_start(out=win_v[:, c, :], in_=w_in[l, c * 128:(c + 1) * 128, :]),
                       writes=[BW], q="pool")
            a = ARENA
            o = 8 * DINL // 2

            def carve(n, dt=F32, shape=None):
                nonlocal o
                v = a[:, o:o + n]
                o += n
                return v

            xt_v = [carve(4096).rearrange("p (j n) -> p j n", j=4) for _ in range(1)]
            Bxt = [Buf("xt0")]
            hb_raw = carve(2048)
            hb_v = hb_raw.bitcast(BF16).rearrange("p (j n) -> p j n", j=4)
            Bhb = Buf("hb")
            hT_raw = carve(2048)
            hT_v = hT_raw.bitcast(BF16).rearrange("p (c n) -> p c n", c=8)
            BhT = Buf("hT")
            junk_v = carve(512).bitcast(BF16)
            Bjunk = Buf("junk")
            ss_v = carve(4)
            rs_v = carve(4)
            Bss, Brs = Buf("ss"), Buf("rs")
            u_v = carve(512)
            Bu = Buf("u")
            bg_v = carve(512)
            Bbg = Buf("bg")
            zc_v = [carve(514) for _ in range(2)]
            Bzc = [Buf("zc0"), Buf("zc1")]
            acc_v = carve(512)
            Bacc = Buf("acc")
            yc_v = carve(256).bitcast(BF16)
            Byc = Buf("yc")
            sq_v = carve(256).bitcast(BF16)
            Bsq = Buf("sq")
            sq2_v = [carve(256).bitcast(BF16) for _ in range(2)]
            Bsq2 = [Buf("sq2a"), Buf("sq2b")]
            zq_v = [carve(512) for _ in range(2)]
            Bzq = [Buf("zqa"), Buf("zqb")]
            rstd_v = carve(512)
            Brstd = Buf("rstd")
            qk_v = carve(256).bitcast(BF16)
            Bqk = Buf("qk")
            vt_v = carve(256).bitcast(BF16)[:, 0:VW]
            Bvt = Buf("vt")
            aT_v = carve(512)
            BaT = Buf("aT")
            aTh_v = carve(256).bitcast(BF16)
            aTl_v = carve(256).bitcast(BF16)
            BaTh, BaTl = Buf("aTh"), Buf("aTl")
            lah_v = carve(2 * GW).bitcast(BF16).rearrange("p (j n) -> p j n", j=4)
            lal_v = carve(2 * GW).bitcast(BF16).rearrange("p (j n) -> p j n", j=4)
            Blah, Blal = Buf("lah"), Buf("lal")
            la_v = carve(4 * GW).rearrange("p (j n) -> p j n", j=4)
            Bla = Buf("la")
            ebT_v = [carve(512) for _ in range(2)]
            enbT_v = [carve(512) for _ in range(2)]
            BebT = [Buf("ebT0"), Buf("ebT1")]
            BenbT = [Buf("enbT0"), Buf("enbT1")]
            erev_v = carve(4 * GW).rearrange("p (j n) -> p j n", j=4)
            Berev = Buf("erev")
            qinm_v = carve(512 * NP).bitcast(BF16).rearrange("p (c h n) -> p c h n", c=NP, h=2)
            kin_v = carve(256 * NP).bitcast(BF16).rearrange("p (c n) -> p c n", c=NP)
            Bqin, Bkin = [Buf("qin0"), Buf("qin1")], [Buf("kin0"), Buf("kin1")]
            erevm_v = [carve(4 * GW).rearrange("p (j n) -> p j n", j=4) for _ in range(2)]
            Berevm = [Buf("erevm0"), Buf("erevm1")]
            kendm_v = carve(4 * GW).bitcast(BF16).rearrange("p (j c n) -> p j c n", j=4, c=2)
            vtok_v = carve(2 * GW).bitcast(BF16).rearrange("p (j n) -> p j n", j=4)
            vpad_v = carve(512 * NP).bitcast(BF16).rearrange("p (j a b n) -> p j a b n", j=4, a=NP, b=2)
            Bkend, Bvtok = [Buf(f"kend{j}") for j in range(4)], [Buf(f"vtok{j}") for j in range(4)]
            Bvpad = [Buf(f"vpad{j}") for j in range(4)]
            am_v = carve(128 * NP).bitcast(BF16).rearrange("p (h n) -> p h n", h=2 * NP)
            Bam = [Buf("am0"), Buf("am1")]
            Sf_v = carve(128 * NP).rearrange("p (c n) -> p c n", c=NP)
            BSf = Buf("Sf")
            NSL = 4
            Sbf_v = carve(64 * NP * NSL).bitcast(BF16).rearrange("p (s c n) -> p s c n", s=NSL, c=NP)
            BSbf = [Buf(f"Sbf{i}") for i in range(NSL)]
            sg_v = carve(512)
            Bsg = Buf("sg")
            t1_v = carve(512)
            Bt1 = Buf("t1")
            yg_v = carve(256).bitcast(BF16)
            Byg = Buf("yg")
            assert o <= AWORDS, o

            S_.dve(lambda e: e.memset(aT_v[:, :], 1.0), writes=[BaT])
            S_.dve(lambda e: e.memset(aTh_v[:, :], 1.0), writes=[BaTh])
            S_.dve(lambda e: e.memset(aTl_v[:, :], 0.0), writes=[BaTl])
            S_.dve(lambda e: e.memset(Sf_v[:, :, :], 0.0), writes=[BSf])
            S_.pool(lambda e: e.memset(vpad_v[:, :, :, :, :], 0.0), writes=Bvpad)
            for i in range(2):
                S_.dve(lambda e, i=i: e.memset(zc_v[i][:, 0:2], 0.0), writes=[Bzc[i]])

            rr = [0]

            def nb():
                k = rr[0] % 4
                rr[0] += 1
                return k

            cp = [0]

            def copy_rr(out, in_, reads, writes):
                cp[0] += 1
                if cp[0] % 2 == 0:
                    S_.act(lambda e: e.copy(out=out, in_=in_), reads=reads, writes=writes)
                else:
                    S_.dve(lambda e: e.tensor_copy(out=out, in_=in_), reads=reads, writes=writes)

            def fm_group(col0, ncols, bk, t):
                for c in range(8):
                    S_.pe(lambda e, c=c: e.matmul(bank(bk)[0:ncols, :], lhsT=win_v[:, c, col0:col0 + ncols],
                                                  rhs=hT_v[:, c, :], start=(c == 0), stop=(c == 7)),
                          reads=[BW, BhT], writes=[bankB[bk]])

            chunk_ctr = [0]

            def sec_norm(tn):
                tok0 = tn * 512
                xt = xt_v[0]
                bxt = Bxt[0]
                S_.dma(lambda e, xt=xt, tok0=tok0: e.dma_start(
                    out=xt, in_=x_src[tok0:tok0 + 512, :].rearrange("(j p) n -> p j n", p=128)),
                    reads=[Bx_src], writes=[bxt])
                for j in range(4):
                    S_.act(lambda e, j=j, xt=xt: e.activation(out=junk_v[:, :], in_=xt[:, j, :], func=AF.Square,
                                                              accum_out=ss_v[:, j:j + 1]),
                           reads=[bxt], writes=[Bjunk, Bss])
                S_.act(lambda e: e.activation(out=rs_v[:, :], in_=ss_v[:, :], func=AF.Ln, bias=epsc.t[:, 0:1], scale=1.0 / D),
                       reads=[Bss, epsc.B], writes=[Brs])
                S_.act(lambda e: e.activation(out=rs_v[:, :], in_=rs_v[:, :], func=AF.Exp, scale=-0.5),
                       reads=[Brs], writes=[Brs])
                for j in range(4):
                    S_.dve(lambda e, j=j, xt=xt: e.scalar_tensor_tensor(out=hb_v[:, j, :], in0=xt[:, j, :],
                                                                        scalar=rs_v[:, j:j + 1], in1=gb1.t[:, :],
                                                                        op0=ALU.mult, op1=ALU.mult),
                           reads=[bxt, Brs, gb1.B], writes=[Bhb])


            sec_norm(0)
            for t in range(NT):
                S_.mark(f"p1_tile{t}")
                tok0 = t * 512
                S_.mark(f"p1_t{t}_transposes")
                for c2 in range(4):
                    for cc_ in range(2):
                        c = c2 * 2 + cc_
                        for j in range(4):
                            S_.pe(lambda e, c=c, j=j, cc_=cc_: e.transpose(
                                out=PSb[:, cc_ * 512 + j * 128: cc_ * 512 + (j + 1) * 128],
                                in_=hb_v[:, j, c * 128:(c + 1) * 128], identity=ident.t[:, :]),
                                reads=[Bhb, ident.B], writes=[bankB[7]])
                    copy_rr(hT_v[:, c2 * 2:c2 * 2 + 2, :], PSb[:, :].rearrange("p (c n) -> p c n", c=2), [bankB[7]], [BhT])

                def sec_conv():
                    S_.mark(f"p1_t{t}_conv")
                    for cc in range(NCV):
                        bu = nb()
                        fm_group(C_U + cc * 128, 128, bu, t)
                        S_.act(lambda e, bu=bu: e.copy(out=u_v[:, :], in_=bank(bu)), reads=[bankB[bu]], writes=[Bu])
                        bc = nb()
                        fm_group(C_CC + cc * 128, 128, bc, t)
                        S_.dve(lambda e, bc=bc, cc=cc: e.tensor_tensor(out=zc_v[cc][:, 2:514], in0=bank(bc), in1=u_v[:, :],
                                                                       op=ALU.mult),
                               reads=[bankB[bc], Bu], writes=[Bzc[cc]])
                        S_.dve(lambda e, cc=cc: e.tensor_scalar_mul(out=acc_v[:, :], in0=zc_v[cc][:, 2:514],
                                                                    scalar1=cw.t[:, cc, 2:3]),
                               reads=[Bzc[cc], cw.B], writes=[Bacc])
                        S_.dve(lambda e, cc=cc: e.scalar_tensor_tensor(out=acc_v[:, :], in0=zc_v[cc][:, 1:513],
                                                                       scalar=cw.t[:, cc, 1:2], in1=acc_v[:, :],
                                                                       op0=ALU.mult, op1=ALU.add),
                               reads=[Bzc[cc], cw.B, Bacc], writes=[Bacc])
                        S_.dve(lambda e, cc=cc: e.scalar_tensor_tensor(out=acc_v[:, :], in0=zc_v[cc][:, 0:512],
                                                                       scalar=cw.t[:, cc, 0:1], in1=acc_v[:, :],
                                                                       op0=ALU.mult, op1=ALU.add),
                               reads=[Bzc[cc], cw.B, Bacc], writes=[Bacc])
                        bb = nb()
                        fm_group(C_CB + cc * 128, 128, bb, t)
                        S_.act(lambda e, bb=bb: e.copy(out=bg_v[:, :], in_=bank(bb)), reads=[bankB[bb]], writes=[Bbg])
                        S_.dve(lambda e: e.tensor_tensor(out=yc_v[:, :], in0=bg_v[:, :], in1=acc_v[:, :], op=ALU.mult),
                               reads=[Bbg, Bacc], writes=[Byc])
                        S_.dma(lambda e, cc=cc, tok0=tok0: e.dma_start(out=yT_d[cc, :, tok0:tok0 + 512], in_=yc_v[:, :]),
                               reads=[Byc], writes=[ByT])
                        S_.dve(lambda e, cc=cc: e.tensor_copy(out=zc_v[cc][:, 0:2], in_=zc_v[cc][:, 512:514]),
                               reads=[Bzc[cc]], writes=[Bzc[cc]])


                def sec_qk():
                    S_.mark(f"p1_t{t}_qk")
                    groups_ = [(which, h) for which in range(2) for h in range(NH)]
                    pend = None
                    for gi, g in enumerate(groups_ + [None]):
                        if g is not None:
                            which, h = g
                            col0 = (C_Q if which == 0 else C_K) + h * 128
                            bz = nb()
                            fm_group(col0, 128, bz, t)
                            sqi = gi % 2
                            S_.dve(lambda e, bz=bz, sqi=sqi: e.tensor_copy(out=zq_v[sqi][:, :], in_=bank(bz)),
                                   reads=[bankB[bz]], writes=[Bzq[sqi]])
                            S_.act(lambda e, sqi=sqi: e.activation(out=sq2_v[sqi][:, :], in_=zq_v[sqi][:, :], func=AF.Square),
                                   reads=[Bzq[sqi]], writes=[Bsq2[sqi]])
                        if pend is not None:
                            which_p, h_p, bz_p, sqi_p = pend
                            bm = nb()
                            S_.pe(lambda e, bm=bm, sqi_p=sqi_p: e.matmul(bank(bm), lhsT=blk64.t[:, :], rhs=sq2_v[sqi_p][:, :],
                                                                         start=True, stop=True),
                                  reads=[Bsq2[sqi_p], blk64.B], writes=[bankB[bm]])
                            S_.act(lambda e, bm=bm: e.activation(out=rstd_v[:, :], in_=bank(bm), func=AF.Ln, bias=epsc.t[:, 0:1]),
                                   reads=[bankB[bm], epsc.B], writes=[Brstd])
                            S_.act(lambda e: e.activation(out=rstd_v[:, :], in_=rstd_v[:, :], func=AF.Exp, scale=-0.5),
                                   reads=[Brstd], writes=[Brstd])
                            gcol = qg if which_p == 0 else kg
                            S_.dve(lambda e, sqi_p=sqi_p, gcol=gcol: e.scalar_tensor_tensor(
                                out=qk_v[:, :], in0=zq_v[sqi_p][:, :], scalar=gcol.t[:, 0:1], in1=rstd_v[:, :],
                                op0=ALU.mult, op1=ALU.mult),
                                reads=[Bzq[sqi_p], gcol.B, Brstd], writes=[Bqk])
                            dst = qT_d if which_p == 0 else kT_d
                            Bdst = BqT if which_p == 0 else BkT
                            S_.dma(lambda e, dst=dst, h_p=h_p, tok0=tok0: e.dma_start(out=dst[h_p, :, tok0:tok0 + 512], in_=qk_v[:, :]),
                                   reads=[Bqk], writes=[Bdst])
                        pend = (g[0], g[1], bz, gi % 2) if g is not None else None

                def sec_v():
                    S_.mark(f"p1_t{t}_v")
                    for j in range(4):
                        bv = nb()
                        for c in range(8):
                            S_.pe(lambda e, c=c, j=j, bv=bv: e.matmul(bank(bv)[:, 0:VW], lhsT=hT_v[:, c, j * 128:(j + 1) * 128],
                                                                      rhs=win_v[:, c, C_V:C_V + VW], start=(c == 0), stop=(c == 7)),
                                  reads=[BW, BhT], writes=[bankB[bv]])
                        copy_rr(vt_v[:, :], bank(bv)[:, 0:VW], [bankB[bv]], [Bvt])
                        S_.dma(lambda e, j=j, tok0=tok0: e.dma_start(out=v_d[tok0 + j * 128: tok0 + (j + 1) * 128, :], in_=vt_v[:, :]),
                               reads=[Bvt], writes=[Bv])


                def sec_gla():
                    S_.mark(f"p1_t{t}_gla")
                    ba = nb()
                    fm_group(C_GA, 16, ba, t)
                    S_.act(lambda e, ba=ba: e.copy(out=aT_v[0:16, :], in_=bank(ba)[0:16, :]), reads=[bankB[ba]], writes=[BaT])
                    S_.dve(lambda e: e.tensor_copy(out=aTh_v[0:16, :], in_=aT_v[0:16, :]), reads=[BaT], writes=[BaTh])
                    S_.dve(lambda e: e.tensor_tensor(out=aTl_v[0:16, :], in0=aT_v[0:16, :], in1=aTh_v[0:16, :], op=ALU.subtract),
                           reads=[BaT, BaTh], writes=[BaTl])
                    for j in range(4):
                        bk = 4 + j // BPB
                        passes = [(aTh_v, BaTh, aw_hi), (aTl_v, BaTl, aw_hi), (aTh_v, BaTh, aw_lo)]
                        for pi, (av, aB, wv) in enumerate(passes):
                            S_.pe(lambda e, j=j, bk=bk, av=av, wv=wv, pi=pi: e.matmul(
                                bank(bk)[:, (j % BPB) * GW:(j % BPB + 1) * GW],
                                lhsT=av[0:17, j * 128:(j + 1) * 128], rhs=wv.t[0:17, :],
                                start=(pi == 0), stop=(pi == 2)),
                                reads=[aB, wv.B], writes=[bankB[bk]])
                    for half in range(4 // BPB):
                        S_.act(lambda e, half=half: e.activation(out=la_v[:, BPB * half:BPB * half + BPB, :].rearrange("p j n -> p (j n)"),
                                                                 in_=bank(4 + half), func=AF.Exp, scale=-1.0),
                               reads=[bankB[4 + half]], writes=[Bla])
                    S_.act(lambda e: e.activation(out=la_v[:, :, :].rearrange("p j n -> p (j n)"),
                                                  in_=la_v[:, :, :].rearrange("p j n -> p (j n)"), func=AF.Ln, bias=onec.t[:, 0:1]),
                           reads=[Bla, onec.B], writes=[Bla])
                    S_.dve(lambda e: e.tensor_copy(out=lah_v[:, :, :], in_=la_v[:, :, :]), reads=[Bla], writes=[Blah])
                    S_.dve(lambda e: e.tensor_tensor(out=lal_v[:, :, :], in0=la_v[:, :, :], in1=lah_v[:, :, :], op=ALU.subtract),
                           reads=[Bla, Blah], writes=[Blal])

                def sec_gla_cumsum():
                    S_.mark(f"p1_t{t}_gla_cumsum")
                    for p in range(NP):
                        bk = 4 + p
                        for j in range(4):
                            for pi, (lv, lB) in enumerate([(lah_v, Blah), (lal_v, Blal)]):
                                S_.pe(lambda e, p=p, j=j, bk=bk, lv=lv, pi=pi: e.matmul(
                                    bank(bk)[:, j * 128:(j + 1) * 128],
                                    lhsT=lv[:, j, p * 128:(p + 1) * 128], rhs=triu_b.t[:, :],
                                    start=(pi == 0), stop=(pi == 1)),
                                    reads=[lB, triu_b.B], writes=[bankB[bk]])
                        S_.act(lambda e, p=p, bk=bk: e.activation(out=ebT_v[p][:, :], in_=bank(bk), func=AF.Exp, scale=-1.0 / 16),
                               reads=[bankB[bk]], writes=[BebT[p]])
                        S_.act(lambda e, p=p, bk=bk: e.activation(out=enbT_v[p][:, :], in_=bank(bk), func=AF.Exp, scale=1.0 / 16),
                               reads=[bankB[bk]], writes=[BenbT[p]])
                    for j in range(4):
                        bk = 4 + j // BPB
                        for pi, (lv, lB) in enumerate([(lah_v, Blah), (lal_v, Blal)]):
                            S_.pe(lambda e, j=j, bk=bk, lv=lv, pi=pi: e.matmul(
                                bank(bk)[:, (j % BPB) * GW:(j % BPB + 1) * GW],
                                lhsT=strl_b.t[:, :], rhs=lv[:, j, :], start=(pi == 0), stop=(pi == 1)),
                                reads=[lB, strl_b.B], writes=[bankB[bk]])
                    for half in range(4 // BPB):
                        S_.act(lambda e, half=half: e.activation(out=erev_v[:, BPB * half:BPB * half + BPB, :].rearrange("p j n -> p (j n)"),
                                                                 in_=bank(4 + half), func=AF.Exp, scale=-1.0 / 16),
                               reads=[bankB[4 + half]], writes=[Berev])

                def sec_gla_qin():
                    S_.mark(f"p1_t{t}_gla_qin")
                    for p in range(NP):
                        bq = nb()
                        fm_group(C_GQ + p * 128, 128, bq, t)
                        for hl in range(2):
                            S_.dve(lambda e, p=p, hl=hl, bq=bq: e.scalar_tensor_tensor(
                                out=qinm_v[:, p, hl, :], in0=bank(bq), scalar=hmask.t[:, hl:hl + 1],
                                in1=ebT_v[p][:, :], op0=ALU.mult, op1=ALU.mult),
                                reads=[bankB[bq], BebT[p], hmask.B], writes=[Bqin[p]])
                        bkk = nb()
                        fm_group(C_GK + p * 128, 128, bkk, t)
                        S_.dve(lambda e, p=p, bkk=bkk: e.tensor_tensor(out=kin_v[:, p, :], in0=bank(bkk), in1=enbT_v[p][:, :],
                                                                       op=ALU.mult),
                               reads=[bankB[bkk], BenbT[p]], writes=[Bkin[p]])
                    for j in range(4):
                        bv = nb()
                        for c in range(8):
                            S_.pe(lambda e, c=c, j=j, bv=bv: e.matmul(bank(bv)[:, 0:2 * GW], lhsT=hT_v[:, c, j * 128:(j + 1) * 128],
                                                                      rhs=win_v[:, c, C_GK:C_GK + 2 * GW], start=(c == 0), stop=(c == 7)),
                                  reads=[BW, BhT], writes=[bankB[bv]])
                        for cc in range(2):
                            S_.dve(lambda e, j=j, bv=bv, cc=cc: e.scalar_tensor_tensor(
                                out=kendm_v[:, j, cc, :], in0=bank(bv)[:, 0:GW], scalar=cmask.t[:, cc:cc + 1],
                                in1=erev_v[:, j, :], op0=ALU.mult, op1=ALU.mult),
                                reads=[bankB[bv], Berev, cmask.B], writes=[Bkend[j]])
                        S_.dve(lambda e, j=j, bv=bv: e.tensor_copy(out=vtok_v[:, j, :], in_=bank(bv)[:, GW:2 * GW]),
                               reads=[bankB[bv]], writes=[Bvtok[j]])
                        for hl in range(2):
                            S_.pool(lambda e, j=j, hl=hl: e.tensor_copy(
                                out=vpad_v[:, j, :, hl, hl * 64:(hl + 1) * 64],
                                in_=vtok_v[:, j, :].rearrange("p (a b e) -> p a b e", a=NP, b=2)[:, :, hl, :]),
                                reads=[Bvtok[j]], writes=[Bvpad[j]])

                def sec_gla_blocks():
                    S_.mark(f"p1_t{t}_gla_blocks")
                    for j in range(4):
                        bas = [nb() for _ in range(NP)]
                        for h in range(2 * NP):
                            p, hl = h // 2, h % 2
                            S_.pe(lambda e, p=p, hl=hl, j=j, bas=bas: e.matmul(
                                bank(bas[p])[:, hl * 128:(hl + 1) * 128],
                                lhsT=kin_v[:, p, j * 128:(j + 1) * 128],
                                rhs=qinm_v[:, p, hl, j * 128:(j + 1) * 128], start=True, stop=True),
                                reads=[Bkin[p], Bqin[p]], writes=[bankB[bas[p]]])
                        for p in range(NP):
                            S_.dve(lambda e, p=p, bas=bas: e.tensor_tensor(
                                out=am_v[:, 2 * p:2 * p + 2, :], in0=bank(bas[p])[:, 0:256].rearrange("p (h n) -> p h n", h=2),
                                in1=triu.t[:, :].unsqueeze(1).broadcast_to([128, 2, 128]), op=ALU.mult),
                                reads=[bankB[bas[p]], triu.B], writes=[Bam[p]])
                        for p in range(NP):
                            ob = bank(4 + p)[:, j * 128:(j + 1) * 128]
                            for hl in range(2):
                                S_.pe(lambda e, p=p, hl=hl, j=j, ob=ob: e.matmul(
                                    ob, lhsT=vpad_v[:, j, p, hl, :], rhs=am_v[:, 2 * p + hl, :], start=(hl == 0), stop=False),
                                    reads=[Bvpad[j], Bam[p]], writes=[bankB[4 + p]])
                        for cc in range(2):
                            ch = chunk_ctr[0]
                            chunk_ctr[0] += 1
                            sl = ch % NSL
                            S_.pool(lambda e, sl=sl: e.tensor_copy(out=Sbf_v[:, sl, :, :], in_=Sf_v[:, :, :]),
                                    reads=[BSf], writes=[BSbf[sl]])
                            for p in range(NP):
                                obc = bank(4 + p)[:, j * 128 + cc * 64: j * 128 + (cc + 1) * 64]
                                for hl in range(2):
                                    S_.pe(lambda e, p=p, hl=hl, j=j, cc=cc, sl=sl, obc=obc: e.matmul(
                                        obc, lhsT=Sbf_v[:, sl, p, :],
                                        rhs=qinm_v[:, p, hl, j * 128 + cc * 64: j * 128 + (cc + 1) * 64],
                                        start=False, stop=(cc == 1 and hl == 1)),
                                        reads=[BSbf[sl], Bqin[p]], writes=[bankB[4 + p]])
                            for p in range(NP):
                                S_.pe(lambda e, p=p, j=j, cc=cc: e.matmul(
                                    bank(6)[:, p * 128:(p + 1) * 128],
                                    lhsT=kendm_v[:, j, cc, p * 128:(p + 1) * 128],
                                    rhs=vtok_v[:, j, p * 128:(p + 1) * 128], start=True, stop=True),
                                    reads=[Bkend[j], Bvtok[j]], writes=[bankB[6]])
                            col = j * 128 + cc * 64 + 63
                            for p in range(NP):
                                for hl in range(2):
                                    S_.dve(lambda e, p=p, hl=hl, col=col: e.scalar_tensor_tensor(
                                        out=Sf_v[hl * 64:(hl + 1) * 64, p, hl * 64:(hl + 1) * 64],
                                        in0=Sf_v[hl * 64:(hl + 1) * 64, p, hl * 64:(hl + 1) * 64],
                                        scalar=ebT_v[p][hl * 64:(hl + 1) * 64, col:col + 1],
                                        in1=bank(6)[hl * 64:(hl + 1) * 64, p * 128 + hl * 64: p * 128 + (hl + 1) * 64],
                                        op0=ALU.mult, op1=ALU.add),
                                        reads=[BSf, BebT[p], bankB[6]], writes=[BSf])

                def sec_gla_out():
                    S_.mark(f"p1_t{t}_gla_out")
                    for p in range(NP):
                        bo = 4 + p
                        S_.act(lambda e, bo=bo: e.activation(out=sq_v[:, :], in_=bank(bo), func=AF.Square),
                               reads=[bankB[bo]], writes=[Bsq])
                        bm = nb()
                        S_.pe(lambda e, bm=bm: e.matmul(bank(bm), lhsT=blk64.t[:, :], rhs=sq_v[:, :], start=True, stop=True),
                              reads=[Bsq, blk64.B], writes=[bankB[bm]])
                        S_.act(lambda e, bm=bm: e.activation(out=rstd_v[:, :], in_=bank(bm), func=AF.Ln, bias=epsc.t[:, 0:1]),
                               reads=[bankB[bm], epsc.B], writes=[Brstd])
                        S_.act(lambda e: e.activation(out=rstd_v[:, :], in_=rstd_v[:, :], func=AF.Exp, scale=-0.5),
                               reads=[Brstd], writes=[Brstd])
                        S_.dve(lambda e, bo=bo: e.scalar_tensor_tensor(out=t1_v[:, :], in0=bank(bo), scalar=gng.t[:, 0:1],
                                                                       in1=rstd_v[:, :], op0=ALU.mult, op1=ALU.mult),
                               reads=[bankB[bo], gng.B, Brstd], writes=[Bt1])
                        bg = nb()
                        fm_group(C_GG + p * 128, 128, bg, t)
                        S_.act(lambda e, bg=bg: e.activation(out=sg_v[:, :], in_=bank(bg), func=AF.Silu),
                               reads=[bankB[bg]], writes=[Bsg])
                        S_.dve(lambda e: e.tensor_tensor(out=yg_v[:, :], in0=t1_v[:, :], in1=sg_v[:, :], op=ALU.mult),
                               reads=[Bt1, Bsg], writes=[Byg])
                        S_.dma(lambda e, p=p, tok0=tok0: e.dma_start(out=yT_d[NCV + NH + p, :, tok0:tok0 + 512], in_=yg_v[:, :]),
                               reads=[Byg], writes=[ByT])

                sec_gla()
                sec_conv()
                sec_gla_cumsum()
                sec_qk()
                if t + 1 < NT:
                    sec_norm(t + 1)
                sec_gla_qin()
                sec_v()
                sec_gla_blocks()
                sec_gla_out()


        def phase2(l):
            S_.mark(f"p2_{l}")
            S_.fence()
            a = ARENA
            o = 0

            def carve(n):
                nonlocal o
                v = a[:, o:o + n]
                o += n
                return v
            KT_v = carve(S // 2).bitcast(BF16)
            QT_v = carve(S).bitcast(BF16).rearrange("p (c n) -> p c n", c=2)
            V_v = carve(S // 2).bitcast(BF16).rearrange("p (b e) -> p b e", e=128)
            BKT, BQT, BV = Buf("KT"), Buf("QT"), Buf("V")
            prefetch_w = (o <= 16384) and not pair
            if prefetch_w:
                o = WW
                for c in range(6, 8):
                    S_.dma(lambda e, c=c: e.dma_start(out=w1_v[:, c, :], in_=w_mlp1[l, c * 128:(c + 1) * 128, :]),
                           writes=[BW1hi], q="pool")
                for c in range(32):
                    S_.dma(lambda e, c=c: e.dma_start(out=w2_v[:, c, :], in_=w_mlp2[l, c * 128:(c + 1) * 128, :]),
                           writes=[BW2], q="pool")
            PT_v = [carve(512).bitcast(BF16).rearrange("p (c n) -> p c n", c=2) for _ in range(2)]
            BPT = [Buf("PT0"), Buf("PT1")]
            r_v = carve(1024).rearrange("p (c n) -> p c n", c=2)
            Br = Buf("r")
            Oc_v = [carve(1024).rearrange("p (c n) -> p c n", c=2) for _ in range(2)]
            BOc = [Buf("Oc0"), Buf("Oc1")]
            sums_v = [carve(1024).rearrange("p (c n) -> p c n", c=2) for _ in range(2)]
            Bsums = [Buf("sums0"), Buf("sums1")]
            o_v = [carve(512) for _ in range(2)]
            Bo = [Buf("o0"), Buf("o1")]
            tile_ctr = [0]
            cur_it = [0]
            sq_v = carve(256).bitcast(BF16)
            Bsq = Buf("sq2")
            rstd_v = carve(512)
            Brstd = Buf("rstd2")
            y_v = carve(256).bitcast(BF16)
            By = Buf("y2")
            assert o <= AWORDS, o
            S_.dve(lambda e: e.memset(QT_v[64:128, 0, :], 0.0), writes=[BQT])
            S_.dve(lambda e: e.memset(QT_v[0:64, 1, :], 0.0), writes=[BQT])
            for h in range(NH):
                S_.dma(lambda e, h=h: e.dma_start(out=KT_v[:, :], in_=kT_d[h, :, :]), reads=[BkT], writes=[BKT])
                for comp in range(2):
                    S_.dma(lambda e, h=h, comp=comp: e.dma_start(out=QT_v[comp * 64:(comp + 1) * 64, comp, :],
                                                                 in_=qT_d[h, comp * 64:(comp + 1) * 64, :]),
                           reads=[BqT], writes=[BQT])
                S_.dma(lambda e, h=h: e.dma_start(out=V_v[:, :, :],
                                                  in_=v_d[:, h * 128:(h + 1) * 128].rearrange("(b p) e -> p b e", p=128)),
                       reads=[Bv], writes=[BV])
                THR = 50.0
                its = []
                for qt in range(NT):
                    q0 = qt * 512
                    kbs = []
                    for kb in range(4 * qt + 4):
                        dmin = q0 - (kb * 128 + 127)
                        if (not pair) and kb < 4 * qt and slopes[h] * dmin > THR:
                            continue
                        kbs.append(kb)
                    for n_, kb in enumerate(kbs):
                        its.append(dict(qt=qt, q0=q0, kb=kb, first=(n_ == 0), last=(n_ == len(kbs) - 1), idx=len(its)))

                def emit_qk(itx, h=h):
                    kb, qt, q0 = itx["kb"], itx["qt"], itx["q0"]
                    j = kb - 4 * qt
                    c0 = 0 if j < 0 else j * 128
                    sbk = (itx["idx"] % 2) * 2
                    diag = j >= 0
                    for comp in range(2):
                        S_.pe(lambda e, comp=comp: e.matmul(
                            bank(sbk + comp)[:, c0:512], lhsT=KT_v[:, kb * 128:(kb + 1) * 128],
                            rhs=QT_v[:, comp, q0 + c0:q0 + 512], start=True, stop=(not diag)),
                            reads=[BKT, BQT], writes=[bankB[sbk + comp]])
                        if diag:
                            S_.pe(lambda e, comp=comp: e.matmul(
                                bank(sbk + comp)[:, c0:c0 + 128], lhsT=ident.t[:, :], rhs=tdiag.t[:, h, :],
                                start=False, stop=True),
                                reads=[ident.B, tdiag.B], writes=[bankB[sbk + comp]])

                def emit_exp_pv(itx, h=h):
                    kb, qt, q0 = itx["kb"], itx["qt"], itx["q0"]
                    j = kb - 4 * qt
                    c0 = 0 if j < 0 else j * 128
                    sbk = (itx["idx"] % 2) * 2
                    pt = itx["idx"] % 2
                    di = (kb - 4 * qt) + 4 * (NT - 1)
                    first, last = itx["first"], itx["last"]
                    S_.act(lambda e: e.activation(
                        out=PT_v[pt][:, :, c0:512],
                        in_=PSf[:, sbk * 512:(sbk + 2) * 512].rearrange("p (c n) -> p c n", c=2)[:, :, c0:512],
                        func=AF.Exp, bias=kbias.t[:, h, di:di + 1], scale=1.0),
                        reads=[bankB[sbk], bankB[sbk + 1], kbias.B], writes=[BPT[pt]])
                    for comp in range(2):
                        S_.pe(lambda e, comp=comp: e.matmul(
                            bank(4 + comp)[:, c0:512], lhsT=V_v[:, kb, :], rhs=PT_v[pt][:, comp, c0:512],
                            start=first, stop=last),
                            reads=[BV, BPT[pt]], writes=[bankB[4 + comp]])
                        S_.pe(lambda e, comp=comp: e.matmul(
                            bank(6 + comp)[:, c0:512], lhsT=ones_bf.t[:, :], rhs=PT_v[pt][:, comp, c0:512],
                            start=first, stop=last),
                            reads=[ones_bf.B, BPT[pt]], writes=[bankB[6 + comp]])

                def stage_a(itx, h=h):
                    ab = itx["tile"] % 2
                    S_.dve(lambda e: e.tensor_copy(out=Oc_v[ab][:, :, :],
                                                   in_=PSf[:, 4 * 512:6 * 512].rearrange("p (c n) -> p c n", c=2)),
                           reads=[bankB[4], bankB[5]], writes=[BOc[ab]])
                    S_.dve(lambda e: e.tensor_copy(out=sums_v[ab][:, :, :],
                                                   in_=PSf[:, 6 * 512:8 * 512].rearrange("p (c n) -> p c n", c=2)),
                           reads=[bankB[6], bankB[7]], writes=[Bsums[ab]])

                def stage_b(itx, h=h):
                    ab = itx["tile"] % 2
                    S_.dve(lambda e: e.reciprocal(out=r_v[:, :, :], in_=sums_v[ab][:, :, :]), reads=[Bsums[ab]], writes=[Br])
                    S_.dve(lambda e: e.tensor_tensor(out=Oc_v[ab][:, :, :], in0=Oc_v[ab][:, :, :], in1=r_v[:, :, :], op=ALU.mult),
                           reads=[BOc[ab], Br], writes=[BOc[ab]])
                    S_.dve(lambda e: e.scalar_tensor_tensor(out=o_v[ab][:, :], in0=Oc_v[ab][:, 1, :], scalar=nlam.t[:, 0:1],
                                                            in1=Oc_v[ab][:, 0, :], op0=ALU.mult, op1=ALU.add),
                           reads=[BOc[ab], nlam.B], writes=[Bo[ab]])
                    S_.dve(lambda e: e.tensor_tensor(out=sq_v[:, :], in0=o_v[ab][:, :], in1=o_v[ab][:, :], op=ALU.mult),
                           reads=[Bo[ab]], writes=[Bsq])

                def stage_c(itx, h=h):
                    ab = itx["tile"] % 2
                    q0 = itx["q0"]
                    sbk = (cur_it[0] % 2) * 2
                    S_.pe(lambda e: e.matmul(bank(sbk), lhsT=o128.t[:, :], rhs=sq_v[:, :], start=True, stop=True),
                          reads=[o128.B, Bsq], writes=[bankB[sbk]])
                    S_.act(lambda e: e.activation(out=rstd_v[:, :], in_=bank(sbk), func=AF.Ln, bias=epsc.t[:, 0:1]),
                           reads=[bankB[sbk], epsc.B], writes=[Brstd])
                    S_.act(lambda e: e.activation(out=rstd_v[:, :], in_=rstd_v[:, :], func=AF.Exp, scale=-0.5),
                           reads=[Brstd], writes=[Brstd])
                    S_.dve(lambda e: e.scalar_tensor_tensor(out=y_v[:, :], in0=o_v[ab][:, :], scalar=subg.t[:, 0:1], in1=rstd_v[:, :],
                                                            op0=ALU.mult, op1=ALU.mult),
                           reads=[Bo[ab], subg.B, Brstd], writes=[By])
                    S_.dma(lambda e: e.dma_start(out=yT_d[NCV + h, :, q0:q0 + 512], in_=y_v[:, :]),
                           reads=[By], writes=[ByT])

                for itx in its:
                    if itx["first"]:
                        tile_ctr[0] += 1
                    itx["tile"] = tile_ctr[0]
                pending = []

                def tick():
                    for p_ in pending:
                        p_[0] -= 1
                    while pending and pending[0][0] <= 0:
                        _, fn_, it_ = pending.pop(0)
                        fn_(it_)

                emit_qk(its[0])
                for i_, itx in enumerate(its):
                    if i_ + 1 < len(its):
                        emit_qk(its[i_ + 1])
                    emit_exp_pv(itx)
                    cur_it[0] = itx["idx"]
                    tick()
                    if itx["last"]:
                        stage_a(itx)
                        pending.append([1, stage_b, itx])
                        pending.append([5, stage_c, itx])
                while pending:
                    _, fn_, it_ = pending.pop(0)
                    fn_(it_)

        def phase3(l, x_src, Bx_src, x_dst, Bx_dst, is_out):
            TT = 256
            NJ = TT // 128
            S_.mark(f"p3_{l}")
            S_.fence()
            S_.dma(lambda e: e.dma_start(out=gb2.t[:], in_=ln2_g[l:l + 1, :].partition_broadcast(128)), writes=[gb2.B])
            pre = (S // 2 + S + S // 2 <= 16384) and not pair and stop_after is None
            for c in range(8):
                S_.dma(lambda e, c=c: e.dma_start(out=wout_v[:, c, :], in_=w_out[l, c * 128:(c + 1) * 128, :]),
                       writes=[BWo], q="pool")
            for c in range(8):
                if pre and c >= 6:
                    continue
                S_.dma(lambda e, c=c: e.dma_start(out=w1_v[:, c, :], in_=w_mlp1[l, c * 128:(c + 1) * 128, :]),
                       writes=[BW1lo if c < 6 else BW1hi], q="pool")
            if not pre:
                for c in range(32):
                    S_.dma(lambda e, c=c: e.dma_start(out=w2_v[:, c, :], in_=w_mlp2[l, c * 128:(c + 1) * 128, :]),
                           writes=[BW2], q="pool")
            a = ARENA
            o = WW

            def carve(n):
                nonlocal o
                v = a[:, o:o + n]
                o += n
                return v
            yT_v = [carve(8 * TT // 2).bitcast(BF16).rearrange("p (c n) -> p c n", c=8)] * 2
            ByT_s = [Buf("yTs0")] * 2
            if pair:
                yTb_v = carve(8 * TT // 2).bitcast(BF16).rearrange("p (c n) -> p c n", c=8)
                ByTb = Buf("yTb")
            x_v = [carve(NJ * D).rearrange("p (j n) -> p j n", j=NJ) for _ in range(2)]
            Bx_s = [Buf("xs0"), Buf("xs1")]
            hb_v = carve(NJ * D // 2).bitcast(BF16).rearrange("p (j n) -> p j n", j=NJ)
            Bhb = Buf("hb3")
            hT_v = carve(8 * TT // 2).bitcast(BF16).rearrange("p (c n) -> p c n", c=8)
            BhT = Buf("hT3")
            aT_v = carve(32 * TT // 2).bitcast(BF16).rearrange("p (f n) -> p f n", f=32)
            BaT = [Buf(f"aT{f}") for f in range(32)]
            ss_v = carve(NJ)
            rs_v = carve(NJ)
            Bss, Brs = Buf("ss3"), Buf("rs3")
            relu_v = [carve(TT) for _ in range(2)]
            Brelu = [Buf("relu0"), Buf("relu1")]
            assert o <= AWORDS, o
            rr = [0]

            def nb():
                k = rr[0] % 7
                rr[0] += 1
                return k
            cp = [0]
            NTT = SP // TT

            def st_load(t):
                tok0 = t * TT
                s = t % 2
                if pair:
                    S_.dma(lambda e: e.dma_start(out=yT_v[s][:, :, :],
                                                 in_=yTall_d[:, :, tok0:tok0 + TT].rearrange("c p n -> p c n")),
                           reads=[ByTall], writes=[ByT_s[s]])
                    S_.dma(lambda e: e.dma_start(out=yTb_v[:, :, :],
                                                 in_=yTall_d[:, :, SP + tok0:SP + tok0 + TT].rearrange("c p n -> p c n")),
                           reads=[ByTall], writes=[ByTb])
                    S_.dve(lambda e: e.tensor_scalar_mul(out=yT_v[s][:, :, :], in0=yT_v[s][:, :, :], scalar1=rmask.t[:, 0:1]),
                           reads=[ByT_s[s], rmask.B], writes=[ByT_s[s]])
                    S_.dve(lambda e: e.scalar_tensor_tensor(out=yT_v[s][:, :, :], in0=yTb_v[:, :, :], scalar=rmask.t[:, 1:2],
                                                            in1=yT_v[s][:, :, :], op0=ALU.mult, op1=ALU.add),
                           reads=[ByT_s[s], ByTb, rmask.B], writes=[ByT_s[s]])
                else:
                    S_.dma(lambda e: e.dma_start(out=yT_v[s][:, :, :],
                                                 in_=yT_d[:, :, tok0:tok0 + TT].rearrange("c p n -> p c n")),
                           reads=[ByT], writes=[ByT_s[s]])
                S_.dma(lambda e: e.dma_start(out=x_v[s][:, :, :],
                                             in_=x_src[tok0:tok0 + TT, :].rearrange("(j p) n -> p j n", p=128)),
                       reads=[Bx_src], writes=[Bx_s[s]])

            def st_outproj(t):
                s = t % 2
                for j in range(NJ):
                    for n in range(2):
                        bk = nb()
                        for c in range(8):
                            S_.pe(lambda e, c=c, j=j, n=n, bk=bk: e.matmul(
                                bank(bk), lhsT=yT_v[s][:, c, j * 128:(j + 1) * 128], rhs=wout_v[:, c, n * 512:(n + 1) * 512],
                                start=(c == 0), stop=(c == 7)), reads=[ByT_s[s], BWo], writes=[bankB[bk]])
                        S_.dve(lambda e, j=j, n=n, bk=bk: e.tensor_tensor(
                            out=x_v[s][:, j, n * 512:(n + 1) * 512], in0=bank(bk), in1=x_v[s][:, j, n * 512:(n + 1) * 512],
                            op=ALU.add), reads=[bankB[bk], Bx_s[s]], writes=[Bx_s[s]])

            def st_norm(t):
                s = t % 2
                for j in range(NJ):
                    S_.act(lambda e, j=j: e.activation(out=hb_v[:, j, :], in_=x_v[s][:, j, :], func=AF.Square,
                                                       accum_out=ss_v[:, j:j + 1]),
                           reads=[Bx_s[s]], writes=[Bhb, Bss])
                S_.act(lambda e: e.activation(out=rs_v[:, :], in_=ss_v[:, :], func=AF.Ln, bias=epsc.t[:, 0:1], scale=1.0 / D),
                       reads=[Bss, epsc.B], writes=[Brs])
                S_.act(lambda e: e.activation(out=rs_v[:, :], in_=rs_v[:, :], func=AF.Exp, scale=-0.5),
                       reads=[Brs], writes=[Brs])
                for j in range(NJ):
                    S_.dve(lambda e, j=j: e.scalar_tensor_tensor(out=hb_v[:, j, :], in0=x_v[s][:, j, :],
                                                                 scalar=rs_v[:, j:j + 1], in1=gb2.t[:, :],
                                                                 op0=ALU.mult, op1=ALU.mult),
                           reads=[Bx_s[s], Brs, gb2.B], writes=[Bhb])

            def st_transpose(t):
                for c4 in range(2):
                    for cc_ in range(4):
                        c = c4 * 4 + cc_
                        for j in range(NJ):
                            S_.pe(lambda e, c=c, j=j, cc_=cc_: e.transpose(
                                out=PSb[:, cc_ * TT + j * 128: cc_ * TT + (j + 1) * 128],
                                in_=hb_v[:, j, c * 128:(c + 1) * 128], identity=ident.t[:, :]),
                                reads=[Bhb, ident.B], writes=[bankB[7]])
                    cp[0] += 1
                    if cp[0] % 2 == 0:
                        S_.act(lambda e, c4=c4: e.copy(out=hT_v[:, c4 * 4:c4 * 4 + 4, :],
                                                       in_=PSb[:, :].rearrange("p (c n) -> p c n", c=4)),
                               reads=[bankB[7]], writes=[BhT])
                    else:
                        S_.dve(lambda e, c4=c4: e.tensor_copy(out=hT_v[:, c4 * 4:c4 * 4 + 4, :],
                                                              in_=PSb[:, :].rearrange("p (c n) -> p c n", c=4)),
                               reads=[bankB[7]], writes=[BhT])

            def st_hidden(t):
                for f in range(32):
                    bk = nb()
                    for c in range(8):
                        S_.pe(lambda e, c=c, f=f, bk=bk: e.matmul(
                            bank(bk)[:, 0:TT], lhsT=w1_v[:, c, f * 128:(f + 1) * 128], rhs=hT_v[:, c, :],
                            start=(c == 0), stop=(c == 7)), reads=[BW1lo, BW1hi, BhT], writes=[bankB[bk]])
                    rt = f % 2
                    S_.act(lambda e, rt=rt, bk=bk: e.activation(out=relu_v[rt][:, :], in_=bank(bk)[:, 0:TT], func=AF.Relu),
                           reads=[bankB[bk]], writes=[Brelu[rt]])
                    if f % 2 == 0:
                        S_.dve(lambda e, f=f, rt=rt: e.tensor_tensor(out=aT_v[:, f, :], in0=relu_v[rt][:, :], in1=relu_v[rt][:, :],
                                                                     op=ALU.mult), reads=[Brelu[rt]], writes=[BaT[f]])
                    else:
                        S_.pool(lambda e, f=f, rt=rt: e.tensor_tensor(out=aT_v[:, f, :], in0=relu_v[rt][:, :], in1=relu_v[rt][:, :],
                                                                      op=ALU.mult), reads=[Brelu[rt]], writes=[BaT[f]])

            def st_down(t, j):
                s = t % 2
                for n in range(2):
                    bk = nb()
                    for f in range(32):
                        S_.pe(lambda e, f=f, n=n, bk=bk: e.matmul(
                            bank(bk), lhsT=aT_v[:, f, j * 128:(j + 1) * 128], rhs=w2_v[:, f, n * 512:(n + 1) * 512],
                            start=(f == 0), stop=(f == 31)), reads=[BaT[f], BW2], writes=[bankB[bk]])
                    S_.dve(lambda e, n=n, bk=bk: e.tensor_tensor(
                        out=x_v[s][:, j, n * 512:(n + 1) * 512], in0=bank(bk), in1=x_v[s][:, j, n * 512:(n + 1) * 512],
                        op=ALU.add), reads=[bankB[bk], Bx_s[s]], writes=[Bx_s[s]])

            def st_store(t):
                tok0 = t * TT
                s = t % 2
                op = S_.dma(lambda e: e.dma_start(
                    out=x_dst[tok0:tok0 + TT, :].rearrange("(j p) n -> p j n", p=128), in_=x_v[s][:, :, :]),
                    reads=[Bx_s[s]], writes=[Bx_dst])
                if is_out:
                    out_ops.append(op)

            st_load(0)
            st_outproj(0)
            st_norm(0)
            st_transpose(0)
            for t in range(NTT):
                if t + 1 < NTT:
                    st_load(t + 1)
                st_hidden(t)
                if t + 1 < NTT:
                    st_outproj(t + 1)
                    st_norm(t + 1)
                st_down(t, 0)
                if t + 1 < NTT:
                    st_transpose(t + 1)
                for j in range(1, NJ):
                    st_down(t, j)
                st_store(t)

        for l in range(DEPTH):
            load_params(l)
            x_src, Bsrc = (x_in, Buf("xin")) if l == 0 else (xs_d, Bxs)
            last = (l == DEPTH - 1)
            phase1(l, x_src, Bsrc)
            if stop_after == "p1":
                break
            phase2(l)
            if stop_after == "p2":
                break
            if pair:
                S_.dma(lambda e: e.collective_compute(
                    "AllGather", ALU.bypass, replica_groups=groups,
                    ins=[yT_d.rearrange("c p n -> (c p) n")], outs=[yTall_d.rearrange("c p n -> (c p) n")]),
                    reads=[ByT], writes=[ByTall], q="pool", ring="cc")
                x3_src, B3src = (xh_in, Buf("xhin")) if l == 0 else (xm_d, Bxm)
                x_dst, Bdst = (out_d, Bout) if last else (xm_d, Bxm)
                phase3(l, x3_src, B3src, x_dst, Bdst, last)
                if not last:
                    S_.dma(lambda e: e.collective_compute(
                        "AllGather", ALU.bypass, replica_groups=groups, ins=[xm_d[:, :]], outs=[xs_d[:, :]]),
                        reads=[Bxm], writes=[Bxs], q="pool", ring="cc")
            else:
                x_dst, Bdst = (out_d, Bout) if last else (xs_d, Bxs)
                phase3(l, x_src, Bsrc, x_dst, Bdst, last)

        if trunc is not None:
            S_.ops = S_.ops[:trunc]
            out_ops = []
            for en in ENGS:
                idxs = [i for i, o_ in enumerate(S_.ops) if o_[0] == en]
                if idxs:
                    out_ops.append(idxs[-1])
        if not out_ops:
            out_ops.append(len(S_.ops) - 1)
        S_.emit(nc, final_wait_ops=out_ops)
    return nc, S_


def _pair_layout(r, x_b, p):
    S = x_b.shape[0]
    SP = S // 2
    cols = np.concatenate([
        np.arange(0 + r * 128, 0 + (r + 1) * 128),
        np.arange(256 + r * 128, 256 + (r + 1) * 128),
        np.arange(512 + r * 128, 512 + (r + 1) * 128),
        np.arange(768 + 2 * r * 128, 768 + (2 * r + 2) * 128),
        np.arange(1280 + 2 * r * 128, 1280 + (2 * r + 2) * 128),
        np.arange(1792 + 2 * r * 128, 1792 + (2 * r + 2) * 128),
        np.arange(2304 + r * 128, 2304 + (r + 1) * 128),
        np.arange(2560 + r * 128, 2560 + (r + 1) * 128),
        np.arange(2816 + r * 128, 2816 + (r + 1) * 128),
        np.arange(3072 + r * 128, 3072 + (r + 1) * 128),
        np.arange(3328, 3344),
    ])
    rows = np.concatenate([
        np.concatenate([np.arange(i * 128, (i + 1) * 128),
                        np.arange(256 + 2 * i * 128, 256 + (2 * i + 2) * 128),
                        np.arange(768 + i * 128, 768 + (i + 1) * 128)]) for i in range(2)])
    slopes = np.array([[2.0 ** (-8.0 * (2 * r + hh + 1) / 4) for hh in range(2)]], dtype=np.float32)
    m = dict(p)
    m["x"] = np.ascontiguousarray(x_b)
    m["x_half"] = np.ascontiguousarray(x_b[r * SP:(r + 1) * SP])
    m["w_in"] = np.ascontiguousarray(p["w_in"][:, :, cols])
    m["conv_w"] = np.ascontiguousarray(p["conv_w"][:, :, r * 128:(r + 1) * 128])
    m["gla_alpha_w"] = np.ascontiguousarray(p["gla_alpha_w"][:, :, r * 128:(r + 1) * 128])
    m["gla_alpha_b"] = np.ascontiguousarray(p["gla_alpha_b"][:, r * 128:(r + 1) * 128])
    m["w_out"] = np.ascontiguousarray(p["w_out"][:, rows, :])
    m["slopes"] = slopes
    m["rmask"] = np.array([[1.0 - r, float(r)]], dtype=np.float32)
    return m


def kernel(x, ln1_g, w_in, conv_w, q_norm_g, k_norm_g, diff_lambda, diff_subln_g,
           gla_alpha_w, gla_alpha_b, gla_norm_g, w_out, ln2_g, w_mlp1, w_mlp2):
    x = np.asarray(x, dtype=np.float32)
    B, S, _ = x.shape
    depth = int(np.asarray(ln1_g).shape[0])
    nc, _ = build(S=S, DEPTH=depth)
    shared = dict(ln1_g=ln1_g, w_in=w_in, conv_w=conv_w, q_norm_g=q_norm_g, k_norm_g=k_norm_g,
                  diff_lambda=diff_lambda, diff_subln_g=diff_subln_g, gla_alpha_w=gla_alpha_w,
                  gla_alpha_b=gla_alpha_b, gla_norm_g=gla_norm_g, w_out=w_out, ln2_g=ln2_g,
                  w_mlp1=w_mlp1, w_mlp2=w_mlp2)
    shared = {k: np.ascontiguousarray(np.asarray(v, dtype=np.float32)) for k, v in shared.items()}
    in_maps = []
    for c in range(B):
        m = dict(shared)
        m["x"] = np.ascontiguousarray(x[c])
        in_maps.append(m)
    res = run_bass_kernel_spmd(nc, in_maps, core_ids=list(range(B)))
    return np.stack([np.asarray(r["out"], dtype=np.float32) for r in res.results], axis=0)
```

```python
import math
from contextlib import ExitStack
import numpy as np
import concourse.bass as bass
import concourse.mybir as mybir
from concourse.bass_utils import run_bass_kernel_spmd

F32 = mybir.dt.float32
BF16 = mybir.dt.bfloat16
I32 = mybir.dt.int32
ALU = mybir.AluOpType
AF = mybir.ActivationFunctionType
AX = mybir.AxisListType

ENGS = ("pe", "act", "dve", "pool", "sp")
SEM_EPOCH = 24000
DMA_RING = 8

D = 1024
DIN = 3344
DFF = 4096
EPS = 1e-6
NEG = -30000.0


class Buf:
    __slots__ = ("name", "w", "r")

    def __init__(self, name=""):
        self.name = name
        self.w = None
        self.r = []


class Sched:
    def __init__(self):
        self.ops = []
        self.sameeng_sync = {"act": True, "dve": True, "pool": True, "pe": False, "sp": False}
        self.marks = []
        self.fence_deps = None
        self.fence_id = 0
        self.fence_passed = {}
        self.last_ops = {e: [] for e in ENGS}

    def mark(self, name):
        self.marks.append((name, len(self.ops)))

    def fence(self):
        deps = set()
        for e in ENGS:
            deps.update(self.last_ops[e][-(DMA_RING + 1):])
        self.fence_deps = deps
        self.fence_id += 1

    def add(self, eng, fn, reads=(), writes=(), dma=False, ring="d"):
        idx = len(self.ops)
        deps = set()
        if self.fence_deps is not None and self.fence_passed.get(eng) != self.fence_id:
            deps.update(self.fence_deps)
            self.fence_passed[eng] = self.fence_id
        self.last_ops[eng].append(idx)
        if len(self.last_ops[eng]) > 4 * DMA_RING:
            del self.last_ops[eng][:-2 * DMA_RING]
        for b in reads:
            if b.w is not None:
                deps.add(b.w)
        for b in writes:
            if b.w is not None:
                deps.add(b.w)
            for r in b.r:
                deps.add(r)
        for b in reads:
            b.r.append(idx)
        for b in writes:
            b.w = idx
            b.r = []
        deps.discard(idx)
        self.ops.append([eng, fn, deps, dma, ring])
        return idx

    def pe(self, fn, reads=(), writes=()):
        return self.add("pe", fn, reads, writes)

    def act(self, fn, reads=(), writes=()):
        return self.add("act", fn, reads, writes)

    def dve(self, fn, reads=(), writes=()):
        return self.add("dve", fn, reads, writes)

    def pool(self, fn, reads=(), writes=()):
        return self.add("pool", fn, reads, writes)

    def dma(self, fn, reads=(), writes=(), q="sp", ring="d"):
        return self.add(q, fn, reads, writes, dma=True, ring=ring)

    def emit(self, nc, final_wait_ops=()):
        ops = self.ops
        n = len(ops)
        signal = [False] * n
        for i, (eng, fn, deps, dma, _rg) in enumerate(ops):
            if dma:
                signal[i] = True
            for d in deps:
                if ops[d][3]:
                    continue
                if ops[d][0] != eng or self.sameeng_sync.get(eng, False):
                    signal[d] = True
        for i in final_wait_ops:
            signal[i] = True
        sem_of = [None] * n
        sem_specs = []
        cur = {e: None for e in ENGS}
        dma_ring = {}
        dma_rr = {}
        dma_prev = [None] * n
        ring_last = {}
        for i, (eng, fn, deps, dma, rg) in enumerate(ops):
            if not signal[i]:
                continue
            if dma:
                rk = (eng, rg)
                ring = dma_ring.setdefault(rk, [])
                nslots = DMA_RING if rg == "d" else 1
                slot = dma_rr.get(rk, 0) % nslots
                dma_rr[rk] = dma_rr.get(rk, 0) + 1
                if len(ring) <= slot:
                    sem_specs.append(f"{rg}_{eng}_{slot}_{len(sem_specs)}")
                    ring.append([len(sem_specs) - 1, 0])
                if ring[slot][1] + 16 > SEM_EPOCH:
                    sem_specs.append(f"{rg}_{eng}_{slot}_{len(sem_specs)}")
                    ring[slot] = [len(sem_specs) - 1, 0]
                ring[slot][1] += 16
                sem_of[i] = (ring[slot][0], ring[slot][1])
                key = (eng, rg, slot)
                dma_prev[i] = ring_last.get(key)
                ring_last[key] = i
            else:
                c = cur[eng]
                if c is None or c[1] + 1 > SEM_EPOCH:
                    sem_specs.append(f"s_{eng}_{len(sem_specs)}")
                    c = [len(sem_specs) - 1, 0]
                    cur[eng] = c
                c[1] += 1
                sem_of[i] = (c[0], c[1])
        self.n_sems = len(sem_specs)
        per_eng = {e: [] for e in ENGS}
        for i, op in enumerate(ops):
            per_eng[op[0]].append(i)
        self.stats = {e: len(per_eng[e]) for e in ENGS}
        self.stats["signals"] = sum(signal)

        with ExitStack() as st:
            sems = [st.enter_context(nc.semaphore(nm)) for nm in sem_specs]
            block = st.enter_context(nc.Block())

            def make(engname):
                idxs = per_eng[engname]

                def body(e):
                    seen = {}
                    nwait = 0
                    for i in idxs:
                        eng, fn, deps, dma, _rg = ops[i]
                        waits = {}
                        dl = list(deps)
                        if dma and dma_prev[i] is not None:
                            dl.append(dma_prev[i])
                        for d in dl:
                            if (not ops[d][3]) and ops[d][0] == eng and not self.sameeng_sync.get(eng, False):
                                continue
                            s, v = sem_of[d]
                            if seen.get(s, 0) >= v:
                                continue
                            if waits.get(s, 0) < v:
                                waits[s] = v
                        for s, v in waits.items():
                            e.wait_ge(sems[s], v)
                            seen[s] = v
                            nwait += 1
                        ins = fn(e)
                        if signal[i]:
                            s, v = sem_of[i]
                            ins.then_inc(sems[s], 16 if dma else 1)
                    if engname == "sp":
                        for i in final_wait_ops:
                            s, v = sem_of[i]
                            e.wait_ge(sems[s], v)
                    self.stats["waits_" + engname] = nwait
                return body

            block.tensor(make("pe"))
            block.scalar(make("act"))
            block.vector(make("dve"))
            block.gpsimd(make("pool"))
            block.sync(make("sp"))


class Tl:
    def __init__(self, t, nb=1, name=""):
        self.t = t
        self.b = [Buf(f"{name}{i}") for i in range(nb)]

    @property
    def B(self):
        return self.b[0]


def build(S=8192, DEPTH=4, stop_after=None, debug=False, trunc=None, pair=False, ncores=4):
    NT = S // 512
    NB = S // 128
    nc = bass.Bass("TRN2", target_bir_lowering=False)
    if pair:
        NCV, NH, NP = 1, 2, 1
        C_U, C_CB, C_CC, C_Q, C_K, C_V, C_GQ, C_GK, C_GG, C_GA = 0, 128, 256, 384, 640, 896, 1152, 1280, 1536, 1664
        DINL = 1680
        SP = S // 2
    else:
        NCV, NH, NP = 2, 4, 2
        C_U, C_CB, C_CC, C_Q, C_K, C_V, C_GQ, C_GK, C_GG, C_GA = 0, 256, 512, 768, 1280, 1792, 2304, 2560, 3072, 3328
        DINL = DIN
        SP = S
    GW = NP * 128
    VW = NH * 128
    NYC = NCV + NH + NP
    BPB = 512 // GW

    def din(name, shape):
        return nc.dram_tensor(name, shape, F32, kind="ExternalInput").ap()

    x_in = din("x", [S, D])
    if pair:
        xh_in = din("x_half", [SP, D])
        slopes_in = din("slopes", [1, NH])
        rmask_in = din("rmask", [1, 2])
    ln1_g = din("ln1_g", [DEPTH, D])
    w_in = din("w_in", [DEPTH, D, DINL])
    conv_w = din("conv_w", [DEPTH, 3, NCV * 128])
    q_norm_g = din("q_norm_g", [DEPTH, 64])
    k_norm_g = din("k_norm_g", [DEPTH, 64])
    diff_lambda = din("diff_lambda", [DEPTH, 4, 64])
    diff_subln_g = din("diff_subln_g", [DEPTH, 128])
    gla_alpha_w = din("gla_alpha_w", [DEPTH, 16, GW])
    gla_alpha_b = din("gla_alpha_b", [DEPTH, GW])
    gla_norm_g = din("gla_norm_g", [DEPTH, 64])
    w_out = din("w_out", [DEPTH, D, D])
    ln2_g = din("ln2_g", [DEPTH, D])
    w_mlp1 = din("w_mlp1", [DEPTH, D, DFF])
    w_mlp2 = din("w_mlp2", [DEPTH, DFF, D])
    out_d = nc.dram_tensor("out", [SP, D], F32, kind="ExternalOutput").ap()

    xs_d = nc.dram_tensor("xs_scr", [S, D], F32, kind="Internal").ap()
    sk = "ExternalOutput" if debug else "Internal"
    qT_d = nc.dram_tensor("qT_scr", [NH, 128, S], BF16, kind=sk).ap()
    kT_d = nc.dram_tensor("kT_scr", [NH, 128, S], BF16, kind=sk).ap()
    v_d = nc.dram_tensor("v_scr", [S, VW], BF16, kind=sk).ap()
    yT_d = nc.dram_tensor("yT_scr", [NYC, 128, S], BF16, kind=sk).ap()
    if pair:
        xm_d = nc.dram_tensor("xm_scr", [SP, D], F32, kind="Internal").ap()
        yTall_d = nc.dram_tensor("yTall_scr", [2 * NYC, 128, S], BF16, kind="Internal").ap()
        Bxm, ByTall = Buf("xm"), Buf("yTall")
        groups = [[2 * i, 2 * i + 1] for i in range(ncores // 2)]
    Bxs, BqT, BkT, Bv, ByT, Bout = Buf("xs"), Buf("qT"), Buf("kT"), Buf("v"), Buf("yT"), Buf("out")

    S_ = Sched()
    out_ops = []

    with ExitStack() as st:
        def sb(name, shape, dt, nb=1):
            return Tl(st.enter_context(nc.sbuf_tensor(name, shape, dt)), nb, name)

        PSf = st.enter_context(nc.psum_tensor("psf", [128, 8 * 512], F32))
        bankB = [Buf(f"bank{i}") for i in range(8)]

        def bank(i):
            return PSf[:, i * 512:(i + 1) * 512]

        PSb = bank(7).bitcast(BF16)

        ident = sb("ident", [128, 128], BF16)
        ones_bf = sb("ones_bf", [128, 128], BF16)
        blk64 = sb("blk64", [128, 128], BF16)
        o128 = sb("o128", [128, 128], BF16)
        triu = sb("triu", [128, 128], F32)
        strl = sb("strl", [128, 128], F32)
        triu_b = sb("triu_b", [128, 128], BF16)
        strl_b = sb("strl_b", [128, 128], BF16)
        iot = sb("iot", [128, 128], I32)
        dkq = sb("dkq", [128, 128], F32)
        tdiag = sb("tdiag", [128, 4, 128], BF16)
        kcol = sb("kcol", [128, 1], F32)
        kcoli = sb("kcoli", [128, 1], I32)
        NDEL = 4 * (NT - 1) + 4
        kbias = sb("kbias", [128, 4, NDEL], F32)
        kdel = sb("kdel", [128, NDEL], F32)
        slopec = sb("slopec", [128, 4], F32)
        slopes = [2.0 ** (-8.0 * (h + 1) / 4) for h in range(4)]
        if pair:
            rmask = sb("rmask_sb", [128, 2], F32)
            S_.dma(lambda e: e.dma_start(out=slopec.t[:, 0:NH], in_=slopes_in[0:1, :].partition_broadcast(128)), writes=[slopec.B])
            S_.dma(lambda e: e.dma_start(out=rmask.t[:, :], in_=rmask_in[0:1, :].partition_broadcast(128)), writes=[rmask.B])
        else:
            for h in range(4):
                S_.pool(lambda e, h=h: e.memset(slopec.t[:, h:h + 1], slopes[h]), writes=[slopec.B])

        hmask = sb("hmask", [128, 2], F32)
        cmask = sb("cmask", [128, 2], F32)
        S_.pool(lambda e: e.memset(hmask.t[:], 0.0), writes=[hmask.B])
        S_.pool(lambda e: e.memset(hmask.t[0:64, 0:1], 0.125), writes=[hmask.B])
        S_.pool(lambda e: e.memset(hmask.t[64:128, 1:2], 0.125), writes=[hmask.B])
        S_.pool(lambda e: e.memset(cmask.t[:], 0.0), writes=[cmask.B])
        S_.pool(lambda e: e.memset(cmask.t[0:64, 0:1], 1.0), writes=[cmask.B])
        S_.pool(lambda e: e.memset(cmask.t[64:128, 1:2], 1.0), writes=[cmask.B])
        epsc = sb("epsc", [128, 1], F32)
        onec = sb("onec", [128, 1], F32)
        S_.pool(lambda e: e.memset(epsc.t[:], EPS), writes=[epsc.B])
        S_.pool(lambda e: e.memset(onec.t[:], 1.0), writes=[onec.B])
        S_.pool(lambda e: e.memset(ident.t[:], 0.0), writes=[ident.B])
        S_.pool(lambda e: e.affine_select(out=ident.t[:], in_=ident.t[:], pattern=[[-1, 128]],
                                          compare_op=ALU.not_equal, fill=1.0, base=0, channel_multiplier=1),
                reads=[ident.B], writes=[ident.B])
        S_.pool(lambda e: e.memset(ones_bf.t[:], 1.0), writes=[ones_bf.B])
        S_.pool(lambda e: e.memset(o128.t[:], 1.0 / 128), writes=[o128.B])
        S_.pool(lambda e: e.memset(blk64.t[:], 1.0 / 64), writes=[blk64.B])
        S_.pool(lambda e: e.memset(blk64.t[0:64, 64:128], 0.0), writes=[blk64.B])
        S_.pool(lambda e: e.memset(blk64.t[64:128, 0:64], 0.0), writes=[blk64.B])
        S_.pool(lambda e: e.memset(triu.t[:], 1.0), writes=[triu.B])
        S_.pool(lambda e: e.affine_select(out=triu.t[:], in_=triu.t[:], pattern=[[1, 128]],
                                          compare_op=ALU.is_ge, fill=0.0, base=0, channel_multiplier=-1),
                reads=[triu.B], writes=[triu.B])
        S_.pool(lambda e: e.memset(triu.t[0:64, 64:128], 0.0), writes=[triu.B])
        S_.pool(lambda e: e.memset(strl.t[:], 1.0), writes=[strl.B])
        S_.pool(lambda e: e.affine_select(out=strl.t[:], in_=strl.t[:], pattern=[[-1, 128]],
                                          compare_op=ALU.is_gt, fill=0.0, base=0, channel_multiplier=1),
                reads=[strl.B], writes=[strl.B])
        S_.pool(lambda e: e.memset(strl.t[64:128, 0:64], 0.0), writes=[strl.B])
        S_.dve(lambda e: e.tensor_copy(out=triu_b.t[:], in_=triu.t[:]), reads=[triu.B], writes=[triu_b.B])
        S_.dve(lambda e: e.tensor_copy(out=strl_b.t[:], in_=strl.t[:]), reads=[strl.B], writes=[strl_b.B])
        S_.pool(lambda e: e.iota(iot.t[:], pattern=[[-1, 128]], base=0, channel_multiplier=1), writes=[iot.B])
        S_.dve(lambda e: e.tensor_copy(out=dkq.t[:], in_=iot.t[:]), reads=[iot.B], writes=[dkq.B])
        S_.dve(lambda e: e.tensor_scalar_max(out=dkq.t[:], in0=dkq.t[:], scalar1=0.0), reads=[dkq.B], writes=[dkq.B])
        S_.dve(lambda e: e.tensor_scalar_mul(out=dkq.t[:], in0=dkq.t[:], scalar1=-2.0), reads=[dkq.B], writes=[dkq.B])
        for h in range(NH):
            def f(e, h=h):
                return e.tensor_scalar_mul(out=tdiag.t[:, h, :], in0=dkq.t[:], scalar1=slopec.t[:, h:h + 1])
            S_.dve(f, reads=[dkq.B, slopec.B], writes=[tdiag.B])
        S_.dve(lambda e: e.memset(tdiag.t[64:128, :, 0:64], NEG), reads=[tdiag.B], writes=[tdiag.B])
        S_.pool(lambda e: e.iota(kcoli.t[:], pattern=[[0, 1]], base=0, channel_multiplier=1), writes=[kcoli.B])
        S_.dve(lambda e: e.tensor_copy(out=kcol.t[:], in_=kcoli.t[:]), reads=[kcoli.B], writes=[kcol.B])
        for di in range(NDEL):
            delta = di - 4 * (NT - 1)

            def f(e, di=di, delta=delta):
                return e.tensor_scalar_add(out=kdel.t[:, di:di + 1], in0=kcol.t[:], scalar1=float(128 * delta - 256))
            S_.dve(f, reads=[kcol.B], writes=[kdel.B])
        for h in range(NH):
            S_.dve(lambda e, h=h: e.tensor_scalar_mul(out=kbias.t[:, h, :], in0=kdel.t[:, :], scalar1=slopec.t[:, h:h + 1]),
                   reads=[kdel.B, slopec.B], writes=[kbias.B])

        gb1 = sb("gb", [128, D], F32)
        gb2 = gb1
        cw = sb("cw", [128, 2, 3], F32)
        qg = sb("qg", [128, 1], F32)
        kg = sb("kg", [128, 1], F32)
        lamt = sb("lamt", [128, 4, 64], F32)
        lamw = sb("lamw", [128, 2, 64], F32)
        lams = sb("lams", [128, 2], F32)
        nlam = sb("nlam", [128, 1], F32)
        subg = sb("subg", [128, 1], F32)
        aw = sb("aw", [17, GW], F32)
        aw_hi = sb("aw_hi", [17, GW], BF16)
        aw_lo = sb("aw_lo", [17, GW], BF16)
        gng = sb("gng", [128, 1], F32)

        def load_params(l):
            S_.mark(f"params{l}")
            S_.dma(lambda e: e.dma_start(out=gb1.t[:], in_=ln1_g[l:l + 1, :].partition_broadcast(128)), writes=[gb1.B])
            for cc_ in range(NCV):
                for k_ in range(3):
                    S_.dma(lambda e, cc_=cc_, k_=k_: e.dma_start(
                        out=cw.t[:, cc_, k_:k_ + 1],
                        in_=conv_w[l, k_, cc_ * 128:(cc_ + 1) * 128].rearrange("(p o) -> p o", o=1)), writes=[cw.B])
            for hh in range(2):
                S_.dma(lambda e, hh=hh: e.dma_start(out=qg.t[hh * 64:(hh + 1) * 64, :],
                                                    in_=q_norm_g[l].rearrange("(p o) -> p o", o=1)), writes=[qg.B])
                S_.dma(lambda e, hh=hh: e.dma_start(out=kg.t[hh * 64:(hh + 1) * 64, :],
                                                    in_=k_norm_g[l].rearrange("(p o) -> p o", o=1)), writes=[kg.B])
                S_.dma(lambda e, hh=hh: e.dma_start(out=gng.t[hh * 64:(hh + 1) * 64, :],
                                                    in_=gla_norm_g[l].rearrange("(p o) -> p o", o=1)), writes=[gng.B])
            S_.dma(lambda e: e.dma_start(out=lamt.t[:].rearrange("p a b -> p (a b)"),
                                         in_=diff_lambda[l:l + 1].rearrange("o a b -> o (a b)").partition_broadcast(128)),
                   writes=[lamt.B])
            S_.dma(lambda e: e.dma_start(out=subg.t[:], in_=diff_subln_g[l].rearrange("(p o) -> p o", o=1)),
                   writes=[subg.B])
            S_.dma(lambda e: e.dma_start(out=aw.t[0:16, :], in_=gla_alpha_w[l]), writes=[aw.B])
            S_.dma(lambda e: e.dma_start(out=aw.t[16:17, :], in_=gla_alpha_b[l:l + 1, :]), writes=[aw.B])
            S_.dve(lambda e: e.tensor_copy(out=aw_hi.t[:], in_=aw.t[:]), reads=[aw.B], writes=[aw_hi.B])
            S_.dve(lambda e: e.tensor_tensor(out=aw_lo.t[:], in0=aw.t[:], in1=aw_hi.t[:], op=ALU.subtract),
                   reads=[aw.B, aw_hi.B], writes=[aw_lo.B])
            lam_init = 0.8 - 0.6 * math.exp(-0.3 * l)
            S_.dve(lambda e: e.tensor_scalar_mul(out=qg.t[:], in0=qg.t[:], scalar1=0.125), reads=[qg.B], writes=[qg.B])
            S_.dve(lambda e: e.tensor_tensor(out=lamw.t[:, 0, :], in0=lamt.t[:, 0, :], in1=lamt.t[:, 1, :], op=ALU.mult),
                   reads=[lamt.B], writes=[lamw.B])
            S_.dve(lambda e: e.tensor_tensor(out=lamw.t[:, 1, :], in0=lamt.t[:, 2, :], in1=lamt.t[:, 3, :], op=ALU.mult),
                   reads=[lamt.B], writes=[lamw.B])
            S_.dve(lambda e: e.tensor_reduce(out=lams.t[:], in_=lamw.t[:], axis=AX.X, op=ALU.add),
                   reads=[lamw.B], writes=[lams.B])
            S_.act(lambda e: e.activation(out=lams.t[:], in_=lams.t[:], func=AF.Exp), reads=[lams.B], writes=[lams.B])
            S_.dve(lambda e: e.tensor_tensor(out=nlam.t[:], in0=lams.t[:, 1:2], in1=lams.t[:, 0:1], op=ALU.subtract),
                   reads=[lams.B], writes=[nlam.B])
            S_.dve(lambda e: e.tensor_scalar_add(out=nlam.t[:], in0=nlam.t[:], scalar1=-lam_init),
                   reads=[nlam.B], writes=[nlam.B])
            S_.dve(lambda e: e.tensor_scalar_mul(out=subg.t[:], in0=subg.t[:], scalar1=1.0 - lam_init),
                   reads=[subg.B], writes=[subg.B])

        AWORDS = 48700
        ARENA = st.enter_context(nc.sbuf_tensor("arena", [128, AWORDS], F32))
        WW = (8 * D + 8 * DFF + 32 * D) // 2
        WREG = ARENA[:, 0:WW].bitcast(BF16)
        BW = Buf("wreg")
        BWo, BW1lo, BW1hi, BW2 = Buf("wout"), Buf("w1lo"), Buf("w1hi"), Buf("w2")
        win_v = WREG[:, 0:8 * DINL].rearrange("p (c n) -> p c n", c=8)
        wout_v = WREG[:, 0:8 * D].rearrange("p (c n) -> p c n", c=8)
        w1_v = WREG[:, 8 * D:8 * D + 8 * DFF].rearrange("p (c n) -> p c n", c=8)
        w2_v = WREG[:, 8 * D + 8 * DFF:].rearrange("p (c n) -> p c n", c=32)

        def phase1(l, x_src, Bx_src):
            S_.fence()
            for c in range(8):
                S_.dma(lambda e, c=c: e.dma_start(out=win_v[:, c, :], in_=w_in[l, c * 128:(c + 1) * 128, :]),
                       writes=[BW], q="pool")
            a = ARENA
            o = 8 * DINL // 2

            def carve(n, dt=F32, shape=None):
                nonlocal o
                v = a[:, o:o + n]
                o += n
                return v

            xt_v = [carve(4096).rearrange("p (j n) -> p j n", j=4) for _ in range(1)]
            Bxt = [Buf("xt0")]
            hb_raw = carve(2048)
            hb_v = hb_raw.bitcast(BF16).rearrange("p (j n) -> p j n", j=4)
            Bhb = Buf("hb")
            hT_raw = carve(2048)
            hT_v = hT_raw.bitcast(BF16).rearrange("p (c n) -> p c n", c=8)
            BhT = Buf("hT")
            junk_v = carve(512).bitcast(BF16)
            Bjunk = Buf("junk")
            ss_v = carve(4)
            rs_v = carve(4)
            Bss, Brs = Buf("ss"), Buf("rs")
            u_v = carve(512)
            Bu = Buf("u")
            bg_v = carve(512)
            Bbg = Buf("bg")
            zc_v = [carve(514) for _ in range(2)]
            Bzc = [Buf("zc0"), Buf("zc1")]
            acc_v = carve(512)
            Bacc = Buf("acc")
            yc_v = carve(256).bitcast(BF16)
            Byc = Buf("yc")
            sq_v = carve(256).bitcast(BF16)
            Bsq = Buf("sq")
            sq2_v = [carve(256).bitcast(BF16) for _ in range(2)]
            Bsq2 = [Buf("sq2a"), Buf("sq2b")]
            zq_v = [carve(512) for _ in range(2)]
            Bzq = [Buf("zqa"), Buf("zqb")]
            rstd_v = carve(512)
            Brstd = Buf("rstd")
            qk_v = carve(256).bitcast(BF16)
            Bqk = Buf("qk")
            vt_v = carve(256).bitcast(BF16)[:, 0:VW]
            Bvt = Buf("vt")
            aT_v = carve(512)
            BaT = Buf("aT")
            aTh_v = carve(256).bitcast(BF16)
            aTl_v = carve(256).bitcast(BF16)
            BaTh, BaTl = Buf("aTh"), Buf("aTl")
            lah_v = carve(2 * GW).bitcast(BF16).rearrange("p (j n) -> p j n", j=4)
            lal_v = carve(2 * GW).bitcast(BF16).rearrange("p (j n) -> p j n", j=4)
            Blah, Blal = Buf("lah"), Buf("lal")
            la_v = carve(4 * GW).rearrange("p (j n) -> p j n", j=4)
            Bla = Buf("la")
            ebT_v = [carve(512) for _ in range(2)]
            enbT_v = [carve(512) for _ in range(2)]
            BebT = [Buf("ebT0"), Buf("ebT1")]
            BenbT = [Buf("enbT0"), Buf("enbT1")]
            erev_v = carve(4 * GW).rearrange("p (j n) -> p j n", j=4)
            Berev = Buf("erev")
            qinm_v = carve(512 * NP).bitcast(BF16).rearrange("p (c h n) -> p c h n", c=NP, h=2)
            kin_v = carve(256 * NP).bitcast(BF16).rearrange("p (c n) -> p c n", c=NP)
            Bqin, Bkin = [Buf("qin0"), Buf("qin1")], [Buf("kin0"), Buf("kin1")]
            erevm_v = [carve(4 * GW).rearrange("p (j n) -> p j n", j=4) for _ in range(2)]
            Berevm = [Buf("erevm0"), Buf("erevm1")]
            kendm_v = carve(4 * GW).bitcast(BF16).rearrange("p (j c n) -> p j c n", j=4, c=2)
            vtok_v = carve(2 * GW).bitcast(BF16).rearrange("p (j n) -> p j n", j=4)
            vpad_v = carve(512 * NP).bitcast(BF16).rearrange("p (j a b n) -> p j a b n", j=4, a=NP, b=2)
            Bkend, Bvtok = [Buf(f"kend{j}") for j in range(4)], [Buf(f"vtok{j}") for j in range(4)]
            Bvpad = [Buf(f"vpad{j}") for j in range(4)]
            am_v = carve(128 * NP).bitcast(BF16).rearrange("p (h n) -> p h n", h=2 * NP)
            Bam = [Buf("am0"), Buf("am1")]
            Sf_v = carve(128 * NP).rearrange("p (c n) -> p c n", c=NP)
            BSf = Buf("Sf")
            NSL = 4
            Sbf_v = carve(64 * NP * NSL).bitcast(BF16).rearrange("p (s c n) -> p s c n", s=NSL, c=NP)
            BSbf = [Buf(f"Sbf{i}") for i in range(NSL)]
            sg_v = carve(512)
            Bsg = Buf("sg")
            t1_v = carve(512)
            Bt1 = Buf("t1")
            yg_v = carve(256).bitcast(BF16)
            Byg = Buf("yg")
            assert o <= AWORDS, o

            S_.dve(lambda e: e.memset(aT_v[:, :], 1.0), writes=[BaT])
            S_.dve(lambda e: e.memset(aTh_v[:, :], 1.0), writes=[BaTh])
            S_.dve(lambda e: e.memset(aTl_v[:, :], 0.0), writes=[BaTl])
            S_.dve(lambda e: e.memset(Sf_v[:, :, :], 0.0), writes=[BSf])
            S_.pool(lambda e: e.memset(vpad_v[:, :, :, :, :], 0.0), writes=Bvpad)
            for i in range(2):
                S_.dve(lambda e, i=i: e.memset(zc_v[i][:, 0:2], 0.0), writes=[Bzc[i]])

            rr = [0]

            def nb():
                k = rr[0] % 4
                rr[0] += 1
                return k

            cp = [0]

            def copy_rr(out, in_, reads, writes):
                cp[0] += 1
                if cp[0] % 2 == 0:
                    S_.act(lambda e: e.copy(out=out, in_=in_), reads=reads, writes=writes)
                else:
                    S_.dve(lambda e: e.tensor_copy(out=out, in_=in_), reads=reads, writes=writes)

            def fm_group(col0, ncols, bk, t):
                for c in range(8):
                    S_.pe(lambda e, c=c: e.matmul(bank(bk)[0:ncols, :], lhsT=win_v[:, c, col0:col0 + ncols],
                                                  rhs=hT_v[:, c, :], start=(c == 0), stop=(c == 7)),
                          reads=[BW, BhT], writes=[bankB[bk]])

            chunk_ctr = [0]

            def sec_norm(tn):
                tok0 = tn * 512
                xt = xt_v[0]
                bxt = Bxt[0]
                S_.dma(lambda e, xt=xt, tok0=tok0: e.dma_start(
                    out=xt, in_=x_src[tok0:tok0 + 512, :].rearrange("(j p) n -> p j n", p=128)),
                    reads=[Bx_src], writes=[bxt])
                for j in range(4):
                    S_.act(lambda e, j=j, xt=xt: e.activation(out=junk_v[:, :], in_=xt[:, j, :], func=AF.Square,
                                                              accum_out=ss_v[:, j:j + 1]),
                           reads=[bxt], writes=[Bjunk, Bss])
                S_.act(lambda e: e.activation(out=rs_v[:, :], in_=ss_v[:, :], func=AF.Ln, bias=epsc.t[:, 0:1], scale=1.0 / D),
                       reads=[Bss, epsc.B], writes=[Brs])
                S_.act(lambda e: e.activation(out=rs_v[:, :], in_=rs_v[:, :], func=AF.Exp, scale=-0.5),
                       reads=[Brs], writes=[Brs])
                for j in range(4):
                    S_.dve(lambda e, j=j, xt=xt: e.scalar_tensor_tensor(out=hb_v[:, j, :], in0=xt[:, j, :],
                                                                        scalar=rs_v[:, j:j + 1], in1=gb1.t[:, :],
                                                                        op0=ALU.mult, op1=ALU.mult),
                           reads=[bxt, Brs, gb1.B], writes=[Bhb])


            sec_norm(0)
            for t in range(NT):
                S_.mark(f"p1_tile{t}")
                tok0 = t * 512
                S_.mark(f"p1_t{t}_transposes")
                for c2 in range(4):
                    for cc_ in range(2):
                        c = c2 * 2 + cc_
                        for j in range(4):
                            S_.pe(lambda e, c=c, j=j, cc_=cc_: e.transpose(
                                out=PSb[:, cc_ * 512 + j * 128: cc_ * 512 + (j + 1) * 128],
                                in_=hb_v[:, j, c * 128:(c + 1) * 128], identity=ident.t[:, :]),
                                reads=[Bhb, ident.B], writes=[bankB[7]])
                    copy_rr(hT_v[:, c2 * 2:c2 * 2 + 2, :], PSb[:, :].rearrange("p (c n) -> p c n", c=2), [bankB[7]], [BhT])

                def sec_conv():
                    S_.mark(f"p1_t{t}_conv")
                    for cc in range(NCV):
                        bu = nb()
                        fm_group(C_U + cc * 128, 128, bu, t)
                        S_.act(lambda e, bu=bu: e.copy(out=u_v[:, :], in_=bank(bu)), reads=[bankB[bu]], writes=[Bu])
                        bc = nb()
                        fm_group(C_CC + cc * 128, 128, bc, t)
                        S_.dve(lambda e, bc=bc, cc=cc: e.tensor_tensor(out=zc_v[cc][:, 2:514], in0=bank(bc), in1=u_v[:, :],
                                                                       op=ALU.mult),
                               reads=[bankB[bc], Bu], writes=[Bzc[cc]])
                        S_.dve(lambda e, cc=cc: e.tensor_scalar_mul(out=acc_v[:, :], in0=zc_v[cc][:, 2:514],
                                                                    scalar1=cw.t[:, cc, 2:3]),
                               reads=[Bzc[cc], cw.B], writes=[Bacc])
                        S_.dve(lambda e, cc=cc: e.scalar_tensor_tensor(out=acc_v[:, :], in0=zc_v[cc][:, 1:513],
                                                                       scalar=cw.t[:, cc, 1:2], in1=acc_v[:, :],
                                                                       op0=ALU.mult, op1=ALU.add),
                               reads=[Bzc[cc], cw.B, Bacc], writes=[Bacc])
                        S_.dve(lambda e, cc=cc: e.scalar_tensor_tensor(out=acc_v[:, :], in0=zc_v[cc][:, 0:512],
                                                                       scalar=cw.t[:, cc, 0:1], in1=acc_v[:, :],
                                                                       op0=ALU.mult, op1=ALU.add),
                               reads=[Bzc[cc], cw.B, Bacc], writes=[Bacc])
                        bb = nb()
                        fm_group(C_CB + cc * 128, 128, bb, t)
                        S_.act(lambda e, bb=bb: e.copy(out=bg_v[:, :], in_=bank(bb)), reads=[bankB[bb]], writes=[Bbg])
                        S_.dve(lambda e: e.tensor_tensor(out=yc_v[:, :], in0=bg_v[:, :], in1=acc_v[:, :], op=ALU.mult),
                               reads=[Bbg, Bacc], writes=[Byc])
                        S_.dma(lambda e, cc=cc, tok0=tok0: e.dma_start(out=yT_d[cc, :, tok0:tok0 + 512], in_=yc_v[:, :]),
                               reads=[Byc], writes=[ByT])
                        S_.dve(lambda e, cc=cc: e.tensor_copy(out=zc_v[cc][:, 0:2], in_=zc_v[cc][:, 512:514]),
                               reads=[Bzc[cc]], writes=[Bzc[cc]])


                def sec_qk():
                    S_.mark(f"p1_t{t}_qk")
                    groups_ = [(which, h) for which in range(2) for h in range(NH)]
                    pend = None
                    for gi, g in enumerate(groups_ + [None]):
                        if g is not None:
                            which, h = g
                            col0 = (C_Q if which == 0 else C_K) + h * 128
                            bz = nb()
                            fm_group(col0, 128, bz, t)
                            sqi = gi % 2
                            S_.dve(lambda e, bz=bz, sqi=sqi: e.tensor_copy(out=zq_v[sqi][:, :], in_=bank(bz)),
                                   reads=[bankB[bz]], writes=[Bzq[sqi]])
                            S_.act(lambda e, sqi=sqi: e.activation(out=sq2_v[sqi][:, :], in_=zq_v[sqi][:, :], func=AF.Square),
                                   reads=[Bzq[sqi]], writes=[Bsq2[sqi]])
                        if pend is not None:
                            which_p, h_p, bz_p, sqi_p = pend
                            bm = nb()
                            S_.pe(lambda e, bm=bm, sqi_p=sqi_p: e.matmul(bank(bm), lhsT=blk64.t[:, :], rhs=sq2_v[sqi_p][:, :],
                                                                         start=True, stop=True),
                                  reads=[Bsq2[sqi_p], blk64.B], writes=[bankB[bm]])
                            S_.act(lambda e, bm=bm: e.activation(out=rstd_v[:, :], in_=bank(bm), func=AF.Ln, bias=epsc.t[:, 0:1]),
                                   reads=[bankB[bm], epsc.B], writes=[Brstd])
                            S_.act(lambda e: e.activation(out=rstd_v[:, :], in_=rstd_v[:, :], func=AF.Exp, scale=-0.5),
                                   reads=[Brstd], writes=[Brstd])
                            gcol = qg if which_p == 0 else kg
                            S_.dve(lambda e, sqi_p=sqi_p, gcol=gcol: e.scalar_tensor_tensor(
                                out=qk_v[:, :], in0=zq_v[sqi_p][:, :], scalar=gcol.t[:, 0:1], in1=rstd_v[:, :],
                                op0=ALU.mult, op1=ALU.mult),
                                reads=[Bzq[sqi_p], gcol.B, Brstd], writes=[Bqk])
                            dst = qT_d if which_p == 0 else kT_d
                            Bdst = BqT if which_p == 0 else BkT
                            S_.dma(lambda e, dst=dst, h_p=h_p, tok0=tok0: e.dma_start(out=dst[h_p, :, tok0:tok0 + 512], in_=qk_v[:, :]),
                                   reads=[Bqk], writes=[Bdst])
                        pend = (g[0], g[1], bz, gi % 2) if g is not None else None

                def sec_v():
                    S_.mark(f"p1_t{t}_v")
                    for j in range(4):
                        bv = nb()
                        for c in range(8):
                            S_.pe(lambda e, c=c, j=j, bv=bv: e.matmul(bank(bv)[:, 0:VW], lhsT=hT_v[:, c, j * 128:(j + 1) * 128],
                                                                      rhs=win_v[:, c, C_V:C_V + VW], start=(c == 0), stop=(c == 7)),
                                  reads=[BW, BhT], writes=[bankB[bv]])
                        copy_rr(vt_v[:, :], bank(bv)[:, 0:VW], [bankB[bv]], [Bvt])
                        S_.dma(lambda e, j=j, tok0=tok0: e.dma_start(out=v_d[tok0 + j * 128: tok0 + (j + 1) * 128, :], in_=vt_v[:, :]),
                               reads=[Bvt], writes=[Bv])


                def sec_gla():
                    S_.mark(f"p1_t{t}_gla")
                    ba = nb()
                    fm_group(C_GA, 16, ba, t)
                    S_.act(lambda e, ba=ba: e.copy(out=aT_v[0:16, :], in_=bank(ba)[0:16, :]), reads=[bankB[ba]], writes=[BaT])
                    S_.dve(lambda e: e.tensor_copy(out=aTh_v[0:16, :], in_=aT_v[0:16, :]), reads=[BaT], writes=[BaTh])
                    S_.dve(lambda e: e.tensor_tensor(out=aTl_v[0:16, :], in0=aT_v[0:16, :], in1=aTh_v[0:16, :], op=ALU.subtract),
                           reads=[BaT, BaTh], writes=[BaTl])
                    for j in range(4):
                        bk = 4 + j // BPB
                        passes = [(aTh_v, BaTh, aw_hi), (aTl_v, BaTl, aw_hi), (aTh_v, BaTh, aw_lo)]
                        for pi, (av, aB, wv) in enumerate(passes):
                            S_.pe(lambda e, j=j, bk=bk, av=av, wv=wv, pi=pi: e.matmul(
                                bank(bk)[:, (j % BPB) * GW:(j % BPB + 1) * GW],
                                lhsT=av[0:17, j * 128:(j + 1) * 128], rhs=wv.t[0:17, :],
                                start=(pi == 0), stop=(pi == 2)),
                                reads=[aB, wv.B], writes=[bankB[bk]])
                    for half in range(4 // BPB):
                        S_.act(lambda e, half=half: e.activation(out=la_v[:, BPB * half:BPB * half + BPB, :].rearrange("p j n -> p (j n)"),
                                                                 in_=bank(4 + half), func=AF.Exp, scale=-1.0),
                               reads=[bankB[4 + half]], writes=[Bla])
                    S_.act(lambda e: e.activation(out=la_v[:, :, :].rearrange("p j n -> p (j n)"),
                                                  in_=la_v[:, :, :].rearrange("p j n -> p (j n)"), func=AF.Ln, bias=onec.t[:, 0:1]),
                           reads=[Bla, onec.B], writes=[Bla])
                    S_.dve(lambda e: e.tensor_copy(out=lah_v[:, :, :], in_=la_v[:, :, :]), reads=[Bla], writes=[Blah])
                    S_.dve(lambda e: e.tensor_tensor(out=lal_v[:, :, :], in0=la_v[:, :, :], in1=lah_v[:, :, :], op=ALU.subtract),
                           reads=[Bla, Blah], writes=[Blal])

                def sec_gla_cumsum():
                    S_.mark(f"p1_t{t}_gla_cumsum")
                    for p in range(NP):
                        bk = 4 + p
                        for j in range(4):
                            for pi, (lv, lB) in enumerate([(lah_v, Blah), (lal_v, Blal)]):
                                S_.pe(lambda e, p=p, j=j, bk=bk, lv=lv, pi=pi: e.matmul(
                                    bank(bk)[:, j * 128:(j + 1) * 128],
                                    lhsT=lv[:, j, p * 128:(p + 1) * 128], rhs=triu_b.t[:, :],
                                    start=(pi == 0), stop=(pi == 1)),
                                    reads=[lB, triu_b.B], writes=[bankB[bk]])
                        S_.act(lambda e, p=p, bk=bk: e.activation(out=ebT_v[p][:, :], in_=bank(bk), func=AF.Exp, scale=-1.0 / 16),
                               reads=[bankB[bk]], writes=[BebT[p]])
                        S_.act(lambda e, p=p, bk=bk: e.activation(out=enbT_v[p][:, :], in_=bank(bk), func=AF.Exp, scale=1.0 / 16),
                               reads=[bankB[bk]], writes=[BenbT[p]])
                    for j in range(4):
                        bk = 4 + j // BPB
                        for pi, (lv, lB) in enumerate([(lah_v, Blah), (lal_v, Blal)]):
                            S_.pe(lambda e, j=j, bk=bk, lv=lv, pi=pi: e.matmul(
                                bank(bk)[:, (j % BPB) * GW:(j % BPB + 1) * GW],
                                lhsT=strl_b.t[:, :], rhs=lv[:, j, :], start=(pi == 0), stop=(pi == 1)),
                                reads=[lB, strl_b.B], writes=[bankB[bk]])
                    for half in range(4 // BPB):
                        S_.act(lambda e, half=half: e.activation(out=erev_v[:, BPB * half:BPB * half + BPB, :].rearrange("p j n -> p (j n)"),
                                                                 in_=bank(4 + half), func=AF.Exp, scale=-1.0 / 16),
                               reads=[bankB[4 + half]], writes=[Berev])

                def sec_gla_qin():
                    S_.mark(f"p1_t{t}_gla_qin")
                    for p in range(NP):
                        bq = nb()
                        fm_group(C_GQ + p * 128, 128, bq, t)
                        for hl in range(2):
                            S_.dve(lambda e, p=p, hl=hl, bq=bq: e.scalar_tensor_tensor(
                                out=qinm_v[:, p, hl, :], in0=bank(bq), scalar=hmask.t[:, hl:hl + 1],
                                in1=ebT_v[p][:, :], op0=ALU.mult, op1=ALU.mult),
                                reads=[bankB[bq], BebT[p], hmask.B], writes=[Bqin[p]])
                        bkk = nb()
                        fm_group(C_GK + p * 128, 128, bkk, t)
                        S_.dve(lambda e, p=p, bkk=bkk: e.tensor_tensor(out=kin_v[:, p, :], in0=bank(bkk), in1=enbT_v[p][:, :],
                                                                       op=ALU.mult),
                               reads=[bankB[bkk], BenbT[p]], writes=[Bkin[p]])
                    for j in range(4):
                        bv = nb()
                        for c in range(8):
                            S_.pe(lambda e, c=c, j=j, bv=bv: e.matmul(bank(bv)[:, 0:2 * GW], lhsT=hT_v[:, c, j * 128:(j + 1) * 128],
                                                                      rhs=win_v[:, c, C_GK:C_GK + 2 * GW], start=(c == 0), stop=(c == 7)),
                                  reads=[BW, BhT], writes=[bankB[bv]])
                        for cc in range(2):
                            S_.dve(lambda e, j=j, bv=bv, cc=cc: e.scalar_tensor_tensor(
                                out=kendm_v[:, j, cc, :], in0=bank(bv)[:, 0:GW], scalar=cmask.t[:, cc:cc + 1],
                                in1=erev_v[:, j, :], op0=ALU.mult, op1=ALU.mult),
                                reads=[bankB[bv], Berev, cmask.B], writes=[Bkend[j]])
                        S_.dve(lambda e, j=j, bv=bv: e.tensor_copy(out=vtok_v[:, j, :], in_=bank(bv)[:, GW:2 * GW]),
                               reads=[bankB[bv]], writes=[Bvtok[j]])
                        for hl in range(2):
                            S_.pool(lambda e, j=j, hl=hl: e.tensor_copy(
                                out=vpad_v[:, j, :, hl, hl * 64:(hl + 1) * 64],
                                in_=vtok_v[:, j, :].rearrange("p (a b e) -> p a b e", a=NP, b=2)[:, :, hl, :]),
                                reads=[Bvtok[j]], writes=[Bvpad[j]])

                def sec_gla_blocks():
                    S_.mark(f"p1_t{t}_gla_blocks")
                    for j in range(4):
                        bas = [nb() for _ in range(NP)]
                        for h in range(2 * NP):
                            p, hl = h // 2, h % 2
                            S_.pe(lambda e, p=p, hl=hl, j=j, bas=bas: e.matmul(
                                bank(bas[p])[:, hl * 128:(hl + 1) * 128],
                                lhsT=kin_v[:, p, j * 128:(j + 1) * 128],
                                rhs=qinm_v[:, p, hl, j * 128:(j + 1) * 128], start=True, stop=True),
                                reads=[Bkin[p], Bqin[p]], writes=[bankB[bas[p]]])
                        for p in range(NP):
                            S_.dve(lambda e, p=p, bas=bas: e.tensor_tensor(
                                out=am_v[:, 2 * p:2 * p + 2, :], in0=bank(bas[p])[:, 0:256].rearrange("p (h n) -> p h n", h=2),
                                in1=triu.t[:, :].unsqueeze(1).broadcast_to([128, 2, 128]), op=ALU.mult),
                                reads=[bankB[bas[p]], triu.B], writes=[Bam[p]])
                        for p in range(NP):
                            ob = bank(4 + p)[:, j * 128:(j + 1) * 128]
                            for hl in range(2):
                                S_.pe(lambda e, p=p, hl=hl, j=j, ob=ob: e.matmul(
                                    ob, lhsT=vpad_v[:, j, p, hl, :], rhs=am_v[:, 2 * p + hl, :], start=(hl == 0), stop=False),
                                    reads=[Bvpad[j], Bam[p]], writes=[bankB[4 + p]])
                        for cc in range(2):
                            ch = chunk_ctr[0]
                            chunk_ctr[0] += 1
                            sl = ch % NSL
                            S_.pool(lambda e, sl=sl: e.tensor_copy(out=Sbf_v[:, sl, :, :], in_=Sf_v[:, :, :]),
                                    reads=[BSf], writes=[BSbf[sl]])
                            for p in range(NP):
                                obc = bank(4 + p)[:, j * 128 + cc * 64: j * 128 + (cc + 1) * 64]
                                for hl in range(2):
                                    S_.pe(lambda e, p=p, hl=hl, j=j, cc=cc, sl=sl, obc=obc: e.matmul(
                                        obc, lhsT=Sbf_v[:, sl, p, :],
                                        rhs=qinm_v[:, p, hl, j * 128 + cc * 64: j * 128 + (cc + 1) * 64],
                                        start=False, stop=(cc == 1 and hl == 1)),
                                        reads=[BSbf[sl], Bqin[p]], writes=[bankB[4 + p]])
                            for p in range(NP):
                                S_.pe(lambda e, p=p, j=j, cc=cc: e.matmul(
                                    bank(6)[:, p * 128:(p + 1) * 128],
                                    lhsT=kendm_v[:, j, cc, p * 128:(p + 1) * 128],
                                    rhs=vtok_v[:, j, p * 128:(p + 1) * 128], start=True, stop=True),
                                    reads=[Bkend[j], Bvtok[j]], writes=[bankB[6]])
                            col = j * 128 + cc * 64 + 63
                            for p in range(NP):
                                for hl in range(2):
                                    S_.dve(lambda e, p=p, hl=hl, col=col: e.scalar_tensor_tensor(
                                        out=Sf_v[hl * 64:(hl + 1) * 64, p, hl * 64:(hl + 1) * 64],
                                        in0=Sf_v[hl * 64:(hl + 1) * 64, p, hl * 64:(hl + 1) * 64],
                                        scalar=ebT_v[p][hl * 64:(hl + 1) * 64, col:col + 1],
                                        in1=bank(6)[hl * 64:(hl + 1) * 64, p * 128 + hl * 64: p * 128 + (hl + 1) * 64],
                                        op0=ALU.mult, op1=ALU.add),
                                        reads=[BSf, BebT[p], bankB[6]], writes=[BSf])

                def sec_gla_out():
                    S_.mark(f"p1_t{t}_gla_out")
                    for p in range(NP):
                        bo = 4 + p
                        S_.act(lambda e, bo=bo: e.activation(out=sq_v[:, :], in_=bank(bo), func=AF.Square),
                               reads=[bankB[bo]], writes=[Bsq])
                        bm = nb()
                        S_.pe(lambda e, bm=bm: e.matmul(bank(bm), lhsT=blk64.t[:, :], rhs=sq_v[:, :], start=True, stop=True),
                              reads=[Bsq, blk64.B], writes=[bankB[bm]])
                        S_.act(lambda e, bm=bm: e.activation(out=rstd_v[:, :], in_=bank(bm), func=AF.Ln, bias=epsc.t[:, 0:1]),
                               reads=[bankB[bm], epsc.B], writes=[Brstd])
                        S_.act(lambda e: e.activation(out=rstd_v[:, :], in_=rstd_v[:, :], func=AF.Exp, scale=-0.5),
                               reads=[Brstd], writes=[Brstd])
                        S_.dve(lambda e, bo=bo: e.scalar_tensor_tensor(out=t1_v[:, :], in0=bank(bo), scalar=gng.t[:, 0:1],
                                                                       in1=rstd_v[:, :], op0=ALU.mult, op1=ALU.mult),
                               reads=[bankB[bo], gng.B, Brstd], writes=[Bt1])
                        bg = nb()
                        fm_group(C_GG + p * 128, 128, bg, t)
                        S_.act(lambda e, bg=bg: e.activation(out=sg_v[:, :], in_=bank(bg), func=AF.Silu),
                               reads=[bankB[bg]], writes=[Bsg])
                        S_.dve(lambda e: e.tensor_tensor(out=yg_v[:, :], in0=t1_v[:, :], in1=sg_v[:, :], op=ALU.mult),
                               reads=[Bt1, Bsg], writes=[Byg])
                        S_.dma(lambda e, p=p, tok0=tok0: e.dma_start(out=yT_d[NCV + NH + p, :, tok0:tok0 + 512], in_=yg_v[:, :]),
                               reads=[Byg], writes=[ByT])

                sec_gla()
                sec_conv()
                sec_gla_cumsum()
                sec_qk()
                if t + 1 < NT:
                    sec_norm(t + 1)
                sec_gla_qin()
                sec_v()
                sec_gla_blocks()
                sec_gla_out()


        def phase2(l):
            S_.mark(f"p2_{l}")
            S_.fence()
            a = ARENA
            o = 0

            def carve(n):
                nonlocal o
                v = a[:, o:o + n]
                o += n
                return v
            KT_v = carve(S // 2).bitcast(BF16)
            QT_v = carve(S).bitcast(BF16).rearrange("p (c n) -> p c n", c=2)
            V_v = carve(S // 2).bitcast(BF16).rearrange("p (b e) -> p b e", e=128)
            BKT, BQT, BV = Buf("KT"), Buf("QT"), Buf("V")
            prefetch_w = (o <= 16384) and not pair
            if prefetch_w:
                o = WW
                for c in range(6, 8):
                    S_.dma(lambda e, c=c: e.dma_start(out=w1_v[:, c, :], in_=w_mlp1[l, c * 128:(c + 1) * 128, :]),
                           writes=[BW1hi], q="pool")
                for c in range(32):
                    S_.dma(lambda e, c=c: e.dma_start(out=w2_v[:, c, :], in_=w_mlp2[l, c * 128:(c + 1) * 128, :]),
                           writes=[BW2], q="pool")
            PT_v = [carve(512).bitcast(BF16).rearrange("p (c n) -> p c n", c=2) for _ in range(2)]
            BPT = [Buf("PT0"), Buf("PT1")]
            BPT2 = [[Buf("PT0c0"), Buf("PT0c1")], [Buf("PT1c0"), Buf("PT1c1")]]
            r_v = carve(1024).rearrange("p (c n) -> p c n", c=2)
            Br = Buf("r")
            Oc_v = [carve(1024).rearrange("p (c n) -> p c n", c=2) for _ in range(2)]
            BOc = [Buf("Oc0"), Buf("Oc1")]
            sums_v = [carve(1024).rearrange("p (c n) -> p c n", c=2) for _ in range(2)]
            Bsums = [Buf("sums0"), Buf("sums1")]
            o_v = [carve(512) for _ in range(2)]
            Bo = [Buf("o0"), Buf("o1")]
            tile_ctr = [0]
            cur_it = [0]
            sq_v = carve(256).bitcast(BF16)
            Bsq = Buf("sq2")
            rstd_v = carve(512)
            Brstd = Buf("rstd2")
            y_v = carve(256).bitcast(BF16)
            By = Buf("y2")
            assert o <= AWORDS, o
            S_.dve(lambda e: e.memset(QT_v[64:128, 0, :], 0.0), writes=[BQT])
            S_.dve(lambda e: e.memset(QT_v[0:64, 1, :], 0.0), writes=[BQT])
            for h in range(NH):
                S_.dma(lambda e, h=h: e.dma_start(out=KT_v[:, :], in_=kT_d[h, :, :]), reads=[BkT], writes=[BKT])
                for comp in range(2):
                    S_.dma(lambda e, h=h, comp=comp: e.dma_start(out=QT_v[comp * 64:(comp + 1) * 64, comp, :],
                                                                 in_=qT_d[h, comp * 64:(comp + 1) * 64, :]),
                           reads=[BqT], writes=[BQT])
                S_.dma(lambda e, h=h: e.dma_start(out=V_v[:, :, :],
                                                  in_=v_d[:, h * 128:(h + 1) * 128].rearrange("(b p) e -> p b e", p=128)),
                       reads=[Bv], writes=[BV])
                THR = 50.0
                its = []
                for qt in range(NT):
                    q0 = qt * 512
                    kbs = []
                    for kb in range(4 * qt + 4):
                        dmin = q0 - (kb * 128 + 127)
                        if (not pair) and kb < 4 * qt and slopes[h] * dmin > THR:
                            continue
                        kbs.append(kb)
                    for n_, kb in enumerate(kbs):
                        its.append(dict(qt=qt, q0=q0, kb=kb, first=(n_ == 0), last=(n_ == len(kbs) - 1), idx=len(its)))

                def emit_qk(itx, h=h):
                    kb, qt, q0 = itx["kb"], itx["qt"], itx["q0"]
                    j = kb - 4 * qt
                    c0 = 0 if j < 0 else j * 128
                    sbk = (itx["idx"] % 2) * 2
                    diag = j >= 0
                    for comp in range(2):
                        S_.pe(lambda e, comp=comp: e.matmul(
                            bank(sbk + comp)[:, c0:512], lhsT=KT_v[:, kb * 128:(kb + 1) * 128],
                            rhs=QT_v[:, comp, q0 + c0:q0 + 512], start=True, stop=(not diag)),
                            reads=[BKT, BQT], writes=[bankB[sbk + comp]])
                        if diag:
                            S_.pe(lambda e, comp=comp: e.matmul(
                                bank(sbk + comp)[:, c0:c0 + 128], lhsT=ident.t[:, :], rhs=tdiag.t[:, h, :],
                                start=False, stop=True),
                                reads=[ident.B, tdiag.B], writes=[bankB[sbk + comp]])

                def emit_exp_pv(itx, h=h):
                    kb, qt, q0 = itx["kb"], itx["qt"], itx["q0"]
                    j = kb - 4 * qt
                    c0 = 0 if j < 0 else j * 128
                    sbk = (itx["idx"] % 2) * 2
                    pt = itx["idx"] % 2
                    di = (kb - 4 * qt) + 4 * (NT - 1)
                    first, last = itx["first"], itx["last"]
                    for comp in range(2):
                        S_.act(lambda e, comp=comp: e.activation(
                            out=PT_v[pt][:, comp, c0:512], in_=bank(sbk + comp)[:, c0:512],
                            func=AF.Exp, bias=kbias.t[:, h, di:di + 1], scale=1.0),
                            reads=[bankB[sbk + comp], kbias.B], writes=[BPT2[pt][comp]])
                    for comp in range(2):
                        S_.pe(lambda e, comp=comp: e.matmul(
                            bank(4 + comp)[:, c0:512], lhsT=V_v[:, kb, :], rhs=PT_v[pt][:, comp, c0:512],
                            start=first, stop=last),
                            reads=[BV, BPT2[pt][comp]], writes=[bankB[4 + comp]])
                        S_.pe(lambda e, comp=comp: e.matmul(
                            bank(6 + comp)[:, c0:512], lhsT=ones_bf.t[:, :], rhs=PT_v[pt][:, comp, c0:512],
                            start=first, stop=last),
                            reads=[ones_bf.B, BPT2[pt][comp]], writes=[bankB[6 + comp]])

                def stage_a(itx, h=h):
                    ab = itx["tile"] % 2
                    S_.dve(lambda e: e.tensor_copy(out=Oc_v[ab][:, :, :],
                                                   in_=PSf[:, 4 * 512:6 * 512].rearrange("p (c n) -> p c n", c=2)),
                           reads=[bankB[4], bankB[5]], writes=[BOc[ab]])
                    S_.dve(lambda e: e.tensor_copy(out=sums_v[ab][:, :, :],
                                                   in_=PSf[:, 6 * 512:8 * 512].rearrange("p (c n) -> p c n", c=2)),
                           reads=[bankB[6], bankB[7]], writes=[Bsums[ab]])

                def stage_b(itx, h=h):
                    ab = itx["tile"] % 2
                    S_.dve(lambda e: e.reciprocal(out=r_v[:, :, :], in_=sums_v[ab][:, :, :]), reads=[Bsums[ab]], writes=[Br])
                    S_.dve(lambda e: e.tensor_tensor(out=Oc_v[ab][:, :, :], in0=Oc_v[ab][:, :, :], in1=r_v[:, :, :], op=ALU.mult),
                           reads=[BOc[ab], Br], writes=[BOc[ab]])
                    S_.dve(lambda e: e.scalar_tensor_tensor(out=o_v[ab][:, :], in0=Oc_v[ab][:, 1, :], scalar=nlam.t[:, 0:1],
                                                            in1=Oc_v[ab][:, 0, :], op0=ALU.mult, op1=ALU.add),
                           reads=[BOc[ab], nlam.B], writes=[Bo[ab]])
                    S_.dve(lambda e: e.tensor_tensor(out=sq_v[:, :], in0=o_v[ab][:, :], in1=o_v[ab][:, :], op=ALU.mult),
                           reads=[Bo[ab]], writes=[Bsq])

                def stage_c(itx, h=h):
                    ab = itx["tile"] % 2
                    q0 = itx["q0"]
                    sbk = (cur_it[0] % 2) * 2
                    S_.pe(lambda e: e.matmul(bank(sbk), lhsT=o128.t[:, :], rhs=sq_v[:, :], start=True, stop=True),
                          reads=[o128.B, Bsq], writes=[bankB[sbk]])
                    S_.act(lambda e: e.activation(out=rstd_v[:, :], in_=bank(sbk), func=AF.Ln, bias=epsc.t[:, 0:1]),
                           reads=[bankB[sbk], epsc.B], writes=[Brstd])
                    S_.act(lambda e: e.activation(out=rstd_v[:, :], in_=rstd_v[:, :], func=AF.Exp, scale=-0.5),
                           reads=[Brstd], writes=[Brstd])
                    S_.dve(lambda e: e.scalar_tensor_tensor(out=y_v[:, :], in0=o_v[ab][:, :], scalar=subg.t[:, 0:1], in1=rstd_v[:, :],
                                                            op0=ALU.mult, op1=ALU.mult),
                           reads=[Bo[ab], subg.B, Brstd], writes=[By])
                    S_.dma(lambda e: e.dma_start(out=yT_d[NCV + h, :, q0:q0 + 512], in_=y_v[:, :]),
                           reads=[By], writes=[ByT])

                for itx in its:
                    if itx["first"]:
                        tile_ctr[0] += 1
                    itx["tile"] = tile_ctr[0]
                pending = []

                def tick():
                    for p_ in pending:
                        p_[0] -= 1
                    while pending and pending[0][0] <= 0:
                        _, fn_, it_ = pending.pop(0)
                        fn_(it_)

                emit_qk(its[0])
                for i_, itx in enumerate(its):
                    if i_ + 1 < len(its):
                        emit_qk(its[i_ + 1])
                    emit_exp_pv(itx)
                    cur_it[0] = itx["idx"]
                    tick()
                    if itx["last"]:
                        stage_a(itx)
                        pending.append([1, stage_b, itx])
                        pending.append([5, stage_c, itx])
                while pending:
                    _, fn_, it_ = pending.pop(0)
                    fn_(it_)

        def phase3(l, x_src, Bx_src, x_dst, Bx_dst, is_out):
            TT = 256
            NJ = TT // 128
            S_.mark(f"p3_{l}")
            S_.fence()
            S_.dma(lambda e: e.dma_start(out=gb2.t[:], in_=ln2_g[l:l + 1, :].partition_broadcast(128)), writes=[gb2.B])
            pre = (S // 2 + S + S // 2 <= 16384) and not pair and stop_after is None
            for c in range(8):
                S_.dma(lambda e, c=c: e.dma_start(out=wout_v[:, c, :], in_=w_out[l, c * 128:(c + 1) * 128, :]),
                       writes=[BWo], q="pool")
            for c in range(8):
                if pre and c >= 6:
                    continue
                S_.dma(lambda e, c=c: e.dma_start(out=w1_v[:, c, :], in_=w_mlp1[l, c * 128:(c + 1) * 128, :]),
                       writes=[BW1lo if c < 6 else BW1hi], q="pool")
            if not pre:
                for c in range(32):
                    S_.dma(lambda e, c=c: e.dma_start(out=w2_v[:, c, :], in_=w_mlp2[l, c * 128:(c + 1) * 128, :]),
                           writes=[BW2], q="pool")
            a = ARENA
            o = WW

            def carve(n):
                nonlocal o
                v = a[:, o:o + n]
                o += n
                return v
            yT_v = [carve(8 * TT // 2).bitcast(BF16).rearrange("p (c n) -> p c n", c=8)] * 2
            ByT_s = [Buf("yTs0")] * 2
            if pair:
                yTb_v = carve(8 * TT // 2).bitcast(BF16).rearrange("p (c n) -> p c n", c=8)
                ByTb = Buf("yTb")
            x_v = [carve(NJ * D).rearrange("p (j n) -> p j n", j=NJ) for _ in range(2)]
            Bx_s = [Buf("xs0"), Buf("xs1")]
            hb_v = carve(NJ * D // 2).bitcast(BF16).rearrange("p (j n) -> p j n", j=NJ)
            Bhb = Buf("hb3")
            hT_v = carve(8 * TT // 2).bitcast(BF16).rearrange("p (c n) -> p c n", c=8)
            BhT = Buf("hT3")
            aT_v = carve(32 * TT // 2).bitcast(BF16).rearrange("p (f n) -> p f n", f=32)
            BaT = [Buf(f"aT{f}") for f in range(32)]
            ss_v = carve(NJ)
            rs_v = carve(NJ)
            Bss, Brs = Buf("ss3"), Buf("rs3")
            relu_v = [carve(TT) for _ in range(2)]
            Brelu = [Buf("relu0"), Buf("relu1")]
            assert o <= AWORDS, o
            rr = [0]

            def nb():
                k = rr[0] % 7
                rr[0] += 1
                return k
            cp = [0]
            NTT = SP // TT

            def st_load(t):
                tok0 = t * TT
                s = t % 2
                if pair:
                    S_.dma(lambda e: e.dma_start(out=yT_v[s][:, :, :],
                                                 in_=yTall_d[:, :, tok0:tok0 + TT].rearrange("c p n -> p c n")),
                           reads=[ByTall], writes=[ByT_s[s]])
                    S_.dma(lambda e: e.dma_start(out=yTb_v[:, :, :],
                                                 in_=yTall_d[:, :, SP + tok0:SP + tok0 + TT].rearrange("c p n -> p c n")),
                           reads=[ByTall], writes=[ByTb])
                    S_.dve(lambda e: e.tensor_scalar_mul(out=yT_v[s][:, :, :], in0=yT_v[s][:, :, :], scalar1=rmask.t[:, 0:1]),
                           reads=[ByT_s[s], rmask.B], writes=[ByT_s[s]])
                    S_.dve(lambda e: e.scalar_tensor_tensor(out=yT_v[s][:, :, :], in0=yTb_v[:, :, :], scalar=rmask.t[:, 1:2],
                                                            in1=yT_v[s][:, :, :], op0=ALU.mult, op1=ALU.add),
                           reads=[ByT_s[s], ByTb, rmask.B], writes=[ByT_s[s]])
                else:
                    S_.dma(lambda e: e.dma_start(out=yT_v[s][:, :, :],
                                                 in_=yT_d[:, :, tok0:tok0 + TT].rearrange("c p n -> p c n")),
                           reads=[ByT], writes=[ByT_s[s]])
                S_.dma(lambda e: e.dma_start(out=x_v[s][:, :, :],
                                             in_=x_src[tok0:tok0 + TT, :].rearrange("(j p) n -> p j n", p=128)),
                       reads=[Bx_src], writes=[Bx_s[s]])

            def st_outproj(t):
                s = t % 2
                for j in range(NJ):
                    for n in range(2):
                        bk = nb()
                        for c in range(8):
                            S_.pe(lambda e, c=c, j=j, n=n, bk=bk: e.matmul(
                                bank(bk), lhsT=yT_v[s][:, c, j * 128:(j + 1) * 128], rhs=wout_v[:, c, n * 512:(n + 1) * 512],
                                start=(c == 0), stop=(c == 7)), reads=[ByT_s[s], BWo], writes=[bankB[bk]])
                        S_.dve(lambda e, j=j, n=n, bk=bk: e.tensor_tensor(
                            out=x_v[s][:, j, n * 512:(n + 1) * 512], in0=bank(bk), in1=x_v[s][:, j, n * 512:(n + 1) * 512],
                            op=ALU.add), reads=[bankB[bk], Bx_s[s]], writes=[Bx_s[s]])

            def st_norm(t):
                s = t % 2
                for j in range(NJ):
                    S_.act(lambda e, j=j: e.activation(out=hb_v[:, j, :], in_=x_v[s][:, j, :], func=AF.Square,
                                                       accum_out=ss_v[:, j:j + 1]),
                           reads=[Bx_s[s]], writes=[Bhb, Bss])
                S_.act(lambda e: e.activation(out=rs_v[:, :], in_=ss_v[:, :], func=AF.Ln, bias=epsc.t[:, 0:1], scale=1.0 / D),
                       reads=[Bss, epsc.B], writes=[Brs])
                S_.act(lambda e: e.activation(out=rs_v[:, :], in_=rs_v[:, :], func=AF.Exp, scale=-0.5),
                       reads=[Brs], writes=[Brs])
                for j in range(NJ):
                    S_.dve(lambda e, j=j: e.scalar_tensor_tensor(out=hb_v[:, j, :], in0=x_v[s][:, j, :],
                                                                 scalar=rs_v[:, j:j + 1], in1=gb2.t[:, :],
                                                                 op0=ALU.mult, op1=ALU.mult),
                           reads=[Bx_s[s], Brs, gb2.B], writes=[Bhb])

            def st_transpose(t):
                for c4 in range(2):
                    for cc_ in range(4):
                        c = c4 * 4 + cc_
                        for j in range(NJ):
                            S_.pe(lambda e, c=c, j=j, cc_=cc_: e.transpose(
                                out=PSb[:, cc_ * TT + j * 128: cc_ * TT + (j + 1) * 128],
                                in_=hb_v[:, j, c * 128:(c + 1) * 128], identity=ident.t[:, :]),
                                reads=[Bhb, ident.B], writes=[bankB[7]])
                    cp[0] += 1
                    if cp[0] % 2 == 0:
                        S_.act(lambda e, c4=c4: e.copy(out=hT_v[:, c4 * 4:c4 * 4 + 4, :],
                                                       in_=PSb[:, :].rearrange("p (c n) -> p c n", c=4)),
                               reads=[bankB[7]], writes=[BhT])
                    else:
                        S_.dve(lambda e, c4=c4: e.tensor_copy(out=hT_v[:, c4 * 4:c4 * 4 + 4, :],
                                                              in_=PSb[:, :].rearrange("p (c n) -> p c n", c=4)),
                               reads=[bankB[7]], writes=[BhT])

            def st_hidden(t):
                for f in range(32):
                    bk = nb()
                    for c in range(8):
                        S_.pe(lambda e, c=c, f=f, bk=bk: e.matmul(
                            bank(bk)[:, 0:TT], lhsT=w1_v[:, c, f * 128:(f + 1) * 128], rhs=hT_v[:, c, :],
                            start=(c == 0), stop=(c == 7)), reads=[BW1lo, BW1hi, BhT], writes=[bankB[bk]])
                    rt = f % 2
                    S_.act(lambda e, rt=rt, bk=bk: e.activation(out=relu_v[rt][:, :], in_=bank(bk)[:, 0:TT], func=AF.Relu),
                           reads=[bankB[bk]], writes=[Brelu[rt]])
                    if f % 2 == 0:
                        S_.dve(lambda e, f=f, rt=rt: e.tensor_tensor(out=aT_v[:, f, :], in0=relu_v[rt][:, :], in1=relu_v[rt][:, :],
                                                                     op=ALU.mult), reads=[Brelu[rt]], writes=[BaT[f]])
                    else:
                        S_.pool(lambda e, f=f, rt=rt: e.tensor_tensor(out=aT_v[:, f, :], in0=relu_v[rt][:, :], in1=relu_v[rt][:, :],
                                                                      op=ALU.mult), reads=[Brelu[rt]], writes=[BaT[f]])

            def st_down(t, j):
                s = t % 2
                for n in range(2):
                    bk = nb()
                    for f in range(32):
                        S_.pe(lambda e, f=f, n=n, bk=bk: e.matmul(
                            bank(bk), lhsT=aT_v[:, f, j * 128:(j + 1) * 128], rhs=w2_v[:, f, n * 512:(n + 1) * 512],
                            start=(f == 0), stop=(f == 31)), reads=[BaT[f], BW2], writes=[bankB[bk]])
                    S_.dve(lambda e, n=n, bk=bk: e.tensor_tensor(
                        out=x_v[s][:, j, n * 512:(n + 1) * 512], in0=bank(bk), in1=x_v[s][:, j, n * 512:(n + 1) * 512],
                        op=ALU.add), reads=[bankB[bk], Bx_s[s]], writes=[Bx_s[s]])

            def st_store(t):
                tok0 = t * TT
                s = t % 2
                op = S_.dma(lambda e: e.dma_start(
                    out=x_dst[tok0:tok0 + TT, :].rearrange("(j p) n -> p j n", p=128), in_=x_v[s][:, :, :]),
                    reads=[Bx_s[s]], writes=[Bx_dst])
                if is_out:
                    out_ops.append(op)

            st_load(0)
            st_outproj(0)
            st_norm(0)
            st_transpose(0)
            for t in range(NTT):
                if t + 1 < NTT:
                    st_load(t + 1)
                st_hidden(t)
                if t + 1 < NTT:
                    st_outproj(t + 1)
                    st_norm(t + 1)
                st_down(t, 0)
                if t + 1 < NTT:
                    st_transpose(t + 1)
                for j in range(1, NJ):
                    st_down(t, j)
                st_store(t)

        for l in range(DEPTH):
            load_params(l)
            x_src, Bsrc = (x_in, Buf("xin")) if l == 0 else (xs_d, Bxs)
            last = (l == DEPTH - 1)
            phase1(l, x_src, Bsrc)
            if stop_after == "p1":
                break
            phase2(l)
            if stop_after == "p2":
                break
            if pair:
                S_.dma(lambda e: e.collective_compute(
                    "AllGather", ALU.bypass, replica_groups=groups,
                    ins=[yT_d.rearrange("c p n -> (c p) n")], outs=[yTall_d.rearrange("c p n -> (c p) n")]),
                    reads=[ByT], writes=[ByTall], q="pool", ring="cc")
                x3_src, B3src = (xh_in, Buf("xhin")) if l == 0 else (xm_d, Bxm)
                x_dst, Bdst = (out_d, Bout) if last else (xm_d, Bxm)
                phase3(l, x3_src, B3src, x_dst, Bdst, last)
                if not last:
                    S_.dma(lambda e: e.collective_compute(
                        "AllGather", ALU.bypass, replica_groups=groups, ins=[xm_d[:, :]], outs=[xs_d[:, :]]),
                        reads=[Bxm], writes=[Bxs], q="pool", ring="cc")
            else:
                x_dst, Bdst = (out_d, Bout) if last else (xs_d, Bxs)
                phase3(l, x_src, Bsrc, x_dst, Bdst, last)

        if trunc is not None:
            S_.ops = S_.ops[:trunc]
            out_ops = []
            for en in ENGS:
                idxs = [i for i, o_ in enumerate(S_.ops) if o_[0] == en]
                if idxs:
                    out_ops.append(idxs[-1])
        if not out_ops:
            out_ops.append(len(S_.ops) - 1)
        S_.emit(nc, final_wait_ops=out_ops)
    return nc, S_


def _pair_layout(r, x_b, p):
    S = x_b.shape[0]
    SP = S // 2
    cols = np.concatenate([
        np.arange(0 + r * 128, 0 + (r + 1) * 128),
        np.arange(256 + r * 128, 256 + (r + 1) * 128),
        np.arange(512 + r * 128, 512 + (r + 1) * 128),
        np.arange(768 + 2 * r * 128, 768 + (2 * r + 2) * 128),
        np.arange(1280 + 2 * r * 128, 1280 + (2 * r + 2) * 128),
        np.arange(1792 + 2 * r * 128, 1792 + (2 * r + 2) * 128),
        np.arange(2304 + r * 128, 2304 + (r + 1) * 128),
        np.arange(2560 + r * 128, 2560 + (r + 1) * 128),
        np.arange(2816 + r * 128, 2816 + (r + 1) * 128),
        np.arange(3072 + r * 128, 3072 + (r + 1) * 128),
        np.arange(3328, 3344),
    ])
    rows = np.concatenate([
        np.concatenate([np.arange(i * 128, (i + 1) * 128),
                        np.arange(256 + 2 * i * 128, 256 + (2 * i + 2) * 128),
                        np.arange(768 + i * 128, 768 + (i + 1) * 128)]) for i in range(2)])
    slopes = np.array([[2.0 ** (-8.0 * (2 * r + hh + 1) / 4) for hh in range(2)]], dtype=np.float32)
    m = dict(p)
    m["x"] = np.ascontiguousarray(x_b)
    m["x_half"] = np.ascontiguousarray(x_b[r * SP:(r + 1) * SP])
    m["w_in"] = np.ascontiguousarray(p["w_in"][:, :, cols])
    m["conv_w"] = np.ascontiguousarray(p["conv_w"][:, :, r * 128:(r + 1) * 128])
    m["gla_alpha_w"] = np.ascontiguousarray(p["gla_alpha_w"][:, :, r * 128:(r + 1) * 128])
    m["gla_alpha_b"] = np.ascontiguousarray(p["gla_alpha_b"][:, r * 128:(r + 1) * 128])
    m["w_out"] = np.ascontiguousarray(p["w_out"][:, rows, :])
    m["slopes"] = slopes
    m["rmask"] = np.array([[1.0 - r, float(r)]], dtype=np.float32)
    return m


def kernel(x, ln1_g, w_in, conv_w, q_norm_g, k_norm_g, diff_lambda, diff_subln_g,
           gla_alpha_w, gla_alpha_b, gla_norm_g, w_out, ln2_g, w_mlp1, w_mlp2):
    x = np.asarray(x, dtype=np.float32)
    B, S, _ = x.shape
    depth = int(np.asarray(ln1_g).shape[0])
    nc, _ = build(S=S, DEPTH=depth)
    shared = dict(ln1_g=ln1_g, w_in=w_in, conv_w=conv_w, q_norm_g=q_norm_g, k_norm_g=k_norm_g,
                  diff_lambda=diff_lambda, diff_subln_g=diff_subln_g, gla_alpha_w=gla_alpha_w,
                  gla_alpha_b=gla_alpha_b, gla_norm_g=gla_norm_g, w_out=w_out, ln2_g=ln2_g,
                  w_mlp1=w_mlp1, w_mlp2=w_mlp2)
    shared = {k: np.ascontiguousarray(np.asarray(v, dtype=np.float32)) for k, v in shared.items()}
    in_maps = []
    for c in range(B):
        m = dict(shared)
        m["x"] = np.ascontiguousarray(x[c])
        in_maps.append(m)
    res = run_bass_kernel_spmd(nc, in_maps, core_ids=list(range(B)))
    return np.stack([np.asarray(r["out"], dtype=np.float32) for r in res.results], axis=0)
```

```python
import math
from contextlib import ExitStack
import numpy as np
import concourse.bass as bass
import concourse.mybir as mybir
from concourse.bass_utils import run_bass_kernel_spmd

F32 = mybir.dt.float32
BF16 = mybir.dt.bfloat16
I32 = mybir.dt.int32
ALU = mybir.AluOpType
AF = mybir.ActivationFunctionType
AX = mybir.AxisListType

ENGS = ("pe", "act", "dve", "pool", "sp")
SEM_EPOCH = 24000
DMA_RING = 8

D = 1024
DIN = 3344
DFF = 4096
EPS = 1e-6
NEG = -30000.0


class Buf:
    __slots__ = ("name", "w", "r")

    def __init__(self, name=""):
        self.name = name
        self.w = None
        self.r = []


class Sched:
    def __init__(self):
        self.ops = []
        self.sameeng_sync = {"act": True, "dve": True, "pool": True, "pe": False, "sp": False}
        self.marks = []
        self.fence_deps = None
        self.fence_id = 0
        self.fence_passed = {}
        self.last_ops = {e: [] for e in ENGS}

    def mark(self, name):
        self.marks.append((name, len(self.ops)))

    def fence(self):
        deps = set()
        for e in ENGS:
            deps.update(self.last_ops[e][-(DMA_RING + 1):])
        self.fence_deps = deps
        self.fence_id += 1

    def add(self, eng, fn, reads=(), writes=(), dma=False, ring="d"):
        idx = len(self.ops)
        deps = set()
        if self.fence_deps is not None and self.fence_passed.get(eng) != self.fence_id:
            deps.update(self.fence_deps)
            self.fence_passed[eng] = self.fence_id
        self.last_ops[eng].append(idx)
        if len(self.last_ops[eng]) > 4 * DMA_RING:
            del self.last_ops[eng][:-2 * DMA_RING]
        for b in reads:
            if b.w is not None:
                deps.add(b.w)
        for b in writes:
            if b.w is not None:
                deps.add(b.w)
            for r in b.r:
                deps.add(r)
        for b in reads:
            b.r.append(idx)
        for b in writes:
            b.w = idx
            b.r = []
        deps.discard(idx)
        self.ops.append([eng, fn, deps, dma, ring])
        return idx

    def pe(self, fn, reads=(), writes=()):
        return self.add("pe", fn, reads, writes)

    def act(self, fn, reads=(), writes=()):
        return self.add("act", fn, reads, writes)

    def dve(self, fn, reads=(), writes=()):
        return self.add("dve", fn, reads, writes)

    def pool(self, fn, reads=(), writes=()):
        return self.add("pool", fn, reads, writes)

    def dma(self, fn, reads=(), writes=(), q="sp", ring="d"):
        return self.add(q, fn, reads, writes, dma=True, ring=ring)

    def emit(self, nc, final_wait_ops=()):
        ops = self.ops
        n = len(ops)
        signal = [False] * n
        for i, (eng, fn, deps, dma, _rg) in enumerate(ops):
            if dma:
                signal[i] = True
            for d in deps:
                if ops[d][3]:
                    continue
                if ops[d][0] != eng or self.sameeng_sync.get(eng, False):
                    signal[d] = True
        for i in final_wait_ops:
            signal[i] = True
        sem_of = [None] * n
        sem_specs = []
        cur = {e: None for e in ENGS}
        dma_ring = {}
        dma_rr = {}
        dma_prev = [None] * n
        ring_last = {}
        for i, (eng, fn, deps, dma, rg) in enumerate(ops):
            if not signal[i]:
                continue
            if dma:
                rk = (eng, rg)
                ring = dma_ring.setdefault(rk, [])
                nslots = DMA_RING if rg == "d" else 1
                slot = dma_rr.get(rk, 0) % nslots
                dma_rr[rk] = dma_rr.get(rk, 0) + 1
                if len(ring) <= slot:
                    sem_specs.append(f"{rg}_{eng}_{slot}_{len(sem_specs)}")
                    ring.append([len(sem_specs) - 1, 0])
                if ring[slot][1] + 16 > SEM_EPOCH:
                    sem_specs.append(f"{rg}_{eng}_{slot}_{len(sem_specs)}")
                    ring[slot] = [len(sem_specs) - 1, 0]
                ring[slot][1] += 16
                sem_of[i] = (ring[slot][0], ring[slot][1])
                key = (eng, rg, slot)
                dma_prev[i] = ring_last.get(key)
                ring_last[key] = i
            else:
                c = cur[eng]
                if c is None or c[1] + 1 > SEM_EPOCH:
                    sem_specs.append(f"s_{eng}_{len(sem_specs)}")
                    c = [len(sem_specs) - 1, 0]
                    cur[eng] = c
                c[1] += 1
                sem_of[i] = (c[0], c[1])
        self.n_sems = len(sem_specs)
        per_eng = {e: [] for e in ENGS}
        for i, op in enumerate(ops):
            per_eng[op[0]].append(i)
        self.stats = {e: len(per_eng[e]) for e in ENGS}
        self.stats["signals"] = sum(signal)

        with ExitStack() as st:
            sems = [st.enter_context(nc.semaphore(nm)) for nm in sem_specs]
            block = st.enter_context(nc.Block())

            def make(engname):
                idxs = per_eng[engname]

                def body(e):
                    seen = {}
                    nwait = 0
                    for i in idxs:
                        eng, fn, deps, dma, _rg = ops[i]
                        waits = {}
                        dl = list(deps)
                        if dma and dma_prev[i] is not None:
                            dl.append(dma_prev[i])
                        for d in dl:
                            if (not ops[d][3]) and ops[d][0] == eng and not self.sameeng_sync.get(eng, False):
                                continue
                            s, v = sem_of[d]
                            if seen.get(s, 0) >= v:
                                continue
                            if waits.get(s, 0) < v:
                                waits[s] = v
                        for s, v in waits.items():
                            e.wait_ge(sems[s], v)
                            seen[s] = v
                            nwait += 1
                        ins = fn(e)
                        if signal[i]:
                            s, v = sem_of[i]
                            ins.then_inc(sems[s], 16 if dma else 1)
                    if engname == "sp":
                        for i in final_wait_ops:
                            s, v = sem_of[i]
                            e.wait_ge(sems[s], v)
                    self.stats["waits_" + engname] = nwait
                return body

            block.tensor(make("pe"))
            block.scalar(make("act"))
            block.vector(make("dve"))
            block.gpsimd(make("pool"))
            block.sync(make("sp"))


class Tl:
    def __init__(self, t, nb=1, name=""):
        self.t = t
        self.b = [Buf(f"{name}{i}") for i in range(nb)]

    @property
    def B(self):
        return self.b[0]


def build(S=8192, DEPTH=4, stop_after=None, debug=False, trunc=None, pair=False, ncores=4):
    NT = S // 512
    NB = S // 128
    nc = bass.Bass("TRN2", target_bir_lowering=False)
    if pair:
        NCV, NH, NP = 1, 2, 1
        C_U, C_CB, C_CC, C_Q, C_K, C_V, C_GQ, C_GK, C_GG, C_GA = 0, 128, 256, 384, 640, 896, 1152, 1280, 1536, 1664
        DINL = 1680
        SP = S // 2
    else:
        NCV, NH, NP = 2, 4, 2
        C_U, C_CB, C_CC, C_Q, C_K, C_V, C_GQ, C_GK, C_GG, C_GA = 0, 256, 512, 768, 1280, 1792, 2304, 2560, 3072, 3328
        DINL = DIN
        SP = S
    GW = NP * 128
    VW = NH * 128
    NYC = NCV + NH + NP
    BPB = 512 // GW

    def din(name, shape):
        return nc.dram_tensor(name, shape, F32, kind="ExternalInput").ap()

    x_in = din("x", [S, D])
    if pair:
        xh_in = din("x_half", [SP, D])
        slopes_in = din("slopes", [1, NH])
        rmask_in = din("rmask", [1, 2])
    ln1_g = din("ln1_g", [DEPTH, D])
    w_in = din("w_in", [DEPTH, D, DINL])
    conv_w = din("conv_w", [DEPTH, 3, NCV * 128])
    q_norm_g = din("q_norm_g", [DEPTH, 64])
    k_norm_g = din("k_norm_g", [DEPTH, 64])
    diff_lambda = din("diff_lambda", [DEPTH, 4, 64])
    diff_subln_g = din("diff_subln_g", [DEPTH, 128])
    gla_alpha_w = din("gla_alpha_w", [DEPTH, 16, GW])
    gla_alpha_b = din("gla_alpha_b", [DEPTH, GW])
    gla_norm_g = din("gla_norm_g", [DEPTH, 64])
    w_out = din("w_out", [DEPTH, D, D])
    ln2_g = din("ln2_g", [DEPTH, D])
    w_mlp1 = din("w_mlp1", [DEPTH, D, DFF])
    w_mlp2 = din("w_mlp2", [DEPTH, DFF, D])
    out_d = nc.dram_tensor("out", [SP, D], F32, kind="ExternalOutput").ap()

    xs_d = nc.dram_tensor("xs_scr", [S, D], F32, kind="Internal").ap()
    sk = "ExternalOutput" if debug else "Internal"
    qT_d = nc.dram_tensor("qT_scr", [NH, 128, S], BF16, kind=sk).ap()
    kT_d = nc.dram_tensor("kT_scr", [NH, 128, S], BF16, kind=sk).ap()
    v_d = nc.dram_tensor("v_scr", [S, VW], BF16, kind=sk).ap()
    yT_d = nc.dram_tensor("yT_scr", [NYC, 128, S], BF16, kind=sk).ap()
    if pair:
        xm_d = nc.dram_tensor("xm_scr", [SP, D], F32, kind="Internal").ap()
        yTall_d = nc.dram_tensor("yTall_scr", [2 * NYC, 128, S], BF16, kind="Internal").ap()
        Bxm, ByTall = Buf("xm"), Buf("yTall")
        groups = [[2 * i, 2 * i + 1] for i in range(ncores // 2)]
    Bxs, BqT, BkT, Bv, ByT, Bout = Buf("xs"), Buf("qT"), Buf("kT"), Buf("v"), Buf("yT"), Buf("out")

    S_ = Sched()
    out_ops = []

    with ExitStack() as st:
        def sb(name, shape, dt, nb=1):
            return Tl(st.enter_context(nc.sbuf_tensor(name, shape, dt)), nb, name)

        PSf = st.enter_context(nc.psum_tensor("psf", [128, 8 * 512], F32))
        bankB = [Buf(f"bank{i}") for i in range(8)]

        def bank(i):
            return PSf[:, i * 512:(i + 1) * 512]

        PSb = bank(7).bitcast(BF16)

        ident = sb("ident", [128, 128], BF16)
        ones_bf = sb("ones_bf", [128, 128], BF16)
        blk64 = sb("blk64", [128, 128], BF16)
        o128 = sb("o128", [128, 128], BF16)
        triu = sb("triu", [128, 128], F32)
        strl = sb("strl", [128, 128], F32)
        triu_b = sb("triu_b", [128, 128], BF16)
        strl_b = sb("strl_b", [128, 128], BF16)
        iot = sb("iot", [128, 128], I32)
        dkq = sb("dkq", [128, 128], F32)
        tdiag = sb("tdiag", [128, 4, 128], BF16)
        kcol = sb("kcol", [128, 1], F32)
        kcoli = sb("kcoli", [128, 1], I32)
        NDEL = 4 * (NT - 1) + 4
        kbias = sb("kbias", [128, 4, NDEL], F32)
        kdel = sb("kdel", [128, NDEL], F32)
        slopec = sb("slopec", [128, 4], F32)
        slopes = [2.0 ** (-8.0 * (h + 1) / 4) for h in range(4)]
        if pair:
            rmask = sb("rmask_sb", [128, 2], F32)
            S_.dma(lambda e: e.dma_start(out=slopec.t[:, 0:NH], in_=slopes_in[0:1, :].partition_broadcast(128)), writes=[slopec.B])
            S_.dma(lambda e: e.dma_start(out=rmask.t[:, :], in_=rmask_in[0:1, :].partition_broadcast(128)), writes=[rmask.B])
        else:
            for h in range(4):
                S_.pool(lambda e, h=h: e.memset(slopec.t[:, h:h + 1], slopes[h]), writes=[slopec.B])

        hmask = sb("hmask", [128, 2], F32)
        cmask = sb("cmask", [128, 2], F32)
        S_.pool(lambda e: e.memset(hmask.t[:], 0.0), writes=[hmask.B])
        S_.pool(lambda e: e.memset(hmask.t[0:64, 0:1], 0.125), writes=[hmask.B])
        S_.pool(lambda e: e.memset(hmask.t[64:128, 1:2], 0.125), writes=[hmask.B])
        S_.pool(lambda e: e.memset(cmask.t[:], 0.0), writes=[cmask.B])
        S_.pool(lambda e: e.memset(cmask.t[0:64, 0:1], 1.0), writes=[cmask.B])
        S_.pool(lambda e: e.memset(cmask.t[64:128, 1:2], 1.0), writes=[cmask.B])
        epsc = sb("epsc", [128, 1], F32)
        onec = sb("onec", [128, 1], F32)
        S_.pool(lambda e: e.memset(epsc.t[:], EPS), writes=[epsc.B])
        S_.pool(lambda e: e.memset(onec.t[:], 1.0), writes=[onec.B])
        S_.pool(lambda e: e.memset(ident.t[:], 0.0), writes=[ident.B])
        S_.pool(lambda e: e.affine_select(out=ident.t[:], in_=ident.t[:], pattern=[[-1, 128]],
                                          compare_op=ALU.not_equal, fill=1.0, base=0, channel_multiplier=1),
                reads=[ident.B], writes=[ident.B])
        S_.pool(lambda e: e.memset(ones_bf.t[:], 1.0), writes=[ones_bf.B])
        S_.pool(lambda e: e.memset(o128.t[:], 1.0 / 128), writes=[o128.B])
        S_.pool(lambda e: e.memset(blk64.t[:], 1.0 / 64), writes=[blk64.B])
        S_.pool(lambda e: e.memset(blk64.t[0:64, 64:128], 0.0), writes=[blk64.B])
        S_.pool(lambda e: e.memset(blk64.t[64:128, 0:64], 0.0), writes=[blk64.B])
        S_.pool(lambda e: e.memset(triu.t[:], 1.0), writes=[triu.B])
        S_.pool(lambda e: e.affine_select(out=triu.t[:], in_=triu.t[:], pattern=[[1, 128]],
                                          compare_op=ALU.is_ge, fill=0.0, base=0, channel_multiplier=-1),
                reads=[triu.B], writes=[triu.B])
        S_.pool(lambda e: e.memset(triu.t[0:64, 64:128], 0.0), writes=[triu.B])
        S_.pool(lambda e: e.memset(strl.t[:], 1.0), writes=[strl.B])
        S_.pool(lambda e: e.affine_select(out=strl.t[:], in_=strl.t[:], pattern=[[-1, 128]],
                                          compare_op=ALU.is_gt, fill=0.0, base=0, channel_multiplier=1),
                reads=[strl.B], writes=[strl.B])
        S_.pool(lambda e: e.memset(strl.t[64:128, 0:64], 0.0), writes=[strl.B])
        S_.dve(lambda e: e.tensor_copy(out=triu_b.t[:], in_=triu.t[:]), reads=[triu.B], writes=[triu_b.B])
        S_.dve(lambda e: e.tensor_copy(out=strl_b.t[:], in_=strl.t[:]), reads=[strl.B], writes=[strl_b.B])
        S_.pool(lambda e: e.iota(iot.t[:], pattern=[[-1, 128]], base=0, channel_multiplier=1), writes=[iot.B])
        S_.dve(lambda e: e.tensor_copy(out=dkq.t[:], in_=iot.t[:]), reads=[iot.B], writes=[dkq.B])
        S_.dve(lambda e: e.tensor_scalar_max(out=dkq.t[:], in0=dkq.t[:], scalar1=0.0), reads=[dkq.B], writes=[dkq.B])
        S_.dve(lambda e: e.tensor_scalar_mul(out=dkq.t[:], in0=dkq.t[:], scalar1=-2.0), reads=[dkq.B], writes=[dkq.B])
        for h in range(NH):
            def f(e, h=h):
                return e.tensor_scalar_mul(out=tdiag.t[:, h, :], in0=dkq.t[:], scalar1=slopec.t[:, h:h + 1])
            S_.dve(f, reads=[dkq.B, slopec.B], writes=[tdiag.B])
        S_.dve(lambda e: e.memset(tdiag.t[64:128, :, 0:64], NEG), reads=[tdiag.B], writes=[tdiag.B])
        S_.pool(lambda e: e.iota(kcoli.t[:], pattern=[[0, 1]], base=0, channel_multiplier=1), writes=[kcoli.B])
        S_.dve(lambda e: e.tensor_copy(out=kcol.t[:], in_=kcoli.t[:]), reads=[kcoli.B], writes=[kcol.B])
        for di in range(NDEL):
            delta = di - 4 * (NT - 1)

            def f(e, di=di, delta=delta):
                return e.tensor_scalar_add(out=kdel.t[:, di:di + 1], in0=kcol.t[:], scalar1=float(128 * delta - 256))
            S_.dve(f, reads=[kcol.B], writes=[kdel.B])
        for h in range(NH):
            S_.dve(lambda e, h=h: e.tensor_scalar_mul(out=kbias.t[:, h, :], in0=kdel.t[:, :], scalar1=slopec.t[:, h:h + 1]),
                   reads=[kdel.B, slopec.B], writes=[kbias.B])

        gb1 = sb("gb", [128, D], F32)
        gb2 = gb1
        cw = sb("cw", [128, 2, 3], F32)
        qg = sb("qg", [128, 1], F32)
        kg = sb("kg", [128, 1], F32)
        lamt = sb("lamt", [128, 4, 64], F32)
        lamw = sb("lamw", [128, 2, 64], F32)
        lams = sb("lams", [128, 2], F32)
        nlam = sb("nlam", [128, 1], F32)
        subg = sb("subg", [128, 1], F32)
        aw = sb("aw", [17, GW], F32)
        aw_hi = sb("aw_hi", [17, GW], BF16)
        aw_lo = sb("aw_lo", [17, GW], BF16)
        gng = sb("gng", [128, 1], F32)

        def load_params(l):
            S_.mark(f"params{l}")
            S_.dma(lambda e: e.dma_start(out=gb1.t[:], in_=ln1_g[l:l + 1, :].partition_broadcast(128)), writes=[gb1.B])
            for cc_ in range(NCV):
                for k_ in range(3):
                    S_.dma(lambda e, cc_=cc_, k_=k_: e.dma_start(
                        out=cw.t[:, cc_, k_:k_ + 1],
                        in_=conv_w[l, k_, cc_ * 128:(cc_ + 1) * 128].rearrange("(p o) -> p o", o=1)), writes=[cw.B])
            for hh in range(2):
                S_.dma(lambda e, hh=hh: e.dma_start(out=qg.t[hh * 64:(hh + 1) * 64, :],
                                                    in_=q_norm_g[l].rearrange("(p o) -> p o", o=1)), writes=[qg.B])
                S_.dma(lambda e, hh=hh: e.dma_start(out=kg.t[hh * 64:(hh + 1) * 64, :],
                                                    in_=k_norm_g[l].rearrange("(p o) -> p o", o=1)), writes=[kg.B])
                S_.dma(lambda e, hh=hh: e.dma_start(out=gng.t[hh * 64:(hh + 1) * 64, :],
                                                    in_=gla_norm_g[l].rearrange("(p o) -> p o", o=1)), writes=[gng.B])
            S_.dma(lambda e: e.dma_start(out=lamt.t[:].rearrange("p a b -> p (a b)"),
                                         in_=diff_lambda[l:l + 1].rearrange("o a b -> o (a b)").partition_broadcast(128)),
                   writes=[lamt.B])
            S_.dma(lambda e: e.dma_start(out=subg.t[:], in_=diff_subln_g[l].rearrange("(p o) -> p o", o=1)),
                   writes=[subg.B])
            S_.dma(lambda e: e.dma_start(out=aw.t[0:16, :], in_=gla_alpha_w[l]), writes=[aw.B])
            S_.dma(lambda e: e.dma_start(out=aw.t[16:17, :], in_=gla_alpha_b[l:l + 1, :]), writes=[aw.B])
            S_.dve(lambda e: e.tensor_copy(out=aw_hi.t[:], in_=aw.t[:]), reads=[aw.B], writes=[aw_hi.B])
            S_.dve(lambda e: e.tensor_tensor(out=aw_lo.t[:], in0=aw.t[:], in1=aw_hi.t[:], op=ALU.subtract),
                   reads=[aw.B, aw_hi.B], writes=[aw_lo.B])
            lam_init = 0.8 - 0.6 * math.exp(-0.3 * l)
            S_.dve(lambda e: e.tensor_scalar_mul(out=qg.t[:], in0=qg.t[:], scalar1=0.125), reads=[qg.B], writes=[qg.B])
            S_.dve(lambda e: e.tensor_tensor(out=lamw.t[:, 0, :], in0=lamt.t[:, 0, :], in1=lamt.t[:, 1, :], op=ALU.mult),
                   reads=[lamt.B], writes=[lamw.B])
            S_.dve(lambda e: e.tensor_tensor(out=lamw.t[:, 1, :], in0=lamt.t[:, 2, :], in1=lamt.t[:, 3, :], op=ALU.mult),
                   reads=[lamt.B], writes=[lamw.B])
            S_.dve(lambda e: e.tensor_reduce(out=lams.t[:], in_=lamw.t[:], axis=AX.X, op=ALU.add),
                   reads=[lamw.B], writes=[lams.B])
            S_.act(lambda e: e.activation(out=lams.t[:], in_=lams.t[:], func=AF.Exp), reads=[lams.B], writes=[lams.B])
            S_.dve(lambda e: e.tensor_tensor(out=nlam.t[:], in0=lams.t[:, 1:2], in1=lams.t[:, 0:1], op=ALU.subtract),
                   reads=[lams.B], writes=[nlam.B])
            S_.dve(lambda e: e.tensor_scalar_add(out=nlam.t[:], in0=nlam.t[:], scalar1=-lam_init),
                   reads=[nlam.B], writes=[nlam.B])
            S_.dve(lambda e: e.tensor_scalar_mul(out=subg.t[:], in0=subg.t[:], scalar1=1.0 - lam_init),
                   reads=[subg.B], writes=[subg.B])

        AWORDS = 48700
        ARENA = st.enter_context(nc.sbuf_tensor("arena", [128, AWORDS], F32))
        WW = (8 * D + 8 * DFF + 32 * D) // 2
        WREG = ARENA[:, 0:WW].bitcast(BF16)
        BW = Buf("wreg")
        BWin = [Buf(f"win{c}") for c in range(8)]
        BWo = [Buf(f"wout{c}") for c in range(8)]
        BW1 = [Buf(f"w1_{c}") for c in range(8)]
        BW2 = [Buf(f"w2_{c}") for c in range(32)]
        win_v = WREG[:, 0:8 * DINL].rearrange("p (c n) -> p c n", c=8)
        wout_v = WREG[:, 0:8 * D].rearrange("p (c n) -> p c n", c=8)
        w1_v = WREG[:, 8 * D:8 * D + 8 * DFF].rearrange("p (c n) -> p c n", c=8)
        w2_v = WREG[:, 8 * D + 8 * DFF:].rearrange("p (c n) -> p c n", c=32)

        def phase1(l, x_src, Bx_src):
            S_.fence()
            for c in range(8):
                S_.dma(lambda e, c=c: e.dma_start(out=win_v[:, c, :], in_=w_in[l, c * 128:(c + 1) * 128, :]),
                       writes=[BWin[c]], q="pool")
            a = ARENA
            o = 8 * DINL // 2

            def carve(n, dt=F32, shape=None):
                nonlocal o
                v = a[:, o:o + n]
                o += n
                return v

            xt_v = [carve(4096).rearrange("p (j n) -> p j n", j=4) for _ in range(1)]
            Bxt = [Buf("xt0")]
            hb_raw = carve(2048)
            hb_v = hb_raw.bitcast(BF16).rearrange("p (j n) -> p j n", j=4)
            Bhb = Buf("hb")
            hT_raw = carve(2048)
            hT_v = hT_raw.bitcast(BF16).rearrange("p (c n) -> p c n", c=8)
            BhT = Buf("hT")
            junk_v = carve(512).bitcast(BF16)
            Bjunk = Buf("junk")
            ss_v = carve(4)
            rs_v = carve(4)
            Bss, Brs = Buf("ss"), Buf("rs")
            u_v = carve(512)
            Bu = Buf("u")
            bg_v = carve(512)
            Bbg = Buf("bg")
            zc_v = [carve(514) for _ in range(2)]
            Bzc = [Buf("zc0"), Buf("zc1")]
            acc_v = carve(512)
            Bacc = Buf("acc")
            yc_v = carve(256).bitcast(BF16)
            Byc = Buf("yc")
            sq_v = carve(256).bitcast(BF16)
            Bsq = Buf("sq")
            sq2_v = [carve(256).bitcast(BF16) for _ in range(2)]
            Bsq2 = [Buf("sq2a"), Buf("sq2b")]
            zq_v = [carve(512) for _ in range(2)]
            Bzq = [Buf("zqa"), Buf("zqb")]
            rstd_v = carve(512)
            Brstd = Buf("rstd")
            qk_v = carve(256).bitcast(BF16)
            Bqk = Buf("qk")
            vt_v = carve(256).bitcast(BF16)[:, 0:VW]
            Bvt = Buf("vt")
            aT_v = carve(512)
            BaT = Buf("aT")
            aTh_v = carve(256).bitcast(BF16)
            aTl_v = carve(256).bitcast(BF16)
            BaTh, BaTl = Buf("aTh"), Buf("aTl")
            lah_v = carve(2 * GW).bitcast(BF16).rearrange("p (j n) -> p j n", j=4)
            lal_v = carve(2 * GW).bitcast(BF16).rearrange("p (j n) -> p j n", j=4)
            Blah, Blal = Buf("lah"), Buf("lal")
            la_v = carve(4 * GW).rearrange("p (j n) -> p j n", j=4)
            Bla = Buf("la")
            ebT_v = [carve(512) for _ in range(2)]
            enbT_v = [carve(512) for _ in range(2)]
            BebT = [Buf("ebT0"), Buf("ebT1")]
            BenbT = [Buf("enbT0"), Buf("enbT1")]
            erev_v = carve(4 * GW).rearrange("p (j n) -> p j n", j=4)
            Berev = Buf("erev")
            qinm_v = carve(512 * NP).bitcast(BF16).rearrange("p (c h n) -> p c h n", c=NP, h=2)
            kin_v = carve(256 * NP).bitcast(BF16).rearrange("p (c n) -> p c n", c=NP)
            Bqin, Bkin = [Buf("qin0"), Buf("qin1")], [Buf("kin0"), Buf("kin1")]
            erevm_v = [carve(4 * GW).rearrange("p (j n) -> p j n", j=4) for _ in range(2)]
            Berevm = [Buf("erevm0"), Buf("erevm1")]
            kendm_v = carve(4 * GW).bitcast(BF16).rearrange("p (j c n) -> p j c n", j=4, c=2)
            vtok_v = carve(2 * GW).bitcast(BF16).rearrange("p (j n) -> p j n", j=4)
            vpad_v = carve(512 * NP).bitcast(BF16).rearrange("p (j a b n) -> p j a b n", j=4, a=NP, b=2)
            Bkend, Bvtok = [Buf(f"kend{j}") for j in range(4)], [Buf(f"vtok{j}") for j in range(4)]
            Bvpad = [Buf(f"vpad{j}") for j in range(4)]
            am_v = carve(128 * NP).bitcast(BF16).rearrange("p (h n) -> p h n", h=2 * NP)
            Bam = [Buf("am0"), Buf("am1")]
            Sf_v = carve(128 * NP).rearrange("p (c n) -> p c n", c=NP)
            BSf = Buf("Sf")
            NSL = 4
            Sbf_v = carve(64 * NP * NSL).bitcast(BF16).rearrange("p (s c n) -> p s c n", s=NSL, c=NP)
            BSbf = [Buf(f"Sbf{i}") for i in range(NSL)]
            sg_v = carve(512)
            Bsg = Buf("sg")
            t1_v = carve(512)
            Bt1 = Buf("t1")
            yg_v = carve(256).bitcast(BF16)
            Byg = Buf("yg")
            assert o <= AWORDS, o

            S_.dve(lambda e: e.memset(aT_v[:, :], 1.0), writes=[BaT])
            S_.dve(lambda e: e.memset(aTh_v[:, :], 1.0), writes=[BaTh])
            S_.dve(lambda e: e.memset(aTl_v[:, :], 0.0), writes=[BaTl])
            S_.dve(lambda e: e.memset(Sf_v[:, :, :], 0.0), writes=[BSf])
            S_.pool(lambda e: e.memset(vpad_v[:, :, :, :, :], 0.0), writes=Bvpad)
            for i in range(2):
                S_.dve(lambda e, i=i: e.memset(zc_v[i][:, 0:2], 0.0), writes=[Bzc[i]])

            rr = [0]

            def nb():
                k = rr[0] % 4
                rr[0] += 1
                return k

            cp = [0]

            def copy_rr(out, in_, reads, writes):
                cp[0] += 1
                if cp[0] % 2 == 0:
                    S_.act(lambda e: e.copy(out=out, in_=in_), reads=reads, writes=writes)
                else:
                    S_.dve(lambda e: e.tensor_copy(out=out, in_=in_), reads=reads, writes=writes)

            def fm_group(col0, ncols, bk, t):
                for c in range(8):
                    S_.pe(lambda e, c=c: e.matmul(bank(bk)[0:ncols, :], lhsT=win_v[:, c, col0:col0 + ncols],
                                                  rhs=hT_v[:, c, :], start=(c == 0), stop=(c == 7)),
                          reads=[BWin[c], BhT], writes=[bankB[bk]])

            chunk_ctr = [0]

            def sec_norm(tn):
                tok0 = tn * 512
                xt = xt_v[0]
                bxt = Bxt[0]
                S_.dma(lambda e, xt=xt, tok0=tok0: e.dma_start(
                    out=xt, in_=x_src[tok0:tok0 + 512, :].rearrange("(j p) n -> p j n", p=128)),
                    reads=[Bx_src], writes=[bxt])
                for j in range(4):
                    S_.act(lambda e, j=j, xt=xt: e.activation(out=junk_v[:, :], in_=xt[:, j, :], func=AF.Square,
                                                              accum_out=ss_v[:, j:j + 1]),
                           reads=[bxt], writes=[Bjunk, Bss])
                S_.act(lambda e: e.activation(out=rs_v[:, :], in_=ss_v[:, :], func=AF.Ln, bias=epsc.t[:, 0:1], scale=1.0 / D),
                       reads=[Bss, epsc.B], writes=[Brs])
                S_.act(lambda e: e.activation(out=rs_v[:, :], in_=rs_v[:, :], func=AF.Exp, scale=-0.5),
                       reads=[Brs], writes=[Brs])
                for j in range(4):
                    S_.dve(lambda e, j=j, xt=xt: e.scalar_tensor_tensor(out=hb_v[:, j, :], in0=xt[:, j, :],
                                                                        scalar=rs_v[:, j:j + 1], in1=gb1.t[:, :],
                                                                        op0=ALU.mult, op1=ALU.mult),
                           reads=[bxt, Brs, gb1.B], writes=[Bhb])


            sec_norm(0)
            for t in range(NT):
                S_.mark(f"p1_tile{t}")
                tok0 = t * 512
                S_.mark(f"p1_t{t}_transposes")
                for c2 in range(4):
                    for cc_ in range(2):
                        c = c2 * 2 + cc_
                        for j in range(4):
                            S_.pe(lambda e, c=c, j=j, cc_=cc_: e.transpose(
                                out=PSb[:, cc_ * 512 + j * 128: cc_ * 512 + (j + 1) * 128],
                                in_=hb_v[:, j, c * 128:(c + 1) * 128], identity=ident.t[:, :]),
                                reads=[Bhb, ident.B], writes=[bankB[7]])
                    copy_rr(hT_v[:, c2 * 2:c2 * 2 + 2, :], PSb[:, :].rearrange("p (c n) -> p c n", c=2), [bankB[7]], [BhT])

                def sec_conv():
                    S_.mark(f"p1_t{t}_conv")
                    for cc in range(NCV):
                        bu = nb()
                        fm_group(C_U + cc * 128, 128, bu, t)
                        S_.act(lambda e, bu=bu: e.copy(out=u_v[:, :], in_=bank(bu)), reads=[bankB[bu]], writes=[Bu])
                        bc = nb()
                        fm_group(C_CC + cc * 128, 128, bc, t)
                        S_.dve(lambda e, bc=bc, cc=cc: e.tensor_tensor(out=zc_v[cc][:, 2:514], in0=bank(bc), in1=u_v[:, :],
                                                                       op=ALU.mult),
                               reads=[bankB[bc], Bu], writes=[Bzc[cc]])
                        S_.dve(lambda e, cc=cc: e.tensor_scalar_mul(out=acc_v[:, :], in0=zc_v[cc][:, 2:514],
                                                                    scalar1=cw.t[:, cc, 2:3]),
                               reads=[Bzc[cc], cw.B], writes=[Bacc])
                        S_.dve(lambda e, cc=cc: e.scalar_tensor_tensor(out=acc_v[:, :], in0=zc_v[cc][:, 1:513],
                                                                       scalar=cw.t[:, cc, 1:2], in1=acc_v[:, :],
                                                                       op0=ALU.mult, op1=ALU.add),
                               reads=[Bzc[cc], cw.B, Bacc], writes=[Bacc])
                        S_.dve(lambda e, cc=cc: e.scalar_tensor_tensor(out=acc_v[:, :], in0=zc_v[cc][:, 0:512],
                                                                       scalar=cw.t[:, cc, 0:1], in1=acc_v[:, :],
                                                                       op0=ALU.mult, op1=ALU.add),
                               reads=[Bzc[cc], cw.B, Bacc], writes=[Bacc])
                        bb = nb()
                        fm_group(C_CB + cc * 128, 128, bb, t)
                        S_.act(lambda e, bb=bb: e.copy(out=bg_v[:, :], in_=bank(bb)), reads=[bankB[bb]], writes=[Bbg])
                        S_.dve(lambda e: e.tensor_tensor(out=yc_v[:, :], in0=bg_v[:, :], in1=acc_v[:, :], op=ALU.mult),
                               reads=[Bbg, Bacc], writes=[Byc])
                        S_.dma(lambda e, cc=cc, tok0=tok0: e.dma_start(out=yT_d[cc, :, tok0:tok0 + 512], in_=yc_v[:, :]),
                               reads=[Byc], writes=[ByT])
                        S_.dve(lambda e, cc=cc: e.tensor_copy(out=zc_v[cc][:, 0:2], in_=zc_v[cc][:, 512:514]),
                               reads=[Bzc[cc]], writes=[Bzc[cc]])


                def sec_qk():
                    S_.mark(f"p1_t{t}_qk")
                    groups_ = [(which, h) for which in range(2) for h in range(NH)]
                    pend = None
                    for gi, g in enumerate(groups_ + [None]):
                        if g is not None:
                            which, h = g
                            col0 = (C_Q if which == 0 else C_K) + h * 128
                            bz = nb()
                            fm_group(col0, 128, bz, t)
                            sqi = gi % 2
                            S_.dve(lambda e, bz=bz, sqi=sqi: e.tensor_copy(out=zq_v[sqi][:, :], in_=bank(bz)),
                                   reads=[bankB[bz]], writes=[Bzq[sqi]])
                            S_.act(lambda e, sqi=sqi: e.activation(out=sq2_v[sqi][:, :], in_=zq_v[sqi][:, :], func=AF.Square),
                                   reads=[Bzq[sqi]], writes=[Bsq2[sqi]])
                        if pend is not None:
                            which_p, h_p, bz_p, sqi_p = pend
                            bm = nb()
                            S_.pe(lambda e, bm=bm, sqi_p=sqi_p: e.matmul(bank(bm), lhsT=blk64.t[:, :], rhs=sq2_v[sqi_p][:, :],
                                                                         start=True, stop=True),
                                  reads=[Bsq2[sqi_p], blk64.B], writes=[bankB[bm]])
                            S_.act(lambda e, bm=bm: e.activation(out=rstd_v[:, :], in_=bank(bm), func=AF.Ln, bias=epsc.t[:, 0:1]),
                                   reads=[bankB[bm], epsc.B], writes=[Brstd])
                            S_.act(lambda e: e.activation(out=rstd_v[:, :], in_=rstd_v[:, :], func=AF.Exp, scale=-0.5),
                                   reads=[Brstd], writes=[Brstd])
                            gcol = qg if which_p == 0 else kg
                            S_.dve(lambda e, sqi_p=sqi_p, gcol=gcol: e.scalar_tensor_tensor(
                                out=qk_v[:, :], in0=zq_v[sqi_p][:, :], scalar=gcol.t[:, 0:1], in1=rstd_v[:, :],
                                op0=ALU.mult, op1=ALU.mult),
                                reads=[Bzq[sqi_p], gcol.B, Brstd], writes=[Bqk])
                            dst = qT_d if which_p == 0 else kT_d
                            Bdst = BqT if which_p == 0 else BkT
                            S_.dma(lambda e, dst=dst, h_p=h_p, tok0=tok0: e.dma_start(out=dst[h_p, :, tok0:tok0 + 512], in_=qk_v[:, :]),
                                   reads=[Bqk], writes=[Bdst])
                        pend = (g[0], g[1], bz, gi % 2) if g is not None else None

                def sec_v():
                    S_.mark(f"p1_t{t}_v")
                    for j in range(4):
                        bv = nb()
                        for c in range(8):
                            S_.pe(lambda e, c=c, j=j, bv=bv: e.matmul(bank(bv)[:, 0:VW], lhsT=hT_v[:, c, j * 128:(j + 1) * 128],
                                                                      rhs=win_v[:, c, C_V:C_V + VW], start=(c == 0), stop=(c == 7)),
                                  reads=[BWin[c], BhT], writes=[bankB[bv]])
                        copy_rr(vt_v[:, :], bank(bv)[:, 0:VW], [bankB[bv]], [Bvt])
                        S_.dma(lambda e, j=j, tok0=tok0: e.dma_start(out=v_d[tok0 + j * 128: tok0 + (j + 1) * 128, :], in_=vt_v[:, :]),
                               reads=[Bvt], writes=[Bv])


                def sec_gla():
                    S_.mark(f"p1_t{t}_gla")
                    ba = nb()
                    fm_group(C_GA, 16, ba, t)
                    S_.act(lambda e, ba=ba: e.copy(out=aT_v[0:16, :], in_=bank(ba)[0:16, :]), reads=[bankB[ba]], writes=[BaT])
                    S_.dve(lambda e: e.tensor_copy(out=aTh_v[0:16, :], in_=aT_v[0:16, :]), reads=[BaT], writes=[BaTh])
                    S_.dve(lambda e: e.tensor_tensor(out=aTl_v[0:16, :], in0=aT_v[0:16, :], in1=aTh_v[0:16, :], op=ALU.subtract),
                           reads=[BaT, BaTh], writes=[BaTl])
                    for j in range(4):
                        bk = 4 + j // BPB
                        passes = [(aTh_v, BaTh, aw_hi), (aTl_v, BaTl, aw_hi), (aTh_v, BaTh, aw_lo)]
                        for pi, (av, aB, wv) in enumerate(passes):
                            S_.pe(lambda e, j=j, bk=bk, av=av, wv=wv, pi=pi: e.matmul(
                                bank(bk)[:, (j % BPB) * GW:(j % BPB + 1) * GW],
                                lhsT=av[0:17, j * 128:(j + 1) * 128], rhs=wv.t[0:17, :],
                                start=(pi == 0), stop=(pi == 2)),
                                reads=[aB, wv.B], writes=[bankB[bk]])
                    for half in range(4 // BPB):
                        S_.act(lambda e, half=half: e.activation(out=la_v[:, BPB * half:BPB * half + BPB, :].rearrange("p j n -> p (j n)"),
                                                                 in_=bank(4 + half), func=AF.Exp, scale=-1.0),
                               reads=[bankB[4 + half]], writes=[Bla])
                    S_.act(lambda e: e.activation(out=la_v[:, :, :].rearrange("p j n -> p (j n)"),
                                                  in_=la_v[:, :, :].rearrange("p j n -> p (j n)"), func=AF.Ln, bias=onec.t[:, 0:1]),
                           reads=[Bla, onec.B], writes=[Bla])
                    S_.dve(lambda e: e.tensor_copy(out=lah_v[:, :, :], in_=la_v[:, :, :]), reads=[Bla], writes=[Blah])
                    S_.dve(lambda e: e.tensor_tensor(out=lal_v[:, :, :], in0=la_v[:, :, :], in1=lah_v[:, :, :], op=ALU.subtract),
                           reads=[Bla, Blah], writes=[Blal])

                def sec_gla_cumsum():
                    S_.mark(f"p1_t{t}_gla_cumsum")
                    for p in range(NP):
                        bk = 4 + p
                        for j in range(4):
                            for pi, (lv, lB) in enumerate([(lah_v, Blah), (lal_v, Blal)]):
                                S_.pe(lambda e, p=p, j=j, bk=bk, lv=lv, pi=pi: e.matmul(
                                    bank(bk)[:, j * 128:(j + 1) * 128],
                                    lhsT=lv[:, j, p * 128:(p + 1) * 128], rhs=triu_b.t[:, :],
                                    start=(pi == 0), stop=(pi == 1)),
                                    reads=[lB, triu_b.B], writes=[bankB[bk]])
                        S_.act(lambda e, p=p, bk=bk: e.activation(out=ebT_v[p][:, :], in_=bank(bk), func=AF.Exp, scale=-1.0 / 16),
                               reads=[bankB[bk]], writes=[BebT[p]])
                        S_.act(lambda e, p=p, bk=bk: e.activation(out=enbT_v[p][:, :], in_=bank(bk), func=AF.Exp, scale=1.0 / 16),
                               reads=[bankB[bk]], writes=[BenbT[p]])
                    for j in range(4):
                        bk = 4 + j // BPB
                        for pi, (lv, lB) in enumerate([(lah_v, Blah), (lal_v, Blal)]):
                            S_.pe(lambda e, j=j, bk=bk, lv=lv, pi=pi: e.matmul(
                                bank(bk)[:, (j % BPB) * GW:(j % BPB + 1) * GW],
                                lhsT=strl_b.t[:, :], rhs=lv[:, j, :], start=(pi == 0), stop=(pi == 1)),
                                reads=[lB, strl_b.B], writes=[bankB[bk]])
                    for half in range(4 // BPB):
                        S_.act(lambda e, half=half: e.activation(out=erev_v[:, BPB * half:BPB * half + BPB, :].rearrange("p j n -> p (j n)"),
                                                                 in_=bank(4 + half), func=AF.Exp, scale=-1.0 / 16),
                               reads=[bankB[4 + half]], writes=[Berev])

                def sec_gla_qin():
                    S_.mark(f"p1_t{t}_gla_qin")
                    for p in range(NP):
                        bq = nb()
                        fm_group(C_GQ + p * 128, 128, bq, t)
                        for hl in range(2):
                            S_.dve(lambda e, p=p, hl=hl, bq=bq: e.scalar_tensor_tensor(
                                out=qinm_v[:, p, hl, :], in0=bank(bq), scalar=hmask.t[:, hl:hl + 1],
                                in1=ebT_v[p][:, :], op0=ALU.mult, op1=ALU.mult),
                                reads=[bankB[bq], BebT[p], hmask.B], writes=[Bqin[p]])
                        bkk = nb()
                        fm_group(C_GK + p * 128, 128, bkk, t)
                        S_.dve(lambda e, p=p, bkk=bkk: e.tensor_tensor(out=kin_v[:, p, :], in0=bank(bkk), in1=enbT_v[p][:, :],
                                                                       op=ALU.mult),
                               reads=[bankB[bkk], BenbT[p]], writes=[Bkin[p]])
                    for j in range(4):
                        bv = nb()
                        for c in range(8):
                            S_.pe(lambda e, c=c, j=j, bv=bv: e.matmul(bank(bv)[:, 0:2 * GW], lhsT=hT_v[:, c, j * 128:(j + 1) * 128],
                                                                      rhs=win_v[:, c, C_GK:C_GK + 2 * GW], start=(c == 0), stop=(c == 7)),
                                  reads=[BWin[c], BhT], writes=[bankB[bv]])
                        for cc in range(2):
                            S_.dve(lambda e, j=j, bv=bv, cc=cc: e.scalar_tensor_tensor(
                                out=kendm_v[:, j, cc, :], in0=bank(bv)[:, 0:GW], scalar=cmask.t[:, cc:cc + 1],
                                in1=erev_v[:, j, :], op0=ALU.mult, op1=ALU.mult),
                                reads=[bankB[bv], Berev, cmask.B], writes=[Bkend[j]])
                        S_.dve(lambda e, j=j, bv=bv: e.tensor_copy(out=vtok_v[:, j, :], in_=bank(bv)[:, GW:2 * GW]),
                               reads=[bankB[bv]], writes=[Bvtok[j]])
                        for hl in range(2):
                            S_.pool(lambda e, j=j, hl=hl: e.tensor_copy(
                                out=vpad_v[:, j, :, hl, hl * 64:(hl + 1) * 64],
                                in_=vtok_v[:, j, :].rearrange("p (a b e) -> p a b e", a=NP, b=2)[:, :, hl, :]),
                                reads=[Bvtok[j]], writes=[Bvpad[j]])

                def sec_gla_blocks():
                    S_.mark(f"p1_t{t}_gla_blocks")
                    for j in range(4):
                        bas = [nb() for _ in range(NP)]
                        for h in range(2 * NP):
                            p, hl = h // 2, h % 2
                            S_.pe(lambda e, p=p, hl=hl, j=j, bas=bas: e.matmul(
                                bank(bas[p])[:, hl * 128:(hl + 1) * 128],
                                lhsT=kin_v[:, p, j * 128:(j + 1) * 128],
                                rhs=qinm_v[:, p, hl, j * 128:(j + 1) * 128], start=True, stop=True),
                                reads=[Bkin[p], Bqin[p]], writes=[bankB[bas[p]]])
                        for p in range(NP):
                            S_.dve(lambda e, p=p, bas=bas: e.tensor_tensor(
                                out=am_v[:, 2 * p:2 * p + 2, :], in0=bank(bas[p])[:, 0:256].rearrange("p (h n) -> p h n", h=2),
                                in1=triu.t[:, :].unsqueeze(1).broadcast_to([128, 2, 128]), op=ALU.mult),
                                reads=[bankB[bas[p]], triu.B], writes=[Bam[p]])
                        for p in range(NP):
                            ob = bank(4 + p)[:, j * 128:(j + 1) * 128]
                            for hl in range(2):
                                S_.pe(lambda e, p=p, hl=hl, j=j, ob=ob: e.matmul(
                                    ob, lhsT=vpad_v[:, j, p, hl, :], rhs=am_v[:, 2 * p + hl, :], start=(hl == 0), stop=False),
                                    reads=[Bvpad[j], Bam[p]], writes=[bankB[4 + p]])
                        for cc in range(2):
                            ch = chunk_ctr[0]
                            chunk_ctr[0] += 1
                            sl = ch % NSL
                            S_.pool(lambda e, sl=sl: e.tensor_copy(out=Sbf_v[:, sl, :, :], in_=Sf_v[:, :, :]),
                                    reads=[BSf], writes=[BSbf[sl]])
                            for p in range(NP):
                                obc = bank(4 + p)[:, j * 128 + cc * 64: j * 128 + (cc + 1) * 64]
                                for hl in range(2):
                                    S_.pe(lambda e, p=p, hl=hl, j=j, cc=cc, sl=sl, obc=obc: e.matmul(
                                        obc, lhsT=Sbf_v[:, sl, p, :],
                                        rhs=qinm_v[:, p, hl, j * 128 + cc * 64: j * 128 + (cc + 1) * 64],
                                        start=False, stop=(cc == 1 and hl == 1)),
                                        reads=[BSbf[sl], Bqin[p]], writes=[bankB[4 + p]])
                            for p in range(NP):
                                S_.pe(lambda e, p=p, j=j, cc=cc: e.matmul(
                                    bank(6)[:, p * 128:(p + 1) * 128],
                                    lhsT=kendm_v[:, j, cc, p * 128:(p + 1) * 128],
                                    rhs=vtok_v[:, j, p * 128:(p + 1) * 128], start=True, stop=True),
                                    reads=[Bkend[j], Bvtok[j]], writes=[bankB[6]])
                            col = j * 128 + cc * 64 + 63
                            for p in range(NP):
                                for hl in range(2):
                                    S_.dve(lambda e, p=p, hl=hl, col=col: e.scalar_tensor_tensor(
                                        out=Sf_v[hl * 64:(hl + 1) * 64, p, hl * 64:(hl + 1) * 64],
                                        in0=Sf_v[hl * 64:(hl + 1) * 64, p, hl * 64:(hl + 1) * 64],
                                        scalar=ebT_v[p][hl * 64:(hl + 1) * 64, col:col + 1],
                                        in1=bank(6)[hl * 64:(hl + 1) * 64, p * 128 + hl * 64: p * 128 + (hl + 1) * 64],
                                        op0=ALU.mult, op1=ALU.add),
                                        reads=[BSf, BebT[p], bankB[6]], writes=[BSf])

                def sec_gla_out():
                    S_.mark(f"p1_t{t}_gla_out")
                    for p in range(NP):
                        bo = 4 + p
                        S_.act(lambda e, bo=bo: e.activation(out=sq_v[:, :], in_=bank(bo), func=AF.Square),
                               reads=[bankB[bo]], writes=[Bsq])
                        bm = nb()
                        S_.pe(lambda e, bm=bm: e.matmul(bank(bm), lhsT=blk64.t[:, :], rhs=sq_v[:, :], start=True, stop=True),
                              reads=[Bsq, blk64.B], writes=[bankB[bm]])
                        S_.act(lambda e, bm=bm: e.activation(out=rstd_v[:, :], in_=bank(bm), func=AF.Ln, bias=epsc.t[:, 0:1]),
                               reads=[bankB[bm], epsc.B], writes=[Brstd])
                        S_.act(lambda e: e.activation(out=rstd_v[:, :], in_=rstd_v[:, :], func=AF.Exp, scale=-0.5),
                               reads=[Brstd], writes=[Brstd])
                        S_.dve(lambda e, bo=bo: e.scalar_tensor_tensor(out=t1_v[:, :], in0=bank(bo), scalar=gng.t[:, 0:1],
                                                                       in1=rstd_v[:, :], op0=ALU.mult, op1=ALU.mult),
                               reads=[bankB[bo], gng.B, Brstd], writes=[Bt1])
                        bg = nb()
                        fm_group(C_GG + p * 128, 128, bg, t)
                        S_.act(lambda e, bg=bg: e.activation(out=sg_v[:, :], in_=bank(bg), func=AF.Silu),
                               reads=[bankB[bg]], writes=[Bsg])
                        S_.dve(lambda e: e.tensor_tensor(out=yg_v[:, :], in0=t1_v[:, :], in1=sg_v[:, :], op=ALU.mult),
                               reads=[Bt1, Bsg], writes=[Byg])
                        S_.dma(lambda e, p=p, tok0=tok0: e.dma_start(out=yT_d[NCV + NH + p, :, tok0:tok0 + 512], in_=yg_v[:, :]),
                               reads=[Byg], writes=[ByT])

                sec_gla()
                sec_conv()
                sec_gla_cumsum()
                sec_qk()
                if t + 1 < NT:
                    sec_norm(t + 1)
                sec_gla_qin()
                sec_v()
                sec_gla_blocks()
                sec_gla_out()


        def phase2(l):
            S_.mark(f"p2_{l}")
            S_.fence()
            a = ARENA
            o = 0

            def carve(n):
                nonlocal o
                v = a[:, o:o + n]
                o += n
                return v
            KT_v = carve(S // 2).bitcast(BF16)
            QT_v = carve(S).bitcast(BF16).rearrange("p (c n) -> p c n", c=2)
            V_v = carve(S // 2).bitcast(BF16).rearrange("p (b e) -> p b e", e=128)
            BKT, BQT, BV = Buf("KT"), Buf("QT"), Buf("V")
            prefetch_w = (o <= 16384) and not pair
            if prefetch_w:
                o = WW
                for c in range(6, 8):
                    S_.dma(lambda e, c=c: e.dma_start(out=w1_v[:, c, :], in_=w_mlp1[l, c * 128:(c + 1) * 128, :]),
                           writes=[BW1[c]], q="pool")
                for c in range(32):
                    S_.dma(lambda e, c=c: e.dma_start(out=w2_v[:, c, :], in_=w_mlp2[l, c * 128:(c + 1) * 128, :]),
                           writes=[BW2[c]], q="pool")
            PT_v = [carve(512).bitcast(BF16).rearrange("p (c n) -> p c n", c=2) for _ in range(2)]
            BPT = [Buf("PT0"), Buf("PT1")]
            BPT2 = [[Buf("PT0c0"), Buf("PT0c1")], [Buf("PT1c0"), Buf("PT1c1")]]
            r_v = carve(1024).rearrange("p (c n) -> p c n", c=2)
            Br = Buf("r")
            Oc_v = [carve(1024).rearrange("p (c n) -> p c n", c=2) for _ in range(2)]
            BOc = [Buf("Oc0"), Buf("Oc1")]
            sums_v = [carve(1024).rearrange("p (c n) -> p c n", c=2) for _ in range(2)]
            Bsums = [Buf("sums0"), Buf("sums1")]
            o_v = [carve(512) for _ in range(2)]
            Bo = [Buf("o0"), Buf("o1")]
            tile_ctr = [0]
            cur_it = [0]
            sq_v = carve(256).bitcast(BF16)
            Bsq = Buf("sq2")
            rstd_v = carve(512)
            Brstd = Buf("rstd2")
            y_v = carve(256).bitcast(BF16)
            By = Buf("y2")
            assert o <= AWORDS, o
            S_.dve(lambda e: e.memset(QT_v[64:128, 0, :], 0.0), writes=[BQT])
            S_.dve(lambda e: e.memset(QT_v[0:64, 1, :], 0.0), writes=[BQT])
            for h in range(NH):
                S_.dma(lambda e, h=h: e.dma_start(out=KT_v[:, :], in_=kT_d[h, :, :]), reads=[BkT], writes=[BKT])
                for comp in range(2):
                    S_.dma(lambda e, h=h, comp=comp: e.dma_start(out=QT_v[comp * 64:(comp + 1) * 64, comp, :],
                                                                 in_=qT_d[h, comp * 64:(comp + 1) * 64, :]),
                           reads=[BqT], writes=[BQT])
                S_.dma(lambda e, h=h: e.dma_start(out=V_v[:, :, :],
                                                  in_=v_d[:, h * 128:(h + 1) * 128].rearrange("(b p) e -> p b e", p=128)),
                       reads=[Bv], writes=[BV])
                THR = 50.0
                its = []
                for qt in range(NT):
                    q0 = qt * 512
                    kbs = []
                    for kb in range(4 * qt + 4):
                        dmin = q0 - (kb * 128 + 127)
                        if (not pair) and kb < 4 * qt and slopes[h] * dmin > THR:
                            continue
                        kbs.append(kb)
                    for n_, kb in enumerate(kbs):
                        its.append(dict(qt=qt, q0=q0, kb=kb, first=(n_ == 0), last=(n_ == len(kbs) - 1), idx=len(its)))

                def emit_qk(itx, h=h):
                    kb, qt, q0 = itx["kb"], itx["qt"], itx["q0"]
                    j = kb - 4 * qt
                    c0 = 0 if j < 0 else j * 128
                    sbk = (itx["idx"] % 2) * 2
                    diag = j >= 0
                    for comp in range(2):
                        S_.pe(lambda e, comp=comp: e.matmul(
                            bank(sbk + comp)[:, c0:512], lhsT=KT_v[:, kb * 128:(kb + 1) * 128],
                            rhs=QT_v[:, comp, q0 + c0:q0 + 512], start=True, stop=(not diag)),
                            reads=[BKT, BQT], writes=[bankB[sbk + comp]])
                        if diag:
                            S_.pe(lambda e, comp=comp: e.matmul(
                                bank(sbk + comp)[:, c0:c0 + 128], lhsT=ident.t[:, :], rhs=tdiag.t[:, h, :],
                                start=False, stop=True),
                                reads=[ident.B, tdiag.B], writes=[bankB[sbk + comp]])

                def emit_exp_pv(itx, h=h):
                    kb, qt, q0 = itx["kb"], itx["qt"], itx["q0"]
                    j = kb - 4 * qt
                    c0 = 0 if j < 0 else j * 128
                    sbk = (itx["idx"] % 2) * 2
                    pt = itx["idx"] % 2
                    di = (kb - 4 * qt) + 4 * (NT - 1)
                    first, last = itx["first"], itx["last"]
                    for comp in range(2):
                        S_.act(lambda e, comp=comp: e.activation(
                            out=PT_v[pt][:, comp, c0:512], in_=bank(sbk + comp)[:, c0:512],
                            func=AF.Exp, bias=kbias.t[:, h, di:di + 1], scale=1.0),
                            reads=[bankB[sbk + comp], kbias.B], writes=[BPT2[pt][comp]])
                    for comp in range(2):
                        S_.pe(lambda e, comp=comp: e.matmul(
                            bank(4 + comp)[:, c0:512], lhsT=V_v[:, kb, :], rhs=PT_v[pt][:, comp, c0:512],
                            start=first, stop=last),
                            reads=[BV, BPT2[pt][comp]], writes=[bankB[4 + comp]])
                        S_.pe(lambda e, comp=comp: e.matmul(
                            bank(6 + comp)[:, c0:512], lhsT=ones_bf.t[:, :], rhs=PT_v[pt][:, comp, c0:512],
                            start=first, stop=last),
                            reads=[ones_bf.B, BPT2[pt][comp]], writes=[bankB[6 + comp]])

                def stage_a(itx, h=h):
                    ab = itx["tile"] % 2
                    S_.dve(lambda e: e.tensor_copy(out=Oc_v[ab][:, :, :],
                                                   in_=PSf[:, 4 * 512:6 * 512].rearrange("p (c n) -> p c n", c=2)),
                           reads=[bankB[4], bankB[5]], writes=[BOc[ab]])
                    S_.dve(lambda e: e.tensor_copy(out=sums_v[ab][:, :, :],
                                                   in_=PSf[:, 6 * 512:8 * 512].rearrange("p (c n) -> p c n", c=2)),
                           reads=[bankB[6], bankB[7]], writes=[Bsums[ab]])

                def stage_b(itx, h=h):
                    ab = itx["tile"] % 2
                    S_.dve(lambda e: e.reciprocal(out=r_v[:, :, :], in_=sums_v[ab][:, :, :]), reads=[Bsums[ab]], writes=[Br])
                    S_.dve(lambda e: e.tensor_tensor(out=Oc_v[ab][:, :, :], in0=Oc_v[ab][:, :, :], in1=r_v[:, :, :], op=ALU.mult),
                           reads=[BOc[ab], Br], writes=[BOc[ab]])
                    S_.dve(lambda e: e.scalar_tensor_tensor(out=o_v[ab][:, :], in0=Oc_v[ab][:, 1, :], scalar=nlam.t[:, 0:1],
                                                            in1=Oc_v[ab][:, 0, :], op0=ALU.mult, op1=ALU.add),
                           reads=[BOc[ab], nlam.B], writes=[Bo[ab]])
                    S_.dve(lambda e: e.tensor_tensor(out=sq_v[:, :], in0=o_v[ab][:, :], in1=o_v[ab][:, :], op=ALU.mult),
                           reads=[Bo[ab]], writes=[Bsq])

                def stage_c(itx, h=h):
                    ab = itx["tile"] % 2
                    q0 = itx["q0"]
                    sbk = (cur_it[0] % 2) * 2
                    S_.pe(lambda e: e.matmul(bank(sbk), lhsT=o128.t[:, :], rhs=sq_v[:, :], start=True, stop=True),
                          reads=[o128.B, Bsq], writes=[bankB[sbk]])
                    S_.act(lambda e: e.activation(out=rstd_v[:, :], in_=bank(sbk), func=AF.Ln, bias=epsc.t[:, 0:1]),
                           reads=[bankB[sbk], epsc.B], writes=[Brstd])
                    S_.act(lambda e: e.activation(out=rstd_v[:, :], in_=rstd_v[:, :], func=AF.Exp, scale=-0.5),
                           reads=[Brstd], writes=[Brstd])
                    S_.dve(lambda e: e.scalar_tensor_tensor(out=y_v[:, :], in0=o_v[ab][:, :], scalar=subg.t[:, 0:1], in1=rstd_v[:, :],
                                                            op0=ALU.mult, op1=ALU.mult),
                           reads=[Bo[ab], subg.B, Brstd], writes=[By])
                    S_.dma(lambda e: e.dma_start(out=yT_d[NCV + h, :, q0:q0 + 512], in_=y_v[:, :]),
                           reads=[By], writes=[ByT])

                for itx in its:
                    if itx["first"]:
                        tile_ctr[0] += 1
                    itx["tile"] = tile_ctr[0]
                pending = []

                def tick():
                    for p_ in pending:
                        p_[0] -= 1
                    while pending and pending[0][0] <= 0:
                        _, fn_, it_ = pending.pop(0)
                        fn_(it_)

                emit_qk(its[0])
                for i_, itx in enumerate(its):
                    if i_ + 1 < len(its):
                        emit_qk(its[i_ + 1])
                    emit_exp_pv(itx)
                    cur_it[0] = itx["idx"]
                    tick()
                    if itx["last"]:
                        stage_a(itx)
                        pending.append([1, stage_b, itx])
                        pending.append([5, stage_c, itx])
                while pending:
                    _, fn_, it_ = pending.pop(0)
                    fn_(it_)

        def phase3(l, x_src, Bx_src, x_dst, Bx_dst, is_out):
            TT = 256
            NJ = TT // 128
            S_.mark(f"p3_{l}")
            S_.fence()
            S_.dma(lambda e: e.dma_start(out=gb2.t[:], in_=ln2_g[l:l + 1, :].partition_broadcast(128)), writes=[gb2.B])
            pre = (S // 2 + S + S // 2 <= 16384) and not pair and stop_after is None
            for c in range(8):
                S_.dma(lambda e, c=c: e.dma_start(out=wout_v[:, c, :], in_=w_out[l, c * 128:(c + 1) * 128, :]),
                       writes=[BWo[c]], q="pool")
            for c in range(8):
                if pre and c >= 6:
                    continue
                S_.dma(lambda e, c=c: e.dma_start(out=w1_v[:, c, :], in_=w_mlp1[l, c * 128:(c + 1) * 128, :]),
                       writes=[BW1[c]], q="pool")
            if not pre:
                for c in range(32):
                    S_.dma(lambda e, c=c: e.dma_start(out=w2_v[:, c, :], in_=w_mlp2[l, c * 128:(c + 1) * 128, :]),
                           writes=[BW2[c]], q="pool")
            a = ARENA
            o = WW

            def carve(n):
                nonlocal o
                v = a[:, o:o + n]
                o += n
                return v
            yT_v = [carve(8 * TT // 2).bitcast(BF16).rearrange("p (c n) -> p c n", c=8)] * 2
            ByT_s = [Buf("yTs0")] * 2
            if pair:
                yTb_v = carve(8 * TT // 2).bitcast(BF16).rearrange("p (c n) -> p c n", c=8)
                ByTb = Buf("yTb")
            x_v = [carve(NJ * D).rearrange("p (j n) -> p j n", j=NJ) for _ in range(2)]
            Bx_s = [Buf("xs0"), Buf("xs1")]
            hb_v = carve(NJ * D // 2).bitcast(BF16).rearrange("p (j n) -> p j n", j=NJ)
            Bhb = Buf("hb3")
            hT_v = carve(8 * TT // 2).bitcast(BF16).rearrange("p (c n) -> p c n", c=8)
            BhT = Buf("hT3")
            aT_v = carve(32 * TT // 2).bitcast(BF16).rearrange("p (f n) -> p f n", f=32)
            BaT = [Buf(f"aT{f}") for f in range(32)]
            ss_v = carve(NJ)
            rs_v = carve(NJ)
            Bss, Brs = Buf("ss3"), Buf("rs3")
            relu_v = [carve(TT) for _ in range(2)]
            Brelu = [Buf("relu0"), Buf("relu1")]
            assert o <= AWORDS, o
            rr = [0]

            def nb():
                k = rr[0] % 7
                rr[0] += 1
                return k
            cp = [0]
            NTT = SP // TT

            def st_load(t):
                tok0 = t * TT
                s = t % 2
                if pair:
                    S_.dma(lambda e: e.dma_start(out=yT_v[s][:, :, :],
                                                 in_=yTall_d[:, :, tok0:tok0 + TT].rearrange("c p n -> p c n")),
                           reads=[ByTall], writes=[ByT_s[s]])
                    S_.dma(lambda e: e.dma_start(out=yTb_v[:, :, :],
                                                 in_=yTall_d[:, :, SP + tok0:SP + tok0 + TT].rearrange("c p n -> p c n")),
                           reads=[ByTall], writes=[ByTb])
                    S_.dve(lambda e: e.tensor_scalar_mul(out=yT_v[s][:, :, :], in0=yT_v[s][:, :, :], scalar1=rmask.t[:, 0:1]),
                           reads=[ByT_s[s], rmask.B], writes=[ByT_s[s]])
                    S_.dve(lambda e: e.scalar_tensor_tensor(out=yT_v[s][:, :, :], in0=yTb_v[:, :, :], scalar=rmask.t[:, 1:2],
                                                            in1=yT_v[s][:, :, :], op0=ALU.mult, op1=ALU.add),
                           reads=[ByT_s[s], ByTb, rmask.B], writes=[ByT_s[s]])
                else:
                    S_.dma(lambda e: e.dma_start(out=yT_v[s][:, :, :],
                                                 in_=yT_d[:, :, tok0:tok0 + TT].rearrange("c p n -> p c n")),
                           reads=[ByT], writes=[ByT_s[s]])
                S_.dma(lambda e: e.dma_start(out=x_v[s][:, :, :],
                                             in_=x_src[tok0:tok0 + TT, :].rearrange("(j p) n -> p j n", p=128)),
                       reads=[Bx_src], writes=[Bx_s[s]])

            def st_outproj(t):
                s = t % 2
                for j in range(NJ):
                    for n in range(2):
                        bk = nb()
                        for c in range(8):
                            S_.pe(lambda e, c=c, j=j, n=n, bk=bk: e.matmul(
                                bank(bk), lhsT=yT_v[s][:, c, j * 128:(j + 1) * 128], rhs=wout_v[:, c, n * 512:(n + 1) * 512],
                                start=(c == 0), stop=(c == 7)), reads=[ByT_s[s], BWo[c]], writes=[bankB[bk]])
                        S_.dve(lambda e, j=j, n=n, bk=bk: e.tensor_tensor(
                            out=x_v[s][:, j, n * 512:(n + 1) * 512], in0=bank(bk), in1=x_v[s][:, j, n * 512:(n + 1) * 512],
                            op=ALU.add), reads=[bankB[bk], Bx_s[s]], writes=[Bx_s[s]])

            def st_norm(t):
                s = t % 2
                for j in range(NJ):
                    S_.act(lambda e, j=j: e.activation(out=hb_v[:, j, :], in_=x_v[s][:, j, :], func=AF.Square,
                                                       accum_out=ss_v[:, j:j + 1]),
                           reads=[Bx_s[s]], writes=[Bhb, Bss])
                S_.act(lambda e: e.activation(out=rs_v[:, :], in_=ss_v[:, :], func=AF.Ln, bias=epsc.t[:, 0:1], scale=1.0 / D),
                       reads=[Bss, epsc.B], writes=[Brs])
                S_.act(lambda e: e.activation(out=rs_v[:, :], in_=rs_v[:, :], func=AF.Exp, scale=-0.5),
                       reads=[Brs], writes=[Brs])
                for j in range(NJ):
                    S_.dve(lambda e, j=j: e.scalar_tensor_tensor(out=hb_v[:, j, :], in0=x_v[s][:, j, :],
                                                                 scalar=rs_v[:, j:j + 1], in1=gb2.t[:, :],
                                                                 op0=ALU.mult, op1=ALU.mult),
                           reads=[Bx_s[s], Brs, gb2.B], writes=[Bhb])

            def st_transpose(t):
                for c4 in range(2):
                    for cc_ in range(4):
                        c = c4 * 4 + cc_
                        for j in range(NJ):
                            S_.pe(lambda e, c=c, j=j, cc_=cc_: e.transpose(
                                out=PSb[:, cc_ * TT + j * 128: cc_ * TT + (j + 1) * 128],
                                in_=hb_v[:, j, c * 128:(c + 1) * 128], identity=ident.t[:, :]),
                                reads=[Bhb, ident.B], writes=[bankB[7]])
                    cp[0] += 1
                    if cp[0] % 2 == 0:
                        S_.act(lambda e, c4=c4: e.copy(out=hT_v[:, c4 * 4:c4 * 4 + 4, :],
                                                       in_=PSb[:, :].rearrange("p (c n) -> p c n", c=4)),
                               reads=[bankB[7]], writes=[BhT])
                    else:
                        S_.dve(lambda e, c4=c4: e.tensor_copy(out=hT_v[:, c4 * 4:c4 * 4 + 4, :],
                                                              in_=PSb[:, :].rearrange("p (c n) -> p c n", c=4)),
                               reads=[bankB[7]], writes=[BhT])

            def st_hidden(t):
                for f in range(32):
                    bk = nb()
                    for c in range(8):
                        S_.pe(lambda e, c=c, f=f, bk=bk: e.matmul(
                            bank(bk)[:, 0:TT], lhsT=w1_v[:, c, f * 128:(f + 1) * 128], rhs=hT_v[:, c, :],
                            start=(c == 0), stop=(c == 7)), reads=[BW1[c], BhT], writes=[bankB[bk]])
                    rt = f % 2
                    S_.act(lambda e, rt=rt, bk=bk: e.activation(out=relu_v[rt][:, :], in_=bank(bk)[:, 0:TT], func=AF.Relu),
                           reads=[bankB[bk]], writes=[Brelu[rt]])
                    if f % 2 == 0:
                        S_.dve(lambda e, f=f, rt=rt: e.tensor_tensor(out=aT_v[:, f, :], in0=relu_v[rt][:, :], in1=relu_v[rt][:, :],
                                                                     op=ALU.mult), reads=[Brelu[rt]], writes=[BaT[f]])
                    else:
                        S_.pool(lambda e, f=f, rt=rt: e.tensor_tensor(out=aT_v[:, f, :], in0=relu_v[rt][:, :], in1=relu_v[rt][:, :],
                                                                      op=ALU.mult), reads=[Brelu[rt]], writes=[BaT[f]])

            def st_down(t, j):
                s = t % 2
                for n in range(2):
                    bk = nb()
                    for f in range(32):
                        S_.pe(lambda e, f=f, n=n, bk=bk: e.matmul(
                            bank(bk), lhsT=aT_v[:, f, j * 128:(j + 1) * 128], rhs=w2_v[:, f, n * 512:(n + 1) * 512],
                            start=(f == 0), stop=(f == 31)), reads=[BaT[f], BW2[f]], writes=[bankB[bk]])
                    S_.dve(lambda e, n=n, bk=bk: e.tensor_tensor(
                        out=x_v[s][:, j, n * 512:(n + 1) * 512], in0=bank(bk), in1=x_v[s][:, j, n * 512:(n + 1) * 512],
                        op=ALU.add), reads=[bankB[bk], Bx_s[s]], writes=[Bx_s[s]])

            def st_store(t):
                tok0 = t * TT
                s = t % 2
                op = S_.dma(lambda e: e.dma_start(
                    out=x_dst[tok0:tok0 + TT, :].rearrange("(j p) n -> p j n", p=128), in_=x_v[s][:, :, :]),
                    reads=[Bx_s[s]], writes=[Bx_dst])
                if is_out:
                    out_ops.append(op)

            st_load(0)
            st_outproj(0)
            st_norm(0)
            st_transpose(0)
            for t in range(NTT):
                if t + 1 < NTT:
                    st_load(t + 1)
                st_hidden(t)
                if t + 1 < NTT:
                    st_outproj(t + 1)
                    st_norm(t + 1)
                st_down(t, 0)
                if t + 1 < NTT:
                    st_transpose(t + 1)
                for j in range(1, NJ):
                    st_down(t, j)
                st_store(t)

        for l in range(DEPTH):
            load_params(l)
            x_src, Bsrc = (x_in, Buf("xin")) if l == 0 else (xs_d, Bxs)
            last = (l == DEPTH - 1)
            phase1(l, x_src, Bsrc)
            if stop_after == "p1":
                break
            phase2(l)
            if stop_after == "p2":
                break
            if pair:
                S_.dma(lambda e: e.collective_compute(
                    "AllGather", ALU.bypass, replica_groups=groups,
                    ins=[yT_d.rearrange("c p n -> (c p) n")], outs=[yTall_d.rearrange("c p n -> (c p) n")]),
                    reads=[ByT], writes=[ByTall], q="pool", ring="cc")
                x3_src, B3src = (xh_in, Buf("xhin")) if l == 0 else (xm_d, Bxm)
                x_dst, Bdst = (out_d, Bout) if last else (xm_d, Bxm)
                phase3(l, x3_src, B3src, x_dst, Bdst, last)
                if not last:
                    S_.dma(lambda e: e.collective_compute(
                        "AllGather", ALU.bypass, replica_groups=groups, ins=[xm_d[:, :]], outs=[xs_d[:, :]]),
                        reads=[Bxm], writes=[Bxs], q="pool", ring="cc")
            else:
                x_dst, Bdst = (out_d, Bout) if last else (xs_d, Bxs)
                phase3(l, x_src, Bsrc, x_dst, Bdst, last)

        if trunc is not None:
            S_.ops = S_.ops[:trunc]
            out_ops = []
            for en in ENGS:
                idxs = [i for i, o_ in enumerate(S_.ops) if o_[0] == en]
                if idxs:
                    out_ops.append(idxs[-1])
        if not out_ops:
            out_ops.append(len(S_.ops) - 1)
        S_.emit(nc, final_wait_ops=out_ops)
    return nc, S_


def _pair_layout(r, x_b, p):
    S = x_b.shape[0]
    SP = S // 2
    cols = np.concatenate([
        np.arange(0 + r * 128, 0 + (r + 1) * 128),
        np.arange(256 + r * 128, 256 + (r + 1) * 128),
        np.arange(512 + r * 128, 512 + (r + 1) * 128),
        np.arange(768 + 2 * r * 128, 768 + (2 * r + 2) * 128),
        np.arange(1280 + 2 * r * 128, 1280 + (2 * r + 2) * 128),
        np.arange(1792 + 2 * r * 128, 1792 + (2 * r + 2) * 128),
        np.arange(2304 + r * 128, 2304 + (r + 1) * 128),
        np.arange(2560 + r * 128, 2560 + (r + 1) * 128),
        np.arange(2816 + r * 128, 2816 + (r + 1) * 128),
        np.arange(3072 + r * 128, 3072 + (r + 1) * 128),
        np.arange(3328, 3344),
    ])
    rows = np.concatenate([
        np.concatenate([np.arange(i * 128, (i + 1) * 128),
                        np.arange(256 + 2 * i * 128, 256 + (2 * i + 2) * 128),
                        np.arange(768 + i * 128, 768 + (i + 1) * 128)]) for i in range(2)])
    slopes = np.array([[2.0 ** (-8.0 * (2 * r + hh + 1) / 4) for hh in range(2)]], dtype=np.float32)
    m = dict(p)
    m["x"] = np.ascontiguousarray(x_b)
    m["x_half"] = np.ascontiguousarray(x_b[r * SP:(r + 1) * SP])
    m["w_in"] = np.ascontiguousarray(p["w_in"][:, :, cols])
    m["conv_w"] = np.ascontiguousarray(p["conv_w"][:, :, r * 128:(r + 1) * 128])
    m["gla_alpha_w"] = np.ascontiguousarray(p["gla_alpha_w"][:, :, r * 128:(r + 1) * 128])
    m["gla_alpha_b"] = np.ascontiguousarray(p["gla_alpha_b"][:, r * 128:(r + 1) * 128])
    m["w_out"] = np.ascontiguousarray(p["w_out"][:, rows, :])
    m["slopes"] = slopes
    m["rmask"] = np.array([[1.0 - r, float(r)]], dtype=np.float32)
    return m


def kernel(x, ln1_g, w_in, conv_w, q_norm_g, k_norm_g, diff_lambda, diff_subln_g,
           gla_alpha_w, gla_alpha_b, gla_norm_g, w_out, ln2_g, w_mlp1, w_mlp2):
    x = np.asarray(x, dtype=np.float32)
    B, S, _ = x.shape
    depth = int(np.asarray(ln1_g).shape[0])
    nc, _ = build(S=S, DEPTH=depth)
    shared = dict(ln1_g=ln1_g, w_in=w_in, conv_w=conv_w, q_norm_g=q_norm_g, k_norm_g=k_norm_g,
                  diff_lambda=diff_lambda, diff_subln_g=diff_subln_g, gla_alpha_w=gla_alpha_w,
                  gla_alpha_b=gla_alpha_b, gla_norm_g=gla_norm_g, w_out=w_out, ln2_g=ln2_g,
                  w_mlp1=w_mlp1, w_mlp2=w_mlp2)
    shared = {k: np.ascontiguousarray(np.asarray(v, dtype=np.float32)) for k, v in shared.items()}
    in_maps = []
    for c in range(B):
        m = dict(shared)
        m["x"] = np.ascontiguousarray(x[c])
        in_maps.append(m)
    res = run_bass_kernel_spmd(nc, in_maps, core_ids=list(range(B)))
    return np.stack([np.asarray(r["out"], dtype=np.float32) for r in res.results], axis=0)
```
